# Optimizing a Trainium2 kernel written in Bass

```python
import jax, jax.numpy as jnp
from jax import lax
import numpy as np

D_MODEL = 4096
BATCH = 1
SEQ = 8192
DEPTH = 4

MIX_WIDTH = 2 * D_MODEL
ATTN_HEAD_DIM = 128
ATTN_WIDTH = 3 * MIX_WIDTH // 4
ATTN_HEADS = ATTN_WIDTH // ATTN_HEAD_DIM
DILATED_PATTERNS = ((128, 1), (512, 4), (2048, 16))
POOL_WIDTH = MIX_WIDTH - ATTN_WIDTH
POOL_WINDOWS = (2, 4, 8, 16)
POOL_GROUPS = len(POOL_WINDOWS)
POOL_GROUP_DIM = POOL_WIDTH // POOL_GROUPS
EVEN_IN_WIDTH = 3 * ATTN_WIDTH + POOL_WIDTH + MIX_WIDTH
REL_BUCKETS = 32
REL_MAX_DIST = 2048
MLSTM_WIDTH = 2 * D_MODEL
MLSTM_HEADS = 8
MLSTM_HEAD_DIM = MLSTM_WIDTH // MLSTM_HEADS
MLSTM_CHUNK = 128
CONV_WIDTH = 4
QKV_BLOCK = 4
ODD_IN_WIDTH = 3 * MLSTM_WIDTH
NORM_EPS = 1e-6
N_EVEN = (DEPTH + 1) // 2
N_ODD = DEPTH // 2

kernel_name = 'hybrid_dilated_pool_mlstm_trunk'


def _rms_norm(x, gain):
    xf = x.astype(jnp.float32)
    y = xf * lax.rsqrt(jnp.mean(xf * xf, axis=-1, keepdims=True) + NORM_EPS)
    return (y * gain.astype(jnp.float32)).astype(x.dtype)


def _t5_bucket(dist):
    max_exact = REL_BUCKETS // 2
    safe = np.maximum(dist, 1).astype(np.float32)
    large = max_exact + (np.log(safe / max_exact) / np.log(REL_MAX_DIST / max_exact)
                         * (REL_BUCKETS - max_exact)).astype(np.int32)
    large = np.minimum(large, REL_BUCKETS - 1)
    return np.where(dist < max_exact, dist, large).astype(np.int32)


def _band_structure(window, dilation, n_blocks):
    span = window // dilation
    blk = span
    i = np.arange(blk)[:, None]
    j = np.arange(2 * blk)[None, :]
    rel = i + blk - j
    in_band = (rel >= 0) & (rel <= span)
    prev_ok = (np.arange(n_blocks)[:, None, None] > 0) | (j[None] >= blk)
    mask = in_band[None] & prev_ok
    bucket = _t5_bucket(np.clip(rel, 0, None) * dilation)
    return mask, bucket


def _to_dilated_blocks(t, dilation, blk):
    B, S, H, Dh = t.shape
    unit = dilation * blk
    s_pad = -(-S // unit) * unit
    t = jnp.pad(t, ((0, 0), (0, s_pad - S), (0, 0), (0, 0)))
    t = t.reshape(B, s_pad // dilation, dilation, H, Dh).transpose(0, 2, 3, 1, 4)
    return t.reshape(B, dilation, H, s_pad // unit, blk, Dh)


def _from_dilated_blocks(t, seq):
    B, d, H, nb, L = t.shape[:5]
    rest = t.shape[5:]
    t = jnp.moveaxis(t.reshape((B, d, H, nb * L) + rest), 3, 1)
    return t.reshape((B, nb * L * d, H) + rest)[:, :seq]


def _dilated_window_attention(q, k, v, rel_bias, window, dilation):
    B, S, H, Dh = q.shape
    blk = window // dilation
    qb = _to_dilated_blocks(q, dilation, blk)
    kb = _to_dilated_blocks(k, dilation, blk)
    vb = _to_dilated_blocks(v, dilation, blk)
    n_blocks = qb.shape[3]
    pad_prev = ((0, 0), (0, 0), (0, 0), (1, 0), (0, 0), (0, 0))
    kk = jnp.concatenate([jnp.pad(kb[:, :, :, :-1], pad_prev), kb], axis=4)
    vv = jnp.concatenate([jnp.pad(vb[:, :, :, :-1], pad_prev), vb], axis=4)
    mask, bucket = _band_structure(window, dilation, n_blocks)
    bias = jnp.transpose(rel_bias.astype(jnp.float32)[bucket], (2, 0, 1))
    scores = jnp.einsum('brhcid,brhcjd->brhcij', qb, kk).astype(jnp.float32) + bias[None, None, :, None]
    scores = jnp.where(mask[None, None, None], scores, -jnp.inf)
    m = jnp.max(scores, axis=-1, keepdims=True)
    p = jnp.exp(scores - m)
    s = jnp.sum(p, axis=-1)
    o = jnp.einsum('brhcij,brhcjd->brhcid', p, vv.astype(jnp.float32)) / s[..., None]
    lse = m[..., 0] + jnp.log(s)
    return _from_dilated_blocks(o, S), _from_dilated_blocks(lse, S)


def _dilated_mixture_attention(q, k, v, rel_bias):
    out, lse = None, None
    for window, dilation in DILATED_PATTERNS:
        o_g, lse_g = _dilated_window_attention(q, k, v, rel_bias, window, dilation)
        if out is None:
            out, lse = o_g, lse_g
        else:
            new_lse = jnp.logaddexp(lse, lse_g)
            out = out * jnp.exp(lse - new_lse)[..., None] + o_g * jnp.exp(lse_g - new_lse)[..., None]
            lse = new_lse
    return out


def _multiscale_pool(u, pool_w, pool_scale):
    B, S, _ = u.shape
    ug = u.astype(jnp.float32).reshape(B, S, POOL_GROUPS, POOL_GROUP_DIM)
    csum = jnp.cumsum(ug, axis=1)
    counts = jnp.arange(1, S + 1, dtype=jnp.float32)
    outs = []
    for g, w in enumerate(POOL_WINDOWS):
        cg = csum[:, :, g]
        lag = jnp.pad(cg[:, :S - w], ((0, 0), (w, 0), (0, 0)))
        mean = (cg - lag) / jnp.minimum(counts, w)[None, :, None]
        outs.append(mean - ug[:, :, g])
    y = jnp.stack(outs, axis=2)
    y = jnp.einsum('bsgc,gcd->bsgd', y, pool_w.astype(jnp.float32)).reshape(B, S, POOL_WIDTH)
    return (y * pool_scale.astype(jnp.float32)).astype(u.dtype)


def _even_layer(h, rel_bias, norm, w_in, q_gain, k_gain, pool_w, pool_scale, w_out):
    B, S, _ = h.shape
    xn = _rms_norm(h, norm)
    proj = jnp.einsum('bsd,de->bse', xn, w_in)
    q, k, v, u, z = jnp.split(proj, [ATTN_WIDTH, 2 * ATTN_WIDTH, 3 * ATTN_WIDTH,
                                     3 * ATTN_WIDTH + POOL_WIDTH], axis=-1)
    heads = lambda t: t.reshape(B, S, ATTN_HEADS, ATTN_HEAD_DIM)
    q = _rms_norm(heads(q), q_gain) * (ATTN_HEAD_DIM ** -0.5)
    k = _rms_norm(heads(k), k_gain)
    attn = _dilated_mixture_attention(q, k, heads(v), rel_bias).reshape(B, S, ATTN_WIDTH).astype(h.dtype)
    pool = _multiscale_pool(u, pool_w, pool_scale)
    mixed = jnp.concatenate([attn, pool], axis=-1) * jax.nn.silu(z)
    return jnp.einsum('bse,ed->bsd', mixed, w_out)


def _causal_depthwise_conv(x, w, b):
    K, C = w.shape
    y = lax.conv_general_dilated(x, w[:, None, :].astype(x.dtype), window_strides=(1,),
                                 padding=[(K - 1, 0)], dimension_numbers=('NWC', 'WIO', 'NWC'),
                                 feature_group_count=C)
    return y + b


def _block_diag(x, w):
    B, S, E = x.shape
    nb, bs, _ = w.shape
    return jnp.einsum('bsnc,ncd->bsnd', x.reshape(B, S, nb, bs), w).reshape(B, S, E)


def _mlstm_chunkwise(q, k, v, i_pre, f_pre):
    B, H, S, DK = q.shape
    DV = v.shape[-1]
    L = MLSTM_CHUNK
    nc = S // L
    k = k * (DK ** -0.5)
    log_f = jax.nn.log_sigmoid(f_pre)
    causal = np.tril(np.ones((L, L), dtype=bool))

    def chunks(t):
        return jnp.moveaxis(t.reshape((B, H, nc, L) + t.shape[3:]), 2, 0)

    def step(carry, inp):
        C, n, m = carry
        qc, kc, vc, ic, lfc = inp
        b = jnp.cumsum(lfc, axis=-1)
        log_d = jnp.where(causal, b[..., :, None] - b[..., None, :] + ic[..., None, :], -jnp.inf)
        log_inter = b + m[..., None]
        m_t = jnp.maximum(jnp.max(log_d, axis=-1), log_inter)
        dmat = jnp.exp(log_d - m_t[..., None])
        g = jnp.exp(log_inter - m_t)
        s = jnp.einsum('bhid,bhjd->bhij', qc, kc) * dmat
        num = jnp.einsum('bhij,bhjv->bhiv', s, vc) + g[..., None] * jnp.einsum('bhid,bhdv->bhiv', qc, C)
        den = jnp.sum(s, axis=-1) + g * jnp.einsum('bhid,bhd->bhi', qc, n)
        hc = num / jnp.maximum(jnp.abs(den), jnp.exp(-m_t))[..., None]
        m_new = m_t[..., -1]
        decay = jnp.exp(b[..., -1] + m - m_new)
        w = jnp.exp(b[..., -1:] - b + ic - m_new[..., None])
        C_new = decay[..., None, None] * C + jnp.einsum('bhj,bhjd,bhjv->bhdv', w, kc, vc)
        n_new = decay[..., None] * n + jnp.einsum('bhj,bhjd->bhd', w, kc)
        return (C_new, n_new, m_new), hc

    init = (jnp.zeros((B, H, DK, DV), jnp.float32), jnp.zeros((B, H, DK), jnp.float32),
            jnp.full((B, H), -1e30, jnp.float32))
    _, hs = lax.scan(step, init, (chunks(q), chunks(k), chunks(v), chunks(i_pre), chunks(log_f)))
    return jnp.moveaxis(hs, 0, 2).reshape(B, H, S, DV)


def _odd_layer(h, norm, w_up, conv_w, conv_b, wq, wk, wv, w_if, b_if, gn, skip, w_down):
    B, S, _ = h.shape
    f32 = jnp.float32
    xn = _rms_norm(h, norm)
    xm, z, o_pre = jnp.split(jnp.einsum('bsd,de->bse', xn, w_up), 3, axis=-1)
    xc = jax.nn.silu(_causal_depthwise_conv(xm, conv_w, conv_b))
    q = _block_diag(xc, wq)
    k = _block_diag(xc, wk)
    v = _block_diag(xm, wv)
    gates = jnp.einsum('bse,eg->bsg', jnp.concatenate([q, k, v], axis=-1), w_if) + b_if
    i_pre, f_pre = jnp.split(gates.astype(f32), 2, axis=-1)
    heads = lambda t: t.astype(f32).reshape(B, S, MLSTM_HEADS, MLSTM_HEAD_DIM).transpose(0, 2, 1, 3)
    hc = _mlstm_chunkwise(heads(q), heads(k), heads(v), i_pre.transpose(0, 2, 1), f_pre.transpose(0, 2, 1))
    mu = jnp.mean(hc, axis=-1, keepdims=True)
    var = jnp.mean(jnp.square(hc - mu), axis=-1, keepdims=True)
    hn = ((hc - mu) * lax.rsqrt(var + NORM_EPS)).transpose(0, 2, 1, 3).reshape(B, S, MLSTM_WIDTH)
    cell = jax.nn.sigmoid(o_pre.astype(f32)) * (hn * gn.astype(f32))
    out = ((cell + skip.astype(f32) * xc.astype(f32)) * jax.nn.silu(z.astype(f32))).astype(h.dtype)
    return jnp.einsum('bse,ed->bsd', out, w_down)


def setup_inputs(seed: int = 0) -> dict:
    key = jax.random.key(seed)
    ks = jax.random.split(key, 24)
    f32 = jnp.float32
    nrm = lambda kk, shape, scale: jax.random.normal(kk, shape, f32) * scale
    NE, NO, H = N_EVEN, N_ODD, MLSTM_HEADS
    b_i = nrm(ks[17], (NO, H), 0.1)
    b_f = jnp.linspace(3.0, 6.0, H, dtype=f32)[None] + nrm(ks[18], (NO, H), 0.1)
    return {
        'x': nrm(ks[0], (BATCH, SEQ, D_MODEL), 1.0),
        'rel_bias': nrm(ks[1], (REL_BUCKETS, ATTN_HEADS), 0.2),
        'e_norm': 1.0 + nrm(ks[2], (NE, D_MODEL), 0.05),
        'e_w_in': nrm(ks[3], (NE, D_MODEL, EVEN_IN_WIDTH), D_MODEL ** -0.5),
        'e_q_gain': 1.0 + nrm(ks[4], (NE, ATTN_HEAD_DIM), 0.05),
        'e_k_gain': 1.0 + nrm(ks[5], (NE, ATTN_HEAD_DIM), 0.05),
        'e_pool_w': nrm(ks[6], (NE, POOL_GROUPS, POOL_GROUP_DIM, POOL_GROUP_DIM), POOL_GROUP_DIM ** -0.5),
        'e_pool_scale': 1.0 + nrm(ks[7], (NE, POOL_WIDTH), 0.1),
        'e_w_out': nrm(ks[8], (NE, MIX_WIDTH, D_MODEL), MIX_WIDTH ** -0.5),
        'o_norm': 1.0 + nrm(ks[9], (NO, D_MODEL), 0.05),
        'o_w_up': nrm(ks[10], (NO, D_MODEL, ODD_IN_WIDTH), D_MODEL ** -0.5),
        'o_conv_w': nrm(ks[11], (NO, CONV_WIDTH, MLSTM_WIDTH), CONV_WIDTH ** -0.5),
        'o_conv_b': nrm(ks[12], (NO, MLSTM_WIDTH), 0.02),
        'o_wq': nrm(ks[13], (NO, MLSTM_WIDTH // QKV_BLOCK, QKV_BLOCK, QKV_BLOCK), QKV_BLOCK ** -0.5),
        'o_wk': nrm(ks[14], (NO, MLSTM_WIDTH // QKV_BLOCK, QKV_BLOCK, QKV_BLOCK), QKV_BLOCK ** -0.5),
        'o_wv': nrm(ks[15], (NO, MLSTM_WIDTH // QKV_BLOCK, QKV_BLOCK, QKV_BLOCK), QKV_BLOCK ** -0.5),
        'o_w_if': nrm(ks[16], (NO, 3 * MLSTM_WIDTH, 2 * H), (3 * MLSTM_WIDTH) ** -0.5),
        'o_b_if': jnp.concatenate([b_i, b_f], axis=-1),
        'o_gn': 1.0 + nrm(ks[19], (NO, MLSTM_WIDTH), 0.05),
        'o_skip': 1.0 + nrm(ks[20], (NO, MLSTM_WIDTH), 0.05),
        'o_w_down': nrm(ks[21], (NO, MLSTM_WIDTH, D_MODEL), MLSTM_WIDTH ** -0.5),
    }


def reference(x, rel_bias, e_norm, e_w_in, e_q_gain, e_k_gain, e_pool_w, e_pool_scale, e_w_out,
              o_norm, o_w_up, o_conv_w, o_conv_b, o_wq, o_wk, o_wv, o_w_if, o_b_if, o_gn, o_skip,
              o_w_down):
    h = x
    for layer in range(DEPTH):
        j = layer // 2
        if layer % 2 == 0:
            h = h + _even_layer(h, rel_bias, e_norm[j], e_w_in[j], e_q_gain[j], e_k_gain[j],
                                e_pool_w[j], e_pool_scale[j], e_w_out[j]).astype(h.dtype)
        else:
            h = h + _odd_layer(h, o_norm[j], o_w_up[j], o_conv_w[j], o_conv_b[j], o_wq[j], o_wk[j],
                               o_wv[j], o_w_if[j], o_b_if[j], o_gn[j], o_skip[j], o_w_down[j]).astype(h.dtype)
    return h
```

```python
import numpy as np
from contextlib import ExitStack
import ml_dtypes
import concourse.bass as bass
import concourse.mybir as mybir
from concourse.bass_utils import run_bass_kernel_spmd

F32 = mybir.dt.float32
BF16 = mybir.dt.bfloat16
AF = mybir.ActivationFunctionType
ALU = mybir.AluOpType
AX = mybir.AxisListType
NPBF = ml_dtypes.bfloat16

NCORES = 8
S = 8192
D = 4096
EPS = 1e-6
TN = 512
NT = S // TN

_ENGS = ("sync", "scalar", "vector", "gpsimd", "tensor")
SERIALIZE = ("scalar", "vector", "gpsimd")


class Chan:
    def __init__(self, sem, dma):
        self.sem = sem
        self.n = 0
        self.dma = dma


class Prog:
    def __init__(self, nc):
        self.nc = nc
        self.es = ExitStack()
        self.ops = {e: [] for e in _ENGS}
        self.k = 0
        self.ech = {e: self._chan(e, False) for e in _ENGS if e != "sync"}
        self.dchans = []

    def _chan(self, name, dma):
        self.k += 1
        return Chan(self.es.enter_context(self.nc.semaphore(f"s_{name}_{self.k}")), dma)

    def dchan(self, name="d"):
        c = self._chan(name, True)
        self.dchans.append(c)
        return c

    def sb(self, name, shape, dt):
        return self.es.enter_context(self.nc.sbuf_tensor(name, list(shape), dt))

    def ps(self, name, shape, dt):
        return self.es.enter_context(self.nc.psum_tensor(name, list(shape), dt))

    def op(self, eng, fn, waits=(), sig=True):
        if eng in SERIALIZE:
            sig = True
        ch = self.ech[eng] if sig else None
        tk = None
        if ch is not None:
            ch.n += 1
            tk = (ch, ch.n)
        self.ops[eng].append((fn, tuple(w for w in waits if w is not None), ch, 1))
        return tk

    def dma(self, eng, out, in_, chan, waits=()):
        chan.n += 16
        self.ops[eng].append((lambda e, o=out, i=in_: e.dma_start(out=o, in_=i),
                              tuple(w for w in waits if w is not None), chan, 16))
        return (chan, chan.n)

    def simulate(self):
        val = {}
        pos = {e: 0 for e in _ENGS}
        total = sum(len(v) for v in self.ops.values())
        done = 0
        while done < total:
            progressed = False
            for e in _ENGS:
                while pos[e] < len(self.ops[e]):
                    fn, waits, ch, inc = self.ops[e][pos[e]]
                    if any(val.get(id(c), 0) < v for c, v in waits):
                        break
                    if ch is not None:
                        val[id(ch)] = val.get(id(ch), 0) + inc
                    pos[e] += 1
                    done += 1
                    progressed = True
            if not progressed:
                msg = []
                for e in _ENGS:
                    if pos[e] < len(self.ops[e]):
                        fn, waits, ch, inc = self.ops[e][pos[e]]
                        msg.append((e, pos[e], len(self.ops[e]), [(c.sem, v, val.get(id(c), 0)) for c, v in waits if val.get(id(c), 0) < v]))
                raise RuntimeError("DEADLOCK in recorded program: %r" % (msg,))
        return True

    def run(self):
        nc = self.nc
        self.simulate()
        with nc.Block() as block:
            for name in _ENGS:
                def body(eng, name=name):
                    waited = {}
                    nsig = 0
                    for fn, waits, ch, inc in self.ops[name]:
                        if name in SERIALIZE and inc == 1 and nsig > 0:
                            waits = tuple(waits) + ((self.ech[name], nsig),)
                        if name in SERIALIZE and inc == 1:
                            nsig += 1
                        for c, v in waits:
                            if waited.get(id(c), 0) >= v:
                                continue
                            eng.wait_ge(c.sem, v)
                            waited[id(c)] = v
                        ins = fn(eng)
                        if ch is not None:
                            ins.then_inc(ch.sem, inc)
                    if name == "sync":
                        for c in self.dchans:
                            if c.n > 0:
                                eng.wait_ge(c.sem, c.n)
                getattr(block, name)(body)
        self.es.close()


class Slots:
    def __init__(self, P, name, n, shape, dt, dma=False, views=None):
        self.t = list(views) if views is not None else [P.sb(f"{name}{i}", shape, dt) for i in range(n)]
        self.ch = [P.dchan(f"{name}{i}") for i in range(n)] if dma else None
        self.free = [[] for _ in range(n)]
        self.i = -1
        self.n = n

    def next(self):
        self.i = (self.i + 1) % self.n
        fr = self.free[self.i]
        self.free[self.i] = []
        return self.i, fr


def _new_nc():
    return bass.Bass("TRN2", target_bir_lowering=False)


def _din(nc, name, shape, dt):
    return nc.dram_tensor(name, list(shape), dt, kind="ExternalInput").ap()


def _dout(nc, name, shape, dt):
    return nc.dram_tensor(name, list(shape), dt, kind="ExternalOutput").ap()


def build_prep():
    nc = _new_nc()
    hT = _din(nc, "hT", [512, S], F32)
    gain = _din(nc, "gain", [128, 4], F32)
    hg = _dout(nc, "hg", [512, S], BF16)
    ssq = _dout(nc, "ssq", [1, S], F32)
    P = Prog(nc)
    ones = P.sb("ones", [128, 1], F32)
    g = P.sb("g", [128, 4], F32)
    cst = P.dchan("cst")
    hb = Slots(P, "hb", 3, [128, 2048], F32, dma=True)
    sq = Slots(P, "sq", 2, [128, 2048], F32)
    ob = Slots(P, "ob", 3, [128, 2048], BF16, dma=True)
    st = Slots(P, "st", 2, [1, 2048], F32, dma=True)
    acc = [P.ps(f"acc{i}", [128, 512], F32) for i in range(4)]
    ldc = P.dma("sync", g[:], gain, cst)
    t_ones = P.op("vector", lambda e: e.memset(ones[:], 1.0))
    accfree = [None] * 4
    for tb in range(S // 2048):
        mm = None
        for r in range(4):
            hi, fr = hb.next()
            ld = P.dma("sync", hb.t[hi][:], hT[r * 128:(r + 1) * 128, tb * 2048:(tb + 1) * 2048], hb.ch[hi], fr)
            si, fr2 = sq.next()
            tk = P.op("scalar", lambda e, a=sq.t[si], b=hb.t[hi]: e.activation(out=a[:], in_=b[:], func=AF.Square),
                      waits=[ld] + fr2)
            oi, fr4 = ob.next()
            tg = P.op("vector", lambda e, o=ob.t[oi], a=hb.t[hi], r=r: e.tensor_scalar(
                out=o[:], in0=a[:], scalar1=g[:, r:r + 1], scalar2=None, op0=ALU.mult), waits=[ld, ldc] + fr4)
            hb.free[hi] = [tk, tg]
            sd0 = P.dma("gpsimd", hg[r * 128:(r + 1) * 128, tb * 2048:(tb + 1) * 2048], ob.t[oi][:], ob.ch[oi], [tg])
            ob.free[oi] = [sd0]
            for q in range(4):
                mm = P.op("tensor", lambda e, o=acc[q], a=sq.t[si], q=q, r=r: e.matmul(
                    o[0:1, :], lhsT=ones[:, 0:1], rhs=a[:, q * 512:(q + 1) * 512], start=(r == 0), stop=(r == 3)),
                    waits=[tk, t_ones, accfree[q] if r == 0 else None])
            sq.free[si] = [mm]
        oi, fr3 = st.next()
        cp = None
        for q in range(4):
            cp = P.op("vector", lambda e, o=st.t[oi], a=acc[q], q=q: e.tensor_copy(out=o[0:1, q * 512:(q + 1) * 512], in_=a[0:1, :]),
                      waits=[mm] + fr3)
            accfree[q] = cp
        sd = P.dma("sync", ssq[0:1, tb * 2048:(tb + 1) * 2048], st.t[oi][:], st.ch[oi], [cp])
        st.free[oi] = [sd]
    P.run()
    return nc


def build_out():
    nc = _new_nc()
    KC = 64
    mixT = _din(nc, "mixT", [8192, S], BF16)
    w = _din(nc, "w", [8192, 512], F32)
    hT = _din(nc, "hT", [512, S], F32)
    gain = _din(nc, "gain", [128, 4], F32)
    hout = _dout(nc, "hout", [512, S], F32)
    hg = _dout(nc, "hg", [512, S], BF16)
    ssq = _dout(nc, "ssq", [1, S], F32)
    P = Prog(nc)
    gsb = P.sb("gsb", [128, 4], F32)
    gb = Slots(P, "gb", 4, [128, TN], BF16, dma=True)
    wt = P.sb("wt", [128, KC, 512], BF16)
    ones = P.sb("ones", [128, 1], F32)
    G = 16
    xr = Slots(P, "xr", 4, [128, G, TN], BF16, dma=True)
    hb = Slots(P, "hb", 4, [128, TN], F32, dma=True)
    ob = Slots(P, "ob", 4, [128, TN], F32, dma=True)
    sq = Slots(P, "sq", 4, [128, TN], F32)
    st = Slots(P, "st", 2, [1, TN], F32, dma=True)
    acc = [P.ps(f"acc{i}", [128, 512], F32) for i in range(6)]
    sps = P.ps("sps", [128, 512], F32)
    accfree = [None] * 6
    wld = P.dchan("wld")
    cst = P.dchan("cst")
    ldg = P.dma("sync", gsb[:], gain, cst)
    wv = w.rearrange("(kc p) n -> p kc n", p=128)
    wtk = None
    for i in range(8):
        wtk = P.dma("gpsimd", wt[:, 8 * i:8 * i + 8, :], wv[:, 8 * i:8 * i + 8, :], wld)
    t_ones = P.op("vector", lambda e: e.memset(ones[:], 1.0))
    mv = mixT.rearrange("(kc p) t -> p kc t", p=128)
    jobs = [(t, g) for t in range(NT) for g in range(KC // G)]
    loads = {}
    nl = 0

    def issue_loads(upto):
        nonlocal nl
        while nl < min(upto, len(jobs)):
            t, g = jobs[nl]
            xi, fr = xr.next()
            tk = P.dma("sync", xr.t[xi][:], mv[:, g * G:(g + 1) * G, t * TN:(t + 1) * TN], xr.ch[xi], fr)
            loads[(t, g)] = (xi, tk)
            nl += 1

    deferred = []
    spsfree = None
    for t in range(NT):
        banks = [(4 * t + ct) % 6 for ct in range(4)]
        lastmm = [None] * 4
        hl = []
        for ct in range(4):
            hi, fr = hb.next()
            hl.append((hi, P.dma("sync", hb.t[hi][:], hT[ct * 128:(ct + 1) * 128, t * TN:(t + 1) * TN], hb.ch[hi], fr)))
        for g in range(KC // G):
            issue_loads(t * (KC // G) + g + 3)
            xi, ltk = loads[(t, g)]
            mm = None
            for ct in range(4):
                for kc in range(G):
                    first = (g == 0 and kc == 0)
                    lastk = (g == KC // G - 1 and kc == G - 1)
                    mm = P.op("tensor", lambda e, o=acc[banks[ct]], k=g * G + kc, ct=ct, x=xr.t[xi], kc=kc, f=first, l=lastk:
                              e.matmul(o[:, :], lhsT=wt[:, k, ct * 128:(ct + 1) * 128], rhs=x[:, kc, :], start=f, stop=l),
                              waits=[ltk, wtk, accfree[banks[ct]] if first else None], sig=(kc == G - 1))
                if g == KC // G - 1:
                    lastmm[ct] = mm
            xr.free[xi] = [mm]
            if g == 1 and deferred:
                for fn in deferred:
                    fn()
                deferred = []
        sqs = []
        for ct in range(4):
            hi, hld = hl[ct]
            oi, fr = ob.next()
            add = P.op("vector", lambda e, o=ob.t[oi], a=acc[banks[ct]], h=hb.t[hi]: e.tensor_tensor(
                out=o[:], in0=a[:, :], in1=h[:], op=ALU.add), waits=[lastmm[ct], hld] + fr)
            accfree[banks[ct]] = add
            hb.free[hi] = [add]
            sd = P.dma("gpsimd", hout[ct * 128:(ct + 1) * 128, t * TN:(t + 1) * TN], ob.t[oi][:], ob.ch[oi], [add])
            si, fr2 = sq.next()
            s2 = P.op("scalar", lambda e, o=sq.t[si], a=ob.t[oi]: e.activation(out=o[:], in_=a[:], func=AF.Square),
                      waits=[add] + fr2)
            gi, fr5 = gb.next()
            tg = P.op("vector", lambda e, o=gb.t[gi], a=ob.t[oi], ct=ct: e.tensor_scalar(
                out=o[:], in0=a[:], scalar1=gsb[:, ct:ct + 1], scalar2=None, op0=ALU.mult), waits=[ldg] + fr5)
            sdg = P.dma("gpsimd", hg[ct * 128:(ct + 1) * 128, t * TN:(t + 1) * TN], gb.t[gi][:], gb.ch[gi], [tg])
            gb.free[gi] = [sdg]
            ob.free[oi] = [sd, s2, tg]
            sqs.append((si, s2))

        def pe_part(t=t, sqs=sqs):
            nonlocal spsfree
            mm = None
            for ct, (si, s2) in enumerate(sqs):
                mm = P.op("tensor", lambda e, a=sq.t[si], ct=ct: e.matmul(sps[0:1, :], lhsT=ones[:, 0:1], rhs=a[:],
                                                                          start=(ct == 0), stop=(ct == 3)),
                          waits=[s2, t_ones, spsfree if ct == 0 else None])
                sq.free[si] = [mm]
            oi, fr = st.next()
            cp = P.op("vector", lambda e, o=st.t[oi]: e.tensor_copy(out=o[0:1, :], in_=sps[0:1, :]), waits=[mm] + fr)
            spsfree = cp
            sd = P.dma("gpsimd", ssq[0:1, t * TN:(t + 1) * TN], st.t[oi][:], st.ch[oi], [cp])
            st.free[oi] = [sd]
        deferred.append(pe_part)
    for fn in deferred:
        fn()
    P.run()
    return nc


def emit_rstd_scratch(P, nc, ssq_all, acc_bank):
    rs = nc.dram_tensor("rs_scratch", [128, S], F32).ap()
    ones8 = P.sb("r_ones8", [8, 128], F32)
    s8 = Slots(P, "r_s8", 2, [8, TN], F32, dma=True)
    ro = Slots(P, "r_ro", 2, [128, TN], F32, dma=True)
    t_ones = P.op("vector", lambda e: e.memset(ones8[:], 1.0))
    bfree = None
    sds = []
    for t in range(NT):
        si, fr = s8.next()
        ld = P.dma("sync", s8.t[si][:], ssq_all[:, t * TN:(t + 1) * TN], s8.ch[si], fr)
        mm = P.op("tensor", lambda e, a=s8.t[si]: e.matmul(acc_bank[:, :], lhsT=ones8[0:8, :], rhs=a[0:8, :], start=True, stop=True),
                  waits=[ld, t_ones, bfree])
        s8.free[si] = [mm]
        oi, fr2 = ro.next()
        a = P.op("vector", lambda e, o=ro.t[oi]: e.tensor_scalar(out=o[:], in0=acc_bank[:, :], scalar1=1.0 / D, scalar2=EPS,
                                                                op0=ALU.mult, op1=ALU.add), waits=[mm] + fr2)
        bfree = a
        s_ = P.op("scalar", lambda e, o=ro.t[oi]: e.activation(out=o[:], in_=o[:], func=AF.Sqrt), waits=[a])
        r_ = P.op("vector", lambda e, o=ro.t[oi]: e.reciprocal(out=o[:], in_=o[:]), waits=[s_])
        sd = P.dma("sync", rs[:, t * TN:(t + 1) * TN], ro.t[oi][:], ro.ch[oi], [r_])
        ro.free[oi] = [sd]
        sds.append(sd)
    return rs, sds, bfree


class InProj:
    def __init__(self, P, xT, w, groups, wcols, nbanks_acc, acc):
        self.P = P
        self.KC = 32
        self.G = 8
        self.xv = xT.rearrange("(kc p) t -> p kc t", p=128)
        self.wv = w.rearrange("(kc p) n -> p kc n", p=128)
        self.wt = P.sb("ip_wt", [128, self.KC, wcols], BF16)
        self.wld = P.dchan("ip_wld")
        self.xr = Slots(P, "ip_xr", 4, [128, self.G, TN], BF16, dma=True)
        self.groups = groups
        self.jobs = [(gi, t, g) for gi in range(len(groups)) for t in range(NT) for g in range(self.KC // self.G)]
        self.loads = {}
        self.nl = 0
        self.acc = acc
        self.accfree = [None] * len(acc)
        self.bk = -1
        self.wfree = None
        self.lastmm_all = None

    def issue_loads(self, upto):
        P = self.P
        while self.nl < min(upto, len(self.jobs)):
            gi, t, g = self.jobs[self.nl]
            xi, fr = self.xr.next()
            tk = P.dma("sync", self.xr.t[xi][:], self.xv[:, g * self.G:(g + 1) * self.G, t * TN:(t + 1) * TN], self.xr.ch[xi], fr)
            self.loads[(gi, t, g)] = (xi, tk)
            self.nl += 1

    def run_group(self, gi, epilogue, mid_hook=None):
        P = self.P
        col0, ncols = self.groups[gi]
        nct = ncols // 128
        wtk = None
        for i in range(4):
            wtk = P.dma("gpsimd", self.wt[:, 8 * i:8 * i + 8, 0:ncols], self.wv[:, 8 * i:8 * i + 8, col0:col0 + ncols],
                        self.wld, [self.wfree])
        NG = self.KC // self.G
        base = gi * NT * NG
        for t in range(NT):
            banks = []
            for ct in range(nct):
                self.bk = (self.bk + 1) % len(self.acc)
                banks.append(self.bk)
            lastmm = [None] * nct
            for g in range(NG):
                self.issue_loads(base + t * NG + g + 3)
                xi, ltk = self.loads[(gi, t, g)]
                mm = None
                for ct in range(nct):
                    for kc in range(self.G):
                        first = (g == 0 and kc == 0)
                        lastk = (g == NG - 1 and kc == self.G - 1)
                        mm = P.op("tensor", lambda e, o=self.acc[banks[ct]], k=g * self.G + kc, ct=ct, x=self.xr.t[xi], kc=kc, f=first, l=lastk:
                                  e.matmul(o[:, :], lhsT=self.wt[:, k, ct * 128:(ct + 1) * 128], rhs=x[:, kc, :], start=f, stop=l),
                                  waits=[ltk, wtk, self.accfree[banks[ct]] if first else None], sig=(kc == self.G - 1))
                    if g == NG - 1:
                        lastmm[ct] = mm
                self.xr.free[xi] = [mm]
                if g == 1 and mid_hook is not None:
                    mid_hook()
            self.wfree = mm
            epilogue(t, banks, lastmm)


PATTERNS = ((1, 16), (4, 4), (16, 1))


DEBUG_STOP = 0
DEBUG_G = (0, 1, 2)
DEBUG_NSB = 4
DEBUG_NOFIN = False
DEBUG_SB0 = 0
DEBUG_MIDFLUSH = False
DEBUG_PM_ENG = 'gpsimd'


def build_even():
    nc = _new_nc()
    hgT = _din(nc, "hgT", [D, S], BF16)
    ssq_all = _din(nc, "ssq_all", [8, S], F32)
    w = _din(nc, "w", [D, 3840], F32)
    gains = _din(nc, "gains", [128, 2], F32)
    btab = _din(nc, "btab", [128, 6 * 768], F32)
    mtab = _din(nc, "mtab", [128, 256], F32)
    identb = _din(nc, "identb", [128, 128], BF16)
    pw = _din(nc, "pw", [512, 256], F32)
    pvec = _din(nc, "pvec", [128, 8], F32)
    rctab = _din(nc, "rctab", [128, 1024], F32)
    mix = _dout(nc, "mix", [1024, S], BF16)
    P = Prog(nc)

    A = [P.ps(f"A{i}", [128, 512], F32) for i in range(4)]
    N = [P.ps(f"N{i}", [128, 512], F32) for i in range(2)]
    Tb = [P.ps(f"Tps{i}", [128, 1024], BF16) for i in range(2)]

    rs, rs_sds, nfree0 = emit_rstd_scratch(P, nc, ssq_all, N[0])

    cst = P.dchan("cst")
    gsb = P.sb("gsb", [128, 2], F32)
    msb = P.sb("msb", [128, 256], F32)
    idb = P.sb("idb", [128, 128], BF16)
    pvs = P.sb("pvs", [128, 8], F32)
    rcs = P.sb("rcs", [128, 1024], F32)
    pwb = P.sb("pwb", [128, 4, 256], BF16)
    P.dma("sync", gsb[:], gains, cst)
    P.dma("sync", msb[:], mtab, cst)
    P.dma("sync", idb[:], identb, cst)
    P.dma("sync", pvs[:], pvec, cst)
    ldc0 = P.dma("sync", rcs[:], rctab, cst)
    cst2 = P.dchan("cst2")
    ldc = P.dma("gpsimd", pwb[:], pw.rearrange("(kc p) n -> p kc n", p=128), cst2)
    onesq = P.sb("onesq", [128, 128], BF16)
    onesk = P.sb("onesk", [128, 128], BF16)
    epsb = P.sb("epsb", [128, 2], F32)
    P.op("vector", lambda e: e.memset(onesq[:], 1.0))
    P.op("vector", lambda e: e.memset(epsb[:, 0:1], 128.0 * EPS))
    P.op("vector", lambda e: e.memset(epsb[:, 1:2], EPS))
    t_ones = P.op("vector", lambda e: e.memset(onesk[:], 1.0 / 128.0), waits=[ldc0, ldc])

    qT = P.sb("qT", [128, S], BF16)
    kT = P.sb("kT", [128, S], BF16)
    vT = P.sb("vT", [128, S], BF16)
    sz = P.sb("sz", [128, S], BF16)
    accb = P.sb("accb", [128, 2, 2048], F32)
    Eh = [P.sb(f"Eh{i}", [128, 768], F32) for i in range(2)]
    ech = [P.dchan(f"ech{i}") for i in range(2)]
    rsr = Slots(P, "rsr", 2, [128, TN], F32, dma=True)
    f32t = Slots(P, "f32t", 4, [128, TN], F32)
    sqb = Slots(P, "sqb", 4, [128, TN], BF16)
    rqb = Slots(P, "rqb", 3, [128, TN], F32)
    esb = Slots(P, "esb", 3, [128, 256], F32)
    pTb = Slots(P, "pTb", 3, [128, 256], BF16)
    vbb = Slots(P, "vbb", 6, [128, 128], BF16)
    ost = Slots(P, "ost", 2, [128, 2048], BF16, dma=True)

    if DEBUG_STOP == 1:
        P.run()
        return nc
    groups = [(hh * 512, 512) for hh in range(6)] + [(3584, 256), (3072, 512)]
    ip = InProj(P, hgT, w, groups, 512, 5, A)
    nfree = [nfree0, None]
    rs_ready = rs_sds

    deferred = []
    state = {"last_dve": None, "last_act": None}

    def flush_deferred():
        nonlocal deferred
        d, deferred = deferred, []
        for fn in d:
            fn()

    def head_epilogue(t, banks, lastmm):
        hdw = list(state.get("hd", [])) if t == 0 else []
        ri, fr = rsr.next()
        rtk = P.dma("sync", rsr.t[ri][:], rs[:, t * TN:(t + 1) * TN], rsr.ch[ri], list(fr) + (rs_ready if t == 0 else []))
        rst = rsr.t[ri]
        sl = slice(t * TN, (t + 1) * TN)
        users = []
        for which in range(2):
            b = banks[which]
            fi, ffr = f32t.next()
            xs = P.op("vector", lambda e, o=f32t.t[fi], a=A[b]: e.tensor_tensor(out=o[:], in0=a[:, :], in1=rst[:], op=ALU.mult),
                      waits=[lastmm[which], rtk] + ffr + hdw)
            ip.accfree[b] = xs
            users.append(xs)
            si, sfr = sqb.next()
            sq_ = P.op("scalar", lambda e, o=sqb.t[si], a=f32t.t[fi]: e.activation(out=o[:], in_=a[:], func=AF.Square),
                       waits=[xs] + sfr)

            def post(which=which, fi=fi, si=si, sq_=sq_, sl=sl):
                nb = which
                mm = P.op("tensor", lambda e: e.matmul(N[nb][:, :], lhsT=(onesq if which == 0 else onesk)[:, :], rhs=sqb.t[si][:],
                                                       start=True, stop=True), waits=[sq_, t_ones, nfree[nb]])
                sqb.free[si] = [mm]
                qi, qfr = rqb.next()
                rq0 = P.op("scalar", lambda e, o=rqb.t[qi]: e.activation(
                    out=o[:], in_=N[nb][:, :], func=AF.Sqrt, bias=epsb[:, which:which + 1]), waits=[mm, t_ones] + qfr)
                nfree[nb] = rq0
                rq = P.op("vector", lambda e, o=rqb.t[qi]: e.reciprocal(out=o[:], in_=o[:]), waits=[rq0])
                dst = qT if which == 0 else kT
                qn = P.op("vector", lambda e, a=f32t.t[fi], r=rqb.t[qi]: e.scalar_tensor_tensor(
                    out=dst[:, sl], in0=a[:], scalar=gsb[:, which:which + 1], in1=r[:], op0=ALU.mult, op1=ALU.mult))
                f32t.free[fi] = [qn]
                rqb.free[qi] = [qn]
                state["last_dve"] = qn
            deferred.append(post)
        b = banks[2]
        vv = P.op("vector", lambda e, a=A[b]: e.tensor_tensor(out=vT[:, sl], in0=a[:, :], in1=rst[:], op=ALU.mult),
                  waits=[lastmm[2], rtk])
        ip.accfree[b] = vv
        b = banks[3]
        fi, ffr = f32t.next()
        zs = P.op("vector", lambda e, o=f32t.t[fi], a=A[b]: e.tensor_tensor(out=o[:], in0=a[:, :], in1=rst[:], op=ALU.mult),
                  waits=[lastmm[3], rtk] + ffr)
        ip.accfree[b] = zs
        zz = P.op("scalar", lambda e, a=f32t.t[fi]: e.activation(out=sz[:, sl], in_=a[:], func=AF.Silu), waits=[zs] + hdw)
        f32t.free[fi] = [zz]
        rsr.free[ri] = [zs]
        state["last_dve"] = zs
        state["last_act"] = zz

    def blk(ap_t, d, r, cb):
        return ap_t[:].rearrange("p (n d) -> p d n", d=d)[:, r, cb * 128:(cb + 1) * 128]

    tps_free = [None] * 2
    tps_i = [0]

    def attention(hh, ready):
        eb = Eh[hh % 2]
        ld = P.dma("sync", eb[:], btab[:, hh * 768:(hh + 1) * 768], ech[hh % 2], state.get("efree%d" % (hh % 2), []))
        last = None
        for g in range(3):
            last = P.op("vector", lambda e, g=g: e.tensor_tensor(out=eb[:, g * 256:(g + 1) * 256], in0=eb[:, g * 256:(g + 1) * 256],
                                                               in1=msb[:], op=ALU.add), waits=[ld, ldc, ldc0])
        e_ready = P.op("scalar", lambda e: e.activation(out=eb[:], in_=eb[:], func=AF.Exp), waits=[last])
        Sps = [A[0], A[1]]
        Ops = [A[2], A[3]]
        sfree = [ip.accfree[0], ip.accfree[1]]
        ofree = [ip.accfree[2], ip.accfree[3]]
        vcache = {}
        ucount = [0]
        pend = [None]
        last_pool = [None]

        def get_v(g, d, r, cb):
            key = (g, r, cb)
            if key in vcache:
                return vcache[key]
            ti = tps_i[0] % 2
            tps_i[0] += 1
            tr = P.op("tensor", lambda e: e.transpose(Tb[ti][:, 0:128], blk(vT, d, r, cb), idb[:]),
                      waits=list(ready) + [tps_free[ti], ldc, ldc0])
            vi, vfr = vbb.next()
            for k_ in [k_ for k_, v_ in vcache.items() if v_[0] == vi]:
                del vcache[k_]
            ev = P.op("scalar", lambda e, o=vbb.t[vi]: e.activation(out=o[:], in_=Tb[ti][:, 0:128], func=AF.Copy),
                      waits=[tr] + vfr)
            tps_free[ti] = ev
            vcache[key] = (vi, ev)
            return vcache[key]

        def stage34(u):
            (g, d, r, cl, cb, has_prev, pi, ptk, vprev, vcur, SBi) = u
            o = ucount[0] % 2
            ucount[0] += 1
            lo = 0 if has_prev else 128
            waits = [ptk, ofree[o], t_ones]
            mm = None
            if has_prev:
                mm = P.op("tensor", lambda e: e.matmul(Ops[o][:, 0:128], lhsT=vbb.t[vprev[0]][:], rhs=pTb.t[pi][:, 0:128],
                                                       start=True, stop=False), waits=waits + [vprev[1]], sig=False)
            mm = P.op("tensor", lambda e: e.matmul(Ops[o][:, 0:128], lhsT=vbb.t[vcur[0]][:], rhs=pTb.t[pi][:, 128:256],
                                                   start=(not has_prev), stop=True), waits=waits + [vcur[1]], sig=False)
            if has_prev:
                mm = P.op("tensor", lambda e: e.matmul(Ops[o][:, 128:256], lhsT=onesq[:, :], rhs=pTb.t[pi][:, 0:128],
                                                       start=True, stop=False), sig=False)
            mm = P.op("tensor", lambda e: e.matmul(Ops[o][:, 128:256], lhsT=onesq[:, :], rhs=pTb.t[pi][:, 128:256],
                                                   start=(not has_prev), stop=True))
            pTb.free[pi] = [mm]
            if has_prev:
                vbb.free[vprev[0]].append(mm)
            vbb.free[vcur[0]].append(mm)
            dst = accb[:].rearrange("p a (n d) -> p a d n", d=d)[:, :, r, cl * 128:(cl + 1) * 128]
            src = Ops[o][:, 0:256].rearrange("p (a n) -> p a n", a=2)
            if g == 0:
                ac = P.op("vector", lambda e: e.tensor_copy(out=dst, in_=src), waits=[mm])
            else:
                ac = P.op("vector", lambda e: e.tensor_tensor(out=dst, in0=dst, in1=src, op=ALU.add), waits=[mm])
            ofree[o] = ac
            state["last_dve"] = ac

        def finalize(SBi):
            tsl = slice(SBi * 2048, (SBi + 1) * 2048)
            P.op("vector", lambda e: e.reciprocal(out=accb[:, 1, :], in_=accb[:, 1, :]), sig=False)
            P.op("vector", lambda e: e.tensor_tensor(out=accb[:, 0, :], in0=accb[:, 0, :], in1=accb[:, 1, :], op=ALU.mult), sig=False)
            oi, ofr = ost.next()
            fin = P.op("vector", lambda e, o=ost.t[oi], tsl=tsl: e.tensor_tensor(out=o[:], in0=accb[:, 0, :], in1=sz[:, tsl], op=ALU.mult),
                       waits=ofr)
            sd = P.dma("sync", mix[hh * 128:(hh + 1) * 128, tsl], ost.t[oi][:], ost.ch[oi], [fin])
            ost.free[oi] = [sd]
            state["last_dve"] = fin

        for SBi in range(DEBUG_SB0, DEBUG_NSB):
            for g, (d, nb) in enumerate(PATTERNS):
                if g not in DEBUG_G:
                    continue
                for r in range(d):
                    for cl in range(nb):
                        cb = SBi * nb + cl
                        has_prev = cb >= 1
                        s_ = (ucount[0] + (1 if pend[0] is not None else 0)) % 2
                        lo = 0 if has_prev else 128
                        mm = None
                        if has_prev:
                            mm = P.op("tensor", lambda e, d=d, r=r, cb=cb, s_=s_: e.matmul(
                                Sps[s_][:, 0:128], lhsT=blk(kT, d, r, cb - 1), rhs=blk(qT, d, r, cb), start=True, stop=True),
                                waits=list(ready) + [sfree[s_]], sig=False)
                        mm = P.op("tensor", lambda e, d=d, r=r, cb=cb, s_=s_: e.matmul(
                            Sps[s_][:, 128:256], lhsT=blk(kT, d, r, cb), rhs=blk(qT, d, r, cb), start=True, stop=True),
                            waits=list(ready) + [sfree[s_]])
                        vprev = get_v(g, d, r, cb - 1) if has_prev else None
                        vcur = get_v(g, d, r, cb)
                        ei, efr = esb.next()
                        ex = P.op("scalar", lambda e, s_=s_, ei=ei, lo=lo: e.activation(out=esb.t[ei][:, lo:256], in_=Sps[s_][:, lo:256], func=AF.Exp),
                                  waits=[mm] + efr)
                        sfree[s_] = ex
                        pi, pfr = pTb.next()
                        pm = P.op(DEBUG_PM_ENG, lambda e, ei=ei, pi=pi, lo=lo, g=g: e.tensor_tensor(
                            out=pTb.t[pi][:, lo:256], in0=esb.t[ei][:, lo:256], in1=eb[:, g * 256 + lo:(g + 1) * 256], op=ALU.mult),
                            waits=[ex, e_ready] + pfr)
                        esb.free[ei] = [pm]
                        last_pool[0] = pm
                        u = (g, d, r, cl, cb, has_prev, pi, pm, vprev, vcur, SBi)
                        if pend[0] is not None:
                            stage34(pend[0])
                            if pend[0][-1] != SBi:
                                finalize(pend[0][-1])
                        pend[0] = u
            if DEBUG_MIDFLUSH:
                stage34(pend[0])
                finalize(pend[0][-1])
                pend[0] = None
        if pend[0] is not None:
            stage34(pend[0])
            finalize(pend[0][-1])
        pend[0] = None
        state["efree%d" % (hh % 2)] = [last_pool[0]]
        ip.accfree[0] = sfree[0]
        ip.accfree[1] = sfree[1]
        ip.accfree[2] = ofree[0]
        ip.accfree[3] = ofree[1]
        return [state["last_dve"], mm]

    head_done = []
    for hh in range(6):
        state["hd"] = head_done
        ip.run_group(hh, head_epilogue, mid_hook=flush_deferred)
        flush_deferred()
        if DEBUG_STOP == 2 + 10 * hh:
            P.run()
            return nc
        ready = [state["last_dve"], state["last_act"]]
        head_done = attention(hh, ready)
        if DEBUG_STOP == 3 + 10 * hh:
            P.run()
            return nc

    szp = [qT, kT]

    def zp_epilogue(t, banks, lastmm):
        ri, fr = rsr.next()
        rtk = P.dma("sync", rsr.t[ri][:], rs[:, t * TN:(t + 1) * TN], rsr.ch[ri], fr)
        rst = rsr.t[ri]
        sl = slice(t * TN, (t + 1) * TN)
        zs = None
        for oc in range(2):
            b = banks[oc]
            fi, ffr = f32t.next()
            zs = P.op("vector", lambda e, o=f32t.t[fi], a=A[b]: e.tensor_tensor(out=o[:], in0=a[:, :], in1=rst[:], op=ALU.mult),
                      waits=[lastmm[oc], rtk] + ffr + (head_done if t == 0 else []))
            ip.accfree[b] = zs
            zz = P.op("scalar", lambda e, a=f32t.t[fi], oc=oc: e.activation(out=szp[oc][:, sl], in_=a[:], func=AF.Silu),
                      waits=[zs] + (head_done if t == 0 else []))
            f32t.free[fi] = [zz]
            state["last_act"] = zz
        rsr.free[ri] = [zs]

    ip.run_group(6, zp_epilogue)
    if DEBUG_STOP == 70:
        P.run()
        return nc

    W_ = 16 + TN
    flat = accb[:].rearrange("p a n -> p (a n)")
    ubv = [flat[:, c * W_:(c + 1) * W_] for c in range(4)]
    tA = flat[:, 4 * W_:5 * W_]
    tB = flat[:, 5 * W_:6 * W_]
    pooled = Slots(P, "pooled", 2, None, BF16, views=[sz[:, i * 2048:(i + 1) * 2048].rearrange("p (c n) -> p c n", c=4) for i in range(2)])
    pout = Slots(P, "pout", 3, None, BF16, dma=True, views=[sz[:, 4096 + i * TN:4096 + (i + 1) * TN] for i in range(3)])
    pdefer = []

    def u_epilogue(t, banks, lastmm):
        ri, fr = rsr.next()
        rtk = P.dma("sync", rsr.t[ri][:], rs[:, t * TN:(t + 1) * TN], rsr.ch[ri], fr)
        rst = rsr.t[ri]
        sl = slice(t * TN, (t + 1) * TN)
        pi, pfr = pooled.next()
        last = None
        for c in range(4):
            b = banks[c]
            u_ = ubv[c]
            if t == 0:
                P.op("vector", lambda e, u_=u_: e.memset(u_[:, 0:16], 0.0), sig=False, waits=head_done)
            else:
                P.op("vector", lambda e, u_=u_: e.tensor_copy(out=u_[:, 0:16], in_=u_[:, TN:TN + 16]), sig=False)
            ev = P.op("vector", lambda e, u_=u_, a=A[b]: e.tensor_tensor(out=u_[:, 16:16 + TN], in0=a[:, :], in1=rst[:], op=ALU.mult),
                      waits=[lastmm[c], rtk])
            ip.accfree[b] = ev
            P.op("vector", lambda e, u_=u_: e.scalar_tensor_tensor(out=tA[:, 2:W_], in0=u_[:, 1:W_ - 1], scalar=pvs[:, 0:1], in1=u_[:, 2:W_],
                                                                  op0=ALU.mult, op1=ALU.add), sig=False)
            P.op("vector", lambda e: e.scalar_tensor_tensor(out=tB[:, 4:W_], in0=tA[:, 2:W_ - 2], scalar=pvs[:, 1:2], in1=tA[:, 4:W_],
                                                           op0=ALU.mult, op1=ALU.add), sig=False)
            P.op("vector", lambda e: e.scalar_tensor_tensor(out=tA[:, 8:W_], in0=tB[:, 4:W_ - 4], scalar=pvs[:, 2:3], in1=tB[:, 8:W_],
                                                           op0=ALU.mult, op1=ALU.add), sig=False)
            P.op("vector", lambda e: e.scalar_tensor_tensor(out=tB[:, 16:W_], in0=tA[:, 8:W_ - 8], scalar=pvs[:, 3:4], in1=tA[:, 16:W_],
                                                           op0=ALU.mult, op1=ALU.add), sig=False)
            rc = rcs[:, 0:TN] if t == 0 else rcs[:, TN:2 * TN]
            P.op("vector", lambda e, rc=rc: e.tensor_tensor(out=tA[:, 16:W_], in0=tB[:, 16:W_], in1=rc, op=ALU.mult), sig=False)
            last = P.op("vector", lambda e, u_=u_, c=c, pi=pi: e.tensor_tensor(out=pooled.t[pi][:, c, :], in0=tA[:, 16:W_], in1=u_[:, 16:W_],
                                                                             op=ALU.subtract), waits=(pfr + head_done) if c == 0 else [])
        rsr.free[ri] = [last]

        def post(pi=pi, last=last, sl=sl, t=t):
            for oc in range(2):
                mm = None
                for kc in range(4):
                    mm = P.op("tensor", lambda e, oc=oc, kc=kc: e.matmul(N[oc][:, :], lhsT=pwb[:, kc, oc * 128:(oc + 1) * 128],
                                                                          rhs=pooled.t[pi][:, kc, :], start=(kc == 0), stop=(kc == 3)),
                              waits=[last, ldc, ldc0, nfree[oc]], sig=(kc == 3))
                oi, ofr = pout.next()
                fo = P.op("vector", lambda e, oc=oc, o=pout.t[oi]: e.scalar_tensor_tensor(
                    out=o[:], in0=N[oc][:, :], scalar=pvs[:, 4 + oc:5 + oc], in1=szp[oc][:, sl], op0=ALU.mult, op1=ALU.mult),
                    waits=[mm, state["last_act"]] + ofr)
                nfree[oc] = fo
                sd = P.dma("sync", mix[768 + oc * 128:768 + (oc + 1) * 128, sl], pout.t[oi][:], pout.ch[oi], [fo])
                pout.free[oi] = [sd]
            pooled.free[pi] = [mm]
        pdefer.append(post)

    def flush_p():
        nonlocal pdefer
        d, pdefer = pdefer, []
        for fn in d:
            fn()

    ip.run_group(7, u_epilogue, mid_hook=flush_p)
    flush_p()
    P.run()
    return nc


def build_odd1():
    nc = _new_nc()
    hgT = _din(nc, "hgT", [D, S], BF16)
    ssq_all = _din(nc, "ssq_all", [8, S], F32)
    w = _din(nc, "w", [D, 3072], F32)
    cvec = _din(nc, "cvec", [128, 8, 8], F32)
    bd = _din(nc, "bd", [128, 3, 8, 128], F32)
    wif = _din(nc, "wif", [128, 3, 8, 16], F32)
    qT_o = _dout(nc, "qT", [1024, S], BF16)
    kT_o = _dout(nc, "kT", [1024, S], BF16)
    vT_o = _dout(nc, "vT", [1024, S], BF16)
    A_o = _dout(nc, "Ao", [1024, S], BF16)
    B_o = _dout(nc, "Bo", [1024, S], BF16)
    gp_o = _dout(nc, "gpart", [16, S], F32)
    P = Prog(nc)
    A = [P.ps(f"A{i}", [128, 512], F32) for i in range(4)]
    Q = [P.ps(f"Q{i}", [128, 512], F32) for i in range(3)]
    Gp = P.ps("Gp", [128, 512], F32)
    rs, rs_sds, qfree0 = emit_rstd_scratch(P, nc, ssq_all, Q[0])
    cst = P.dchan("cst")
    cst2 = P.dchan("cst2")
    cv = P.sb("cv", [128, 8, 8], F32)
    bdb = P.sb("bdb", [128, 3, 8, 128], BF16)
    wifb = P.sb("wifb", [128, 3, 8, 16], BF16)
    ldc0 = P.dma("sync", cv[:], cvec, cst)
    P.dma("gpsimd", bdb[:], bd, cst2)
    ldc = P.dma("gpsimd", wifb[:], wif, cst2)
    gacc = P.sb("gacc", [16, S], F32)
    t_c = P.op("vector", lambda e: e.memset(gacc[:, 0:1], 0.0), waits=[ldc0, ldc])
    rsr = Slots(P, "rsr", 2, [128, TN], F32, dma=True)
    xmb = Slots(P, "xmb", 2, [128, TN + 3], F32)
    f32a = Slots(P, "f32a", 8, [128, TN], F32)
    bfa = Slots(P, "bfa", 6, [128, TN], BF16)
    outb = Slots(P, "outb", 8, [128, TN], BF16, dma=True)
    groups = [(j * 384, 384) for j in range(8)]
    ip = InProj(P, hgT, w, groups, 384, 4, A)
    qfree = [qfree0, None, None]
    gfree = [None]
    deferred = []
    prev_xm = [None]

    def flush():
        nonlocal deferred
        d, deferred = deferred, []
        for fn in d:
            fn()

    def make_epilogue(j):
        def epilogue(t, banks, lastmm):
            ri, fr = rsr.next()
            rtk = P.dma("sync", rsr.t[ri][:], rs[:, t * TN:(t + 1) * TN], rsr.ch[ri], list(fr) + (rs_sds if (j == 0 and t == 0) else []))
            rst = rsr.t[ri]
            sl = slice(t * TN, (t + 1) * TN)
            xi, xfr = xmb.next()
            xm_ = xmb.t[xi]
            if t == 0:
                P.op("vector", lambda e: e.memset(xm_[:, 0:3], 0.0), waits=xfr)
            else:
                pxm = prev_xm[0]
                P.op("vector", lambda e: e.tensor_copy(out=xm_[:, 0:3], in_=pxm[:, TN:TN + 3]), waits=xfr)
            ev = P.op("vector", lambda e: e.tensor_tensor(out=xm_[:, 3:3 + TN], in0=A[banks[0]][:, :], in1=rst[:], op=ALU.mult),
                      waits=[lastmm[0], rtk, t_c])
            ip.accfree[banks[0]] = ev
            prev_xm[0] = xm_
            bi, bfr = bfa.next()
            xm_bf = bfa.t[bi]
            xmc = P.op("scalar", lambda e: e.activation(out=xm_bf[:], in_=xm_[:, 3:3 + TN], func=AF.Copy), waits=[ev] + bfr)
            ci, cfr = f32a.next()
            pre = f32a.t[ci]
            P.op("vector", lambda e: e.tensor_scalar(out=pre[:], in0=xm_[:, 0:TN], scalar1=cv[:, j, 0:1], scalar2=None, op0=ALU.mult), waits=cfr)
            for k_ in (1, 2, 3):
                cvl = P.op("vector", lambda e, k_=k_: e.scalar_tensor_tensor(out=pre[:], in0=xm_[:, k_:k_ + TN], scalar=cv[:, j, k_:k_ + 1], in1=pre[:],
                                                                             op0=ALU.mult, op1=ALU.add))
            xmb.free[xi] = [cvl, xmc]
            si, sfr = f32a.next()
            sg = f32a.t[si]
            sgt = P.op("scalar", lambda e: e.activation(out=sg[:], in_=pre[:], func=AF.Sigmoid, bias=cv[:, j, 4:5]), waits=[cvl] + sfr)
            xci, xcfr = f32a.next()
            xc = f32a.t[xci]
            xct = P.op("vector", lambda e: e.scalar_tensor_tensor(out=xc[:], in0=pre[:], scalar=cv[:, j, 4:5], in1=sg[:], op0=ALU.add, op1=ALU.mult),
                       waits=[sgt] + xcfr)
            f32a.free[ci] = [xct]
            f32a.free[si] = [xct]
            bi2, bfr2 = bfa.next()
            xc_bf = bfa.t[bi2]
            xcc = P.op("scalar", lambda e: e.activation(out=xc_bf[:], in_=xc[:], func=AF.Copy), waits=[xct] + bfr2)
            zi, zfr = f32a.next()
            zs = f32a.t[zi]
            zt = P.op("vector", lambda e: e.tensor_tensor(out=zs[:], in0=A[banks[1]][:, :], in1=rst[:], op=ALU.mult), waits=[lastmm[1], rtk] + zfr)
            ip.accfree[banks[1]] = zt
            gi, gfr = f32a.next()
            sgz = f32a.t[gi]
            sgzt = P.op("scalar", lambda e: e.activation(out=sgz[:], in_=zs[:], func=AF.Sigmoid), waits=[zt] + gfr)
            szt = P.op("vector", lambda e: e.tensor_tensor(out=zs[:], in0=zs[:], in1=sgz[:], op=ALU.mult), waits=[sgzt])
            f32a.free[gi] = [szt]
            oi, ofr = f32a.next()
            os_ = f32a.t[oi]
            ot = P.op("vector", lambda e: e.tensor_tensor(out=os_[:], in0=A[banks[2]][:, :], in1=rst[:], op=ALU.mult), waits=[lastmm[2], rtk] + ofr)
            ip.accfree[banks[2]] = ot
            rsr.free[ri] = [ot]
            ogt = P.op("scalar", lambda e: e.activation(out=os_[:], in_=os_[:], func=AF.Sigmoid), waits=[ot])
            ai, afr = outb.next()
            at = P.op("vector", lambda e: e.scalar_tensor_tensor(out=outb.t[ai][:], in0=os_[:], scalar=cv[:, j, 5:6], in1=zs[:], op0=ALU.mult, op1=ALU.mult),
                      waits=[ogt, szt] + afr)
            outb.free[ai] = [P.dma("sync", A_o[j * 128:(j + 1) * 128, sl], outb.t[ai][:], outb.ch[ai], [at])]
            f32a.free[oi] = [at]
            bi3, bfr3 = outb.next()
            bt = P.op("vector", lambda e: e.scalar_tensor_tensor(out=outb.t[bi3][:], in0=xc[:], scalar=cv[:, j, 6:7], in1=zs[:], op0=ALU.mult, op1=ALU.mult),
                      waits=bfr3)
            outb.free[bi3] = [P.dma("sync", B_o[j * 128:(j + 1) * 128, sl], outb.t[bi3][:], outb.ch[bi3], [bt])]
            f32a.free[xci] = [bt, xcc]
            f32a.free[zi] = [bt]

            def post():
                srcs = [(0, xc_bf, xcc, qT_o), (1, xc_bf, xcc, kT_o), (2, xm_bf, xmc, vT_o)]
                evs = []
                lastm = [None, None, None]
                for which, src, stk, dst in srcs:
                    mm = P.op("tensor", lambda e, which=which, src=src: e.matmul(Q[which][:, :], lhsT=bdb[:, which, j, :], rhs=src[:],
                                                                                 start=True, stop=True), waits=[stk, ldc, qfree[which]])
                    lastm[which] = mm
                    oi2, ofr2 = outb.next()
                    ev2 = P.op("scalar", lambda e, which=which, oi2=oi2: e.activation(out=outb.t[oi2][:], in_=Q[which][:, :], func=AF.Copy),
                               waits=[mm] + ofr2)
                    qfree[which] = ev2
                    sdq = P.dma("sync", dst[j * 128:(j + 1) * 128, sl], outb.t[oi2][:], outb.ch[oi2], [ev2])
                    evs.append((oi2, ev2, sdq))
                bfa.free[bi] = [lastm[2]]
                bfa.free[bi2] = [lastm[0], lastm[1]]
                gm = None
                for which, (oi2, ev2, sdq) in enumerate(evs):
                    gm = P.op("tensor", lambda e, which=which, oi2=oi2: e.matmul(Gp[0:16, :], lhsT=wifb[:, which, j, :], rhs=outb.t[oi2][:],
                                                                                 start=(which == 0), stop=(which == 2)),
                              waits=[ev2, ldc, gfree[0] if which == 0 else None])
                for (oi2, ev2, sdq) in evs:
                    outb.free[oi2] = [sdq, gm]
                if j == 0:
                    ga = P.op("vector", lambda e: e.tensor_copy(out=gacc[:, sl], in_=Gp[0:16, :]), waits=[gm])
                else:
                    ga = P.op("vector", lambda e: e.tensor_tensor(out=gacc[:, sl], in0=gacc[:, sl], in1=Gp[0:16, :], op=ALU.add), waits=[gm])
                gfree[0] = ga
            deferred.append(post)
        return epilogue

    for j in range(8):
        ip.run_group(j, make_epilogue(j), mid_hook=flush)
    flush()
    gch = P.dchan("gch")
    P.dma("sync", gp_o, gacc[:], gch, [gfree[0]])
    P.run()
    return nc


LN32 = 3.4657359027997265


def build_odd2():
    nc = _new_nc()
    qT = _din(nc, "qT", [1024, S], BF16)
    kT = _din(nc, "kT", [1024, S], BF16)
    ktm = _din(nc, "ktm", [S, 1024], BF16)
    vtm = _din(nc, "vtm", [S, 1024], BF16)
    Ai = _din(nc, "Ao", [1024, S], BF16)
    Bi = _din(nc, "Bo", [1024, S], BF16)
    gp_all = _din(nc, "gp_all", [128, S], F32)
    sel = _din(nc, "sel", [128, 2], F32)
    bsel = _din(nc, "bsel", [128, 2], F32)
    tri_i = _din(nc, "tri", [128, 128], F32)
    idf_i = _din(nc, "identf", [128, 128], F32)
    mix = _dout(nc, "mix", [1024, S], BF16)
    P = Prog(nc)
    Sps = P.ps("Sps", [128, 512], F32)
    NUM = [P.ps(f"NUM{i}", [128, 512], F32) for i in range(2)]
    SM = P.ps("SM", [128, 512], F32)
    DC = [P.ps(f"DC{i}", [128, 512], F32) for i in range(2)]
    TP = [P.ps(f"TP{i}", [128, 512], F32) for i in range(2)]
    NC_ = 64
    cst = P.dchan("cst")
    C = P.sb("C", [128, 8200], F32)
    Cv = C[:, 0:8200].rearrange("p (d n) -> p d n", d=8)
    Cbf = P.sb("Cbf", [128, 8200], BF16)
    Cbv = Cbf[:, 0:8200].rearrange("p (d n) -> p d n", d=8)
    sel_s = P.sb("sel_s", [128, 2], F32)
    bs_s = P.sb("bs_s", [128, 2], F32)
    tri = P.sb("tri_s", [128, 128], F32)
    idf = P.sb("idf", [128, 128], F32)
    P.dma("sync", C[:, 0:S], gp_all, cst)
    P.dma("sync", sel_s[:], sel, cst)
    P.dma("sync", bs_s[:], bsel, cst)
    P.dma("sync", tri[:], tri_i, cst)
    ldc = P.dma("sync", idf[:], idf_i, cst)
    ones_r = P.sb("ones_r", [1, 128], F32)
    ones_b = P.sb("ones_b", [128, 1], BF16)
    epsb = P.sb("epsb", [128, 1], F32)
    sm = {n: P.sb("g_" + n, [128, 64], F32) for n in ("ipre", "fpre", "lf", "b", "a", "w", "g", "fl", "Mb", "mhb", "t1")}
    rowA = P.sb("rowA", [1, 64], F32)
    rowB = P.sb("rowB", [1, 64], F32)
    rowM = P.sb("rowM", [1, 192], F32)
    amT = P.sb("amT", [64, 1], F32)
    P.op("vector", lambda e: e.memset(ones_r[:], 1.0))
    P.op("vector", lambda e: e.memset(ones_b[:], 1.0))
    P.op("vector", lambda e: e.memset(epsb[:], EPS))
    t0 = P.op("vector", lambda e: e.memset(rowM[:], -80.0), waits=[ldc])
    mm = None
    for cc in range(NC_):
        mm = P.op("tensor", lambda e, cc=cc: e.matmul(SM[:, 2 * cc:2 * cc + 2], lhsT=C[:, cc * 128:(cc + 1) * 128], rhs=sel_s[:, 0:2],
                                                      start=True, stop=True), waits=[ldc], sig=(cc == NC_ - 1))
    SMv = SM[:, 0:128].rearrange("p (c two) -> p c two", two=2)
    P.op("vector", lambda e: e.tensor_scalar(out=sm["ipre"][:], in0=SMv[:, :, 0], scalar1=bs_s[:, 0:1], scalar2=None, op0=ALU.add), waits=[mm, t0])
    f1 = P.op("vector", lambda e: e.tensor_scalar(out=sm["fpre"][:], in0=SMv[:, :, 1], scalar1=bs_s[:, 1:2], scalar2=None, op0=ALU.add))
    a1 = P.op("scalar", lambda e: e.activation(out=sm["t1"][:], in_=sm["fpre"][:], func=AF.Exp, scale=-1.0), waits=[f1])
    a2 = P.op("scalar", lambda e: e.activation(out=sm["t1"][:], in_=sm["t1"][:], func=AF.Ln, bias=1.0))
    d1 = P.op("vector", lambda e: e.tensor_scalar(out=sm["lf"][:], in0=sm["t1"][:], scalar1=-1.0, scalar2=None, op0=ALU.mult), waits=[a2])
    m1 = P.op("tensor", lambda e: e.matmul(SM[:, 128:192], lhsT=tri[:, :], rhs=sm["lf"][:], start=True, stop=True), waits=[d1, f1])
    d2 = P.op("vector", lambda e: e.tensor_copy(out=sm["b"][:], in_=SM[:, 128:192]), waits=[m1])
    d3 = P.op("vector", lambda e: e.tensor_tensor(out=sm["a"][:], in0=sm["ipre"][:], in1=sm["b"][:], op=ALU.subtract))
    m2 = P.op("tensor", lambda e: e.transpose(SM[0:64, 192:320], sm["a"][:, 0:64], idf[:, :]), waits=[d3])
    d4 = P.op("vector", lambda e: e.tensor_reduce(out=amT[:], in_=SM[0:64, 192:320], axis=AX.X, op=ALU.max), waits=[m2])
    m3 = P.op("tensor", lambda e: e.transpose(SM[0:1, 320:384], amT[0:64, 0:1], idf[0:64, 0:64]), waits=[d4])
    m4 = P.op("tensor", lambda e: e.matmul(SM[0:1, 384:448], lhsT=idf[:, 127:128], rhs=sm["b"][:], start=True, stop=True), waits=[d2])
    P.op("vector", lambda e: e.tensor_copy(out=rowA[:], in_=SM[0:1, 320:384]), waits=[m3, m4])
    P.op("vector", lambda e: e.tensor_copy(out=rowB[:], in_=SM[0:1, 384:448]))
    for cc in range(NC_):
        P.op("vector", lambda e, cc=cc: e.tensor_tensor(out=rowM[0:1, cc:cc + 1], in0=rowM[0:1, 64 + cc:65 + cc], in1=rowA[0:1, cc:cc + 1], op=ALU.max))
        d5 = P.op("vector", lambda e, cc=cc: e.tensor_tensor(out=rowM[0:1, 65 + cc:66 + cc], in0=rowB[0:1, cc:cc + 1], in1=rowM[0:1, cc:cc + 1], op=ALU.add))
    m5 = P.op("tensor", lambda e: e.matmul(SM[:, 0:128], lhsT=ones_r[0:1, :], rhs=rowM[0:1, 0:128], start=True, stop=True), waits=[d5])
    P.op("vector", lambda e: e.tensor_copy(out=sm["Mb"][:], in_=SM[:, 0:64]), waits=[m5])
    d6 = P.op("vector", lambda e: e.tensor_copy(out=sm["mhb"][:], in_=SM[:, 64:128]))
    P.op("vector", lambda e: e.tensor_tensor(out=sm["w"][:], in0=sm["a"][:], in1=sm["Mb"][:], op=ALU.subtract))
    P.op("vector", lambda e: e.tensor_scalar(out=sm["w"][:], in0=sm["w"][:], scalar1=-LN32, scalar2=None, op0=ALU.add))
    P.op("vector", lambda e: e.tensor_tensor(out=sm["g"][:], in0=sm["mhb"][:], in1=sm["Mb"][:], op=ALU.subtract))
    d7 = P.op("vector", lambda e: e.tensor_tensor(out=sm["fl"][:], in0=sm["b"][:], in1=sm["Mb"][:], op=ALU.add))
    P.op("scalar", lambda e: e.activation(out=sm["w"][:], in_=sm["w"][:], func=AF.Exp), waits=[d7])
    P.op("scalar", lambda e: e.activation(out=sm["g"][:], in_=sm["g"][:], func=AF.Exp))
    gts = P.op("scalar", lambda e: e.activation(out=sm["fl"][:], in_=sm["fl"][:], func=AF.Exp, scale=-1.0))
    czero = P.op("vector", lambda e: e.memset(C[:], 0.0), waits=[gts, m5])
    GR = 4
    qv = qT.rearrange("(d p) t -> p d t", p=128)
    kv = kT.rearrange("(d p) t -> p d t", p=128)
    Av = Ai.rearrange("(d p) t -> p d t", p=128)
    Bv = Bi.rearrange("(d p) t -> p d t", p=128)
    mv_ = mix.rearrange("(d p) t -> p d t", p=128)
    ktv = ktm.rearrange("(c p) d -> p c d", p=128)
    vtv = vtm.rearrange("(c p) d -> p c d", p=128)
    ld = {n: Slots(P, "l_" + n, 2, [128, 8, 512] if n in ("q", "k", "A", "B") else [128, GR, 1024], BF16, dma=True)
          for n in ("q", "k", "A", "B", "kt", "vt")}
    kw = Slots(P, "kw", 2, [128, 1024], BF16)
    STb = Slots(P, "STb", 2, [128, 128], BF16)
    hc = Slots(P, "hc", 2, [128, 1024], F32)
    hn = Slots(P, "hn", 2, [128, 1024], F32)
    otmp = Slots(P, "otmp", 2, [128, 512], F32)
    outst = Slots(P, "outst", 2, [128, 8, 128], BF16, dma=True)
    small = Slots(P, "small", 2, [128, 8], F32)
    stats = Slots(P, "stats", 2, [128, 12], F32)
    free = {"S": None, "NUM": [None, None], "SMd": czero, "DC": [None, None], "TP": [None, None]}
    grp = {}
    users = {}

    def load_group(g):
        sl = slice(g * 512, (g + 1) * 512)
        out = {}
        for n, src in (("q", qv[:, :, sl]), ("k", kv[:, :, sl]), ("A", Av[:, :, sl]), ("B", Bv[:, :, sl]),
                       ("kt", ktv[:, g * GR:(g + 1) * GR, :]), ("vt", vtv[:, g * GR:(g + 1) * GR, :])):
            i_, fr = ld[n].next()
            tk = P.dma("sync", ld[n].t[i_][:], src, ld[n].ch[i_], fr)
            out[n] = (i_, tk)
        grp[g] = out

    load_group(0)
    stt8 = {"upd": czero}

    def chunk(cc):
            g, ci = cc // GR, cc % GR
            if ci == 0 and g + 1 < NC_ // GR:
                load_group(g + 1)
            G_ = grp[g]
            csl = slice(ci * 128, (ci + 1) * 128)
            qg = ld["q"].t[G_["q"][0]]
            kg = ld["k"].t[G_["k"][0]]
            Ag = ld["A"].t[G_["A"][0]]
            Bg = ld["B"].t[G_["B"][0]]
            ktg = ld["kt"].t[G_["kt"][0]]
            vtg = ld["vt"].t[G_["vt"][0]]
            wcol = sm["w"][:, cc:cc + 1]
            gcol = sm["g"][:, cc:cc + 1]
            mm = None
            for dt in range(8):
                mm = P.op("tensor", lambda e, dt=dt: e.matmul(Sps[:, 0:128], lhsT=kg[:, dt, csl], rhs=qg[:, dt, csl], start=(dt == 0), stop=(dt == 7)),
                          waits=[G_["k"][1], G_["q"][1], free["S"]] if dt == 0 else [], sig=(dt == 7))
            si, sfr = STb.next()
            ST = STb.t[si]
            stt = P.op("vector", lambda e: e.scalar_tensor_tensor(out=ST[:], in0=Sps[:, 0:128], scalar=wcol, in1=tri[:], op0=ALU.mult, op1=ALU.mult),
                       waits=[mm, gts] + sfr)
            free["S"] = stt
            ki, kfr = kw.next()
            kwt = P.op("gpsimd", lambda e: e.tensor_scalar(out=kw.t[ki][:], in0=ktg[:, ci, :], scalar1=wcol, scalar2=None, op0=ALU.mult),
                       waits=[G_["kt"][1], gts] + kfr)
            cast = P.op("scalar", lambda e: e.activation(out=Cbf[:], in_=C[:], func=AF.Copy, scale=gcol), waits=[stt8["upd"], users.get("Cbf")])
            lastn = None
            for n in range(2):
                P.op("tensor", lambda e, n=n: e.matmul(NUM[n][:, :], lhsT=ST[:], rhs=vtg[:, ci, n * 512:(n + 1) * 512], start=True, stop=False),
                     waits=[stt, G_["vt"][1], cast, free["NUM"][n]], sig=False)
                for dt in range(8):
                    lastn = P.op("tensor", lambda e, n=n, dt=dt: e.matmul(NUM[n][:, :], lhsT=qg[:, dt, csl], rhs=Cbv[:, dt, n * 512:(n + 1) * 512],
                                                                          start=False, stop=(dt == 7)), sig=(dt == 7))
            P.op("tensor", lambda e: e.matmul(SM[:, 0:1], lhsT=ST[:], rhs=ones_b[:, 0:1], start=True, stop=False), waits=[free["SMd"]], sig=False)
            lastd = None
            for dt in range(8):
                lastd = P.op("tensor", lambda e, dt=dt: e.matmul(SM[:, 0:1], lhsT=qg[:, dt, csl], rhs=Cbv[:, dt, 1024:1025], start=False, stop=(dt == 7)),
                             sig=(dt == 7))
            STb.free[si] = [lastd]
            smi, smf = small.next()
            sv = small.t[smi]
            P.op("vector", lambda e: e.tensor_scalar(out=sv[:, 6:7], in0=SM[:, 0:1], scalar1=-1.0, scalar2=None, op0=ALU.mult), waits=[lastd] + smf)
            P.op("vector", lambda e: e.tensor_tensor(out=sv[:, 0:1], in0=SM[:, 0:1], in1=sv[:, 6:7], op=ALU.max))
            P.op("vector", lambda e: e.tensor_tensor(out=sv[:, 0:1], in0=sv[:, 0:1], in1=sm["fl"][:, cc:cc + 1], op=ALU.max))
            rr = P.op("vector", lambda e: e.reciprocal(out=sv[:, 1:2], in_=sv[:, 0:1]))
            lastnc = None
            upd = None
            for dt in range(8):
                for n in range(2):
                    mmc = P.op("tensor", lambda e, dt=dt, n=n: e.matmul(DC[n][:, :], lhsT=kw.t[ki][:, dt * 128:(dt + 1) * 128],
                                                                         rhs=vtg[:, ci, n * 512:(n + 1) * 512], start=True, stop=True),
                               waits=[kwt, free["DC"][n]])
                    upd = P.op("vector", lambda e, dt=dt, n=n: e.scalar_tensor_tensor(
                        out=Cv[:, dt, n * 512:(n + 1) * 512], in0=Cv[:, dt, n * 512:(n + 1) * 512], scalar=gcol, in1=DC[n][:, :],
                        op0=ALU.mult, op1=ALU.add), waits=[mmc, cast])
                    free["DC"][n] = upd
                lastnc = P.op("tensor", lambda e, dt=dt: e.matmul(SM[:, 8 + dt:9 + dt], lhsT=kw.t[ki][:, dt * 128:(dt + 1) * 128], rhs=ones_b[:, 0:1],
                                                                  start=True, stop=True), waits=[rr] if dt == 0 else [], sig=(dt == 7))
            kw.free[ki] = [lastnc]
            stt8["upd"] = P.op("vector", lambda e: e.scalar_tensor_tensor(out=Cv[:, :, 1024], in0=Cv[:, :, 1024], scalar=gcol, in1=SM[:, 8:16],
                                                                       op0=ALU.mult, op1=ALU.add), waits=[lastnc, cast])
            free["SMd"] = stt8["upd"]
            users["Cbf"] = lastd
            hi, hfr = hc.next()
            hct = None
            for n in range(2):
                hct = P.op("scalar", lambda e, n=n: e.activation(out=hc.t[hi][:, n * 512:(n + 1) * 512], in_=NUM[n][:, :], func=AF.Copy, scale=sv[:, 1:2]),
                           waits=[lastn, rr] + (hfr if n == 0 else []))
                free["NUM"][n] = hct
            sti, stf = stats.next()
            stt_ = stats.t[sti]
            for n in range(2):
                P.op("vector", lambda e, n=n: e.bn_stats(out=stt_[:, n * 6:(n + 1) * 6], in_=hc.t[hi][:, n * 512:(n + 1) * 512]), waits=[hct] + stf)
            ag = P.op("vector", lambda e: e.bn_aggr(out=sv[:, 2:4], in_=stt_[:, 0:12]))
            sq_ = P.op("scalar", lambda e: e.activation(out=sv[:, 4:5], in_=sv[:, 3:4], func=AF.Sqrt, bias=epsb[:, 0:1]), waits=[ag])
            P.op("vector", lambda e: e.reciprocal(out=sv[:, 5:6], in_=sv[:, 4:5]), waits=[sq_])
            ni, nfr = hn.next()
            hnt = P.op("vector", lambda e: e.tensor_scalar(out=hn.t[ni][:], in0=hc.t[hi][:], scalar1=sv[:, 2:3], scalar2=sv[:, 5:6],
                                                          op0=ALU.subtract, op1=ALU.mult), waits=nfr)
            hc.free[hi] = [hnt]
            stats.free[sti] = [hnt]
            small.free[smi] = [hnt]
            oi, ofr = outst.next()
            fin = None
            lasttr = None
            for hb_ in range(2):
                for q4 in range(4):
                    dt = hb_ * 4 + q4
                    lasttr = P.op("tensor", lambda e, dt=dt, q4=q4, hb_=hb_: e.transpose(TP[hb_][:, q4 * 128:(q4 + 1) * 128], hn.t[ni][:, dt * 128:(dt + 1) * 128], idf[:, :]),
                                  waits=[hnt, free["TP"][hb_]] if q4 == 0 else [], sig=(q4 == 3))
                ti_, tfr = otmp.next()
                tpv = TP[hb_][:, :].rearrange("p (d n) -> p d n", d=4)
                o1 = P.op("vector", lambda e, hb_=hb_, ti_=ti_, tpv=tpv: e.tensor_tensor(
                    out=otmp.t[ti_][:].rearrange("p (d n) -> p d n", d=4), in0=tpv, in1=Ag[:, hb_ * 4:(hb_ + 1) * 4, csl], op=ALU.mult),
                    waits=[lasttr, G_["A"][1]] + tfr)
                free["TP"][hb_] = o1
                fin = P.op("vector", lambda e, hb_=hb_, ti_=ti_: e.tensor_tensor(
                    out=outst.t[oi][:, hb_ * 4:(hb_ + 1) * 4, :], in0=otmp.t[ti_][:].rearrange("p (d n) -> p d n", d=4),
                    in1=Bg[:, hb_ * 4:(hb_ + 1) * 4, csl], op=ALU.add), waits=[G_["B"][1]] + (ofr if hb_ == 0 else []))
                otmp.free[ti_] = [fin]
            hn.free[ni] = [lasttr]
            sd = P.dma("sync", mv_[:, :, cc * 128:(cc + 1) * 128], outst.t[oi][:], outst.ch[oi], [fin])
            outst.free[oi] = [sd]
            if ci == GR - 1:
                for n in ("q", "k", "A", "B", "kt", "vt"):
                    ld[n].free[G_[n][0]] = [fin, lastd, lastnc, mm]

    for cc in range(NC_):
        chunk(cc)
    P.run()
    return nc


ATTN_W = 6144
POOL_WINDOWS = (2, 4, 8, 16)


def _t5_bucket(dist):
    max_exact = 16
    safe = np.maximum(dist, 1).astype(np.float32)
    large = max_exact + (np.log(safe / max_exact) / np.log(2048 / max_exact) * (32 - max_exact)).astype(np.int32)
    large = np.minimum(large, 31)
    return np.where(dist < max_exact, dist, large).astype(np.int32)


def _vec4(v512):
    return np.ascontiguousarray(v512.reshape(4, 128).T)


def even_static_inputs(rel_bias, w_in, q_gain, k_gain, pool_w, pool_scale):
    j = np.arange(128)[:, None]
    ip = np.arange(256)[None, :]
    rel = np.where(ip < 128, ip + 128 - j, (ip - 128) - j)
    valid = (rel >= 0) & (rel <= 128)
    mtab = np.where(valid, 0.0, -30000.0).astype(np.float32)
    relc = np.clip(rel, 0, 128)
    buckets = [_t5_bucket(relc * d) for d, _ in PATTERNS]
    identb = np.eye(128, dtype=np.float32).astype(NPBF)
    maps = []
    for c in range(NCORES):
        cols = []
        for hh in range(6):
            h = 6 * c + hh
            for base in (0, ATTN_W, 2 * ATTN_W):
                cols.append(np.arange(base + h * 128, base + (h + 1) * 128))
            cols.append(np.arange(20480 + h * 128, 20480 + (h + 1) * 128))
        g, half = c // 2, c % 2
        cols.append(np.arange(18432 + g * 512, 18432 + (g + 1) * 512))
        cols.append(np.arange(20480 + ATTN_W + g * 512 + half * 256, 20480 + ATTN_W + g * 512 + (half + 1) * 256))
        cols = np.concatenate(cols)
        wc = np.ascontiguousarray(w_in[:, cols])
        bt = np.empty((128, 6, 3, 256), np.float32)
        for hh in range(6):
            for g_ in range(3):
                bt[:, hh, g_, :] = rel_bias[buckets[g_], 6 * c + hh]
        wdw = POOL_WINDOWS[g]
        pvec = np.zeros((128, 8), np.float32)
        for i_, sft in enumerate((1, 2, 4, 8)):
            pvec[:, i_] = 1.0 if sft < wdw else 0.0
        ps = pool_scale[g * 512 + half * 256: g * 512 + (half + 1) * 256]
        pvec[:, 4] = ps[0:128]
        pvec[:, 5] = ps[128:256]
        tt = np.arange(512)
        rctab = np.empty((128, 1024), np.float32)
        rctab[:, 0:512] = (1.0 / np.minimum(tt + 1, wdw))[None, :]
        rctab[:, 512:1024] = 1.0 / wdw
        maps.append({
            "w": wc,
            "gains": np.ascontiguousarray(np.stack([q_gain, k_gain], axis=1).astype(np.float32)),
            "btab": np.ascontiguousarray(bt.reshape(128, 6 * 768)),
            "mtab": mtab, "identb": identb,
            "pw": np.ascontiguousarray(pool_w[g][:, half * 256:(half + 1) * 256]),
            "pvec": pvec, "rctab": rctab,
        })
    return maps


def even_wout_perm():
    rows = []
    for c in range(NCORES):
        rows.append(np.arange(c * 768, (c + 1) * 768))
        g, half = c // 2, c % 2
        rows.append(np.arange(ATTN_W + g * 512 + half * 256, ATTN_W + g * 512 + (half + 1) * 256))
    return np.concatenate(rows)


def odd_static_inputs(w_up, conv_w, conv_b, wq, wk, wv, w_if, b_if, gn, skip):
    maps = []
    MW = 8192
    eye32 = np.zeros((32, 4, 32, 4), np.float32)
    for c in range(NCORES):
        cols = []
        for j in range(8):
            for base in (0, MW, 2 * MW):
                cols.append(np.arange(base + c * 1024 + j * 128, base + c * 1024 + (j + 1) * 128))
        cols = np.concatenate(cols)
        wc = np.ascontiguousarray(w_up[:, cols])
        ch = np.arange(c * 1024, (c + 1) * 1024).reshape(8, 128)
        cvec = np.zeros((128, 8, 8), np.float32)
        for k_ in range(4):
            cvec[:, :, k_] = conv_w[k_][ch].T
        cvec[:, :, 4] = conv_b[ch].T
        cvec[:, :, 5] = gn[ch].T
        cvec[:, :, 6] = skip[ch].T
        bd = np.zeros((128, 3, 8, 128), np.float32)
        for wi, wm in enumerate((wq, wk, wv)):
            for j in range(8):
                b0 = (c * 1024 + j * 128) // 4
                blk = wm[b0:b0 + 32]
                m = np.zeros((32, 4, 32, 4), np.float32)
                idx = np.arange(32)
                m[idx, :, idx, :] = blk
                bd[:, wi, j, :] = m.reshape(128, 128)
        wif = np.zeros((128, 3, 8, 16), np.float32)
        for wi in range(3):
            rows = wi * MW + ch
            wif[:, wi, :, :] = np.transpose(w_if[rows], (1, 0, 2))
        sel = np.zeros((128, 2), np.float32)
        for r in range(8):
            sel[r * 16 + c, 0] = 1.0
            sel[r * 16 + 8 + c, 1] = 1.0
        bsel = np.zeros((128, 2), np.float32)
        bsel[:, 0] = b_if[c]
        bsel[:, 1] = b_if[8 + c]
        maps.append({"w": wc, "cvec": cvec, "bd": bd, "wif": wif, "sel": sel, "bsel": bsel})
    return maps


_CACHE = {}


def _prog(name, builder):
    if name not in _CACHE:
        _CACHE[name] = builder()
    return _CACHE[name]


def _launch(name, builder, in_maps):
    nc = _prog(name, builder)
    res = run_bass_kernel_spmd(nc, in_maps, core_ids=list(range(NCORES)))
    return res.results


def kernel(x, rel_bias, e_norm, e_w_in, e_q_gain, e_k_gain, e_pool_w, e_pool_scale, e_w_out,
           o_norm, o_w_up, o_conv_w, o_conv_b, o_wq, o_wk, o_wv, o_w_if, o_b_if, o_gn, o_skip, o_w_down):
    f32 = np.float32
    x = np.asarray(x, f32)
    xT = np.ascontiguousarray(x[0].T)
    hT = [xT[c * 512:(c + 1) * 512] for c in range(NCORES)]
    norms = [np.asarray(e_norm[0], f32), np.asarray(o_norm[0], f32), np.asarray(e_norm[1], f32), np.asarray(o_norm[1], f32),
             np.ones((D,), f32)]
    res = _launch("prep", build_prep, [{"hT": hT[c], "gain": _vec4(norms[0][c * 512:(c + 1) * 512])} for c in range(NCORES)])
    hg = [r["hg"] for r in res]
    ssq = [r["ssq"][0] for r in res]
    tri = np.triu(np.ones((128, 128), f32))
    identf = np.eye(128, dtype=f32)
    perm = even_wout_perm()
    for layer in range(4):
        j = layer // 2
        hg_full = np.ascontiguousarray(np.concatenate(hg, axis=0))
        ssq_all = np.ascontiguousarray(np.stack(ssq).astype(f32))
        if layer % 2 == 0:
            st = even_static_inputs(np.asarray(rel_bias, f32), np.asarray(e_w_in[j], f32), np.asarray(e_q_gain[j], f32),
                                    np.asarray(e_k_gain[j], f32), np.asarray(e_pool_w[j], f32), np.asarray(e_pool_scale[j], f32))
            res = _launch("even", build_even, [dict(st[c], hgT=hg_full, ssq_all=ssq_all) for c in range(NCORES)])
            mix_full = np.ascontiguousarray(np.concatenate([r["mix"] for r in res], axis=0))
            wout = np.asarray(e_w_out[j], f32)[perm]
        else:
            st = odd_static_inputs(np.asarray(o_w_up[j], f32), np.asarray(o_conv_w[j], f32), np.asarray(o_conv_b[j], f32),
                                   np.asarray(o_wq[j], f32), np.asarray(o_wk[j], f32), np.asarray(o_wv[j], f32),
                                   np.asarray(o_w_if[j], f32), np.asarray(o_b_if[j], f32), np.asarray(o_gn[j], f32),
                                   np.asarray(o_skip[j], f32))
            r1 = _launch("odd1", build_odd1, [dict(w=st[c]["w"], cvec=st[c]["cvec"], bd=st[c]["bd"], wif=st[c]["wif"],
                                                   hgT=hg_full, ssq_all=ssq_all) for c in range(NCORES)])
            gp_all = np.ascontiguousarray(np.concatenate([r["gpart"] for r in r1], axis=0).astype(f32))
            maps = []
            for c in range(NCORES):
                r = r1[c]
                maps.append(dict(qT=r["qT"], kT=r["kT"], ktm=np.ascontiguousarray(r["kT"].T), vtm=np.ascontiguousarray(r["vT"].T),
                                 Ao=r["Ao"], Bo=r["Bo"], gp_all=gp_all, sel=st[c]["sel"], bsel=st[c]["bsel"], tri=tri, identf=identf))
            r2 = _launch("odd2", build_odd2, maps)
            mix_full = np.ascontiguousarray(np.concatenate([r["mix"] for r in r2], axis=0))
            wout = np.asarray(o_w_down[j], f32)
        gnext = norms[layer + 1]
        res = _launch("out", build_out, [{"mixT": mix_full, "w": np.ascontiguousarray(wout[:, c * 512:(c + 1) * 512]), "hT": np.ascontiguousarray(hT[c]),
                                          "gain": _vec4(gnext[c * 512:(c + 1) * 512])} for c in range(NCORES)])
        hT = [r["hout"] for r in res]
        hg = [r["hg"] for r in res]
        ssq = [r["ssq"][0] for r in res]
    out = np.ascontiguousarray(np.concatenate(hT, axis=0).T.astype(f32))[None]
    return out
```

```python
import numpy as np
from contextlib import ExitStack
import ml_dtypes
import concourse.bass as bass
import concourse.mybir as mybir
from concourse.bass_utils import run_bass_kernel_spmd

F32 = mybir.dt.float32
BF16 = mybir.dt.bfloat16
AF = mybir.ActivationFunctionType
ALU = mybir.AluOpType
AX = mybir.AxisListType
NPBF = ml_dtypes.bfloat16

NCORES = 8
S = 8192
D = 4096
EPS = 1e-6
TN = 512
NT = S // TN

_ENGS = ("sync", "scalar", "vector", "gpsimd", "tensor")
SERIALIZE = ("scalar", "vector", "gpsimd")


class Chan:
    def __init__(self, sem, dma):
        self.sem = sem
        self.n = 0
        self.dma = dma


class Prog:
    def __init__(self, nc):
        self.nc = nc
        self.es = ExitStack()
        self.ops = {e: [] for e in _ENGS}
        self.k = 0
        self.ech = {e: self._chan(e, False) for e in _ENGS if e != "sync"}
        self.dchans = []

    def _chan(self, name, dma):
        self.k += 1
        return Chan(self.es.enter_context(self.nc.semaphore(f"s_{name}_{self.k}")), dma)

    def dchan(self, name="d"):
        c = self._chan(name, True)
        self.dchans.append(c)
        return c

    def sb(self, name, shape, dt):
        return self.es.enter_context(self.nc.sbuf_tensor(name, list(shape), dt))

    def ps(self, name, shape, dt):
        return self.es.enter_context(self.nc.psum_tensor(name, list(shape), dt))

    def op(self, eng, fn, waits=(), sig=True):
        if eng in SERIALIZE:
            sig = True
        ch = self.ech[eng] if sig else None
        tk = None
        if ch is not None:
            ch.n += 1
            tk = (ch, ch.n)
        self.ops[eng].append((fn, tuple(w for w in waits if w is not None), ch, 1))
        return tk

    def dma(self, eng, out, in_, chan, waits=()):
        chan.n += 16
        self.ops[eng].append((lambda e, o=out, i=in_: e.dma_start(out=o, in_=i),
                              tuple(w for w in waits if w is not None), chan, 16))
        return (chan, chan.n)

    def simulate(self):
        val = {}
        pos = {e: 0 for e in _ENGS}
        total = sum(len(v) for v in self.ops.values())
        done = 0
        while done < total:
            progressed = False
            for e in _ENGS:
                while pos[e] < len(self.ops[e]):
                    fn, waits, ch, inc = self.ops[e][pos[e]]
                    if any(val.get(id(c), 0) < v for c, v in waits):
                        break
                    if ch is not None:
                        val[id(ch)] = val.get(id(ch), 0) + inc
                    pos[e] += 1
                    done += 1
                    progressed = True
            if not progressed:
                msg = []
                for e in _ENGS:
                    if pos[e] < len(self.ops[e]):
                        fn, waits, ch, inc = self.ops[e][pos[e]]
                        msg.append((e, pos[e], len(self.ops[e]), [(c.sem, v, val.get(id(c), 0)) for c, v in waits if val.get(id(c), 0) < v]))
                raise RuntimeError("DEADLOCK in recorded program: %r" % (msg,))
        return True

    def run(self):
        nc = self.nc
        self.simulate()
        with nc.Block() as block:
            for name in _ENGS:
                def body(eng, name=name):
                    waited = {}
                    nsig = 0
                    for fn, waits, ch, inc in self.ops[name]:
                        if name in SERIALIZE and inc == 1 and nsig > 0:
                            waits = tuple(waits) + ((self.ech[name], nsig),)
                        if name in SERIALIZE and inc == 1:
                            nsig += 1
                        for c, v in waits:
                            if waited.get(id(c), 0) >= v:
                                continue
                            eng.wait_ge(c.sem, v)
                            waited[id(c)] = v
                        ins = fn(eng)
                        if ch is not None:
                            ins.then_inc(ch.sem, inc)
                    if name == "sync":
                        for c in self.dchans:
                            if c.n > 0:
                                eng.wait_ge(c.sem, c.n)
                getattr(block, name)(body)
        self.es.close()


class Slots:
    def __init__(self, P, name, n, shape, dt, dma=False, views=None):
        self.t = list(views) if views is not None else [P.sb(f"{name}{i}", shape, dt) for i in range(n)]
        self.ch = [P.dchan(f"{name}{i}") for i in range(n)] if dma else None
        self.free = [[] for _ in range(n)]
        self.i = -1
        self.n = n

    def next(self):
        self.i = (self.i + 1) % self.n
        fr = self.free[self.i]
        self.free[self.i] = []
        return self.i, fr


def _new_nc():
    return bass.Bass("TRN2", target_bir_lowering=False)


def _din(nc, name, shape, dt):
    return nc.dram_tensor(name, list(shape), dt, kind="ExternalInput").ap()


def _dout(nc, name, shape, dt):
    return nc.dram_tensor(name, list(shape), dt, kind="ExternalOutput").ap()


def build_prep():
    nc = _new_nc()
    hT = _din(nc, "hT", [512, S], F32)
    gain = _din(nc, "gain", [128, 4], F32)
    hg = _dout(nc, "hg", [512, S], BF16)
    ssq = _dout(nc, "ssq", [1, S], F32)
    P = Prog(nc)
    ones = P.sb("ones", [128, 1], F32)
    g = P.sb("g", [128, 4], F32)
    cst = P.dchan("cst")
    hb = Slots(P, "hb", 3, [128, 2048], F32, dma=True)
    sq = Slots(P, "sq", 2, [128, 2048], F32)
    ob = Slots(P, "ob", 3, [128, 2048], BF16, dma=True)
    st = Slots(P, "st", 2, [1, 2048], F32, dma=True)
    acc = [P.ps(f"acc{i}", [128, 512], F32) for i in range(4)]
    ldc = P.dma("sync", g[:], gain, cst)
    t_ones = P.op("vector", lambda e: e.memset(ones[:], 1.0))
    accfree = [None] * 4
    for tb in range(S // 2048):
        mm = None
        for r in range(4):
            hi, fr = hb.next()
            ld = P.dma("sync", hb.t[hi][:], hT[r * 128:(r + 1) * 128, tb * 2048:(tb + 1) * 2048], hb.ch[hi], fr)
            si, fr2 = sq.next()
            tk = P.op("scalar", lambda e, a=sq.t[si], b=hb.t[hi]: e.activation(out=a[:], in_=b[:], func=AF.Square),
                      waits=[ld] + fr2)
            oi, fr4 = ob.next()
            tg = P.op("vector", lambda e, o=ob.t[oi], a=hb.t[hi], r=r: e.tensor_scalar(
                out=o[:], in0=a[:], scalar1=g[:, r:r + 1], scalar2=None, op0=ALU.mult), waits=[ld, ldc] + fr4)
            hb.free[hi] = [tk, tg]
            sd0 = P.dma("gpsimd", hg[r * 128:(r + 1) * 128, tb * 2048:(tb + 1) * 2048], ob.t[oi][:], ob.ch[oi], [tg])
            ob.free[oi] = [sd0]
            for q in range(4):
                mm = P.op("tensor", lambda e, o=acc[q], a=sq.t[si], q=q, r=r: e.matmul(
                    o[0:1, :], lhsT=ones[:, 0:1], rhs=a[:, q * 512:(q + 1) * 512], start=(r == 0), stop=(r == 3)),
                    waits=[tk, t_ones, accfree[q] if r == 0 else None])
            sq.free[si] = [mm]
        oi, fr3 = st.next()
        cp = None
        for q in range(4):
            cp = P.op("vector", lambda e, o=st.t[oi], a=acc[q], q=q: e.tensor_copy(out=o[0:1, q * 512:(q + 1) * 512], in_=a[0:1, :]),
                      waits=[mm] + fr3)
            accfree[q] = cp
        sd = P.dma("sync", ssq[0:1, tb * 2048:(tb + 1) * 2048], st.t[oi][:], st.ch[oi], [cp])
        st.free[oi] = [sd]
    P.run()
    return nc


def build_out():
    nc = _new_nc()
    KC = 64
    mixT = _din(nc, "mixT", [NT, 128, 64, TN], BF16)
    w = _din(nc, "w", [8192, 512], F32)
    hT = _din(nc, "hT", [512, S], F32)
    gain = _din(nc, "gain", [128, 4], F32)
    hout = _dout(nc, "hout", [512, S], F32)
    hg = _dout(nc, "hg", [512, S], BF16)
    ssq = _dout(nc, "ssq", [1, S], F32)
    P = Prog(nc)
    gsb = P.sb("gsb", [128, 4], F32)
    gb = Slots(P, "gb", 4, [128, TN], BF16, dma=True)
    wt = P.sb("wt", [128, KC, 512], BF16)
    ones = P.sb("ones", [128, 1], F32)
    G = 16
    xr = Slots(P, "xr", 4, [128, G, TN], BF16, dma=True)
    hb = Slots(P, "hb", 4, [128, TN], F32, dma=True)
    ob = Slots(P, "ob", 4, [128, TN], F32, dma=True)
    sq = Slots(P, "sq", 4, [128, TN], F32)
    st = Slots(P, "st", 2, [1, TN], F32, dma=True)
    acc = [P.ps(f"acc{i}", [128, 512], F32) for i in range(6)]
    sps = P.ps("sps", [128, 512], F32)
    accfree = [None] * 6
    wld = P.dchan("wld")
    cst = P.dchan("cst")
    ldg = P.dma("sync", gsb[:], gain, cst)
    wv = w.rearrange("(kc p) n -> p kc n", p=128)
    wtk = None
    for i in range(8):
        wtk = P.dma("gpsimd", wt[:, 8 * i:8 * i + 8, :], wv[:, 8 * i:8 * i + 8, :], wld)
    t_ones = P.op("vector", lambda e: e.memset(ones[:], 1.0))
    jobs = [(t, g) for t in range(NT) for g in range(KC // G)]
    loads = {}
    nl = 0

    def issue_loads(upto):
        nonlocal nl
        while nl < min(upto, len(jobs)):
            t, g = jobs[nl]
            xi, fr = xr.next()
            tk = P.dma("sync", xr.t[xi][:], mixT[t, :, g * G:(g + 1) * G, :], xr.ch[xi], fr)
            loads[(t, g)] = (xi, tk)
            nl += 1

    deferred = []
    spsfree = None
    for t in range(NT):
        banks = [(4 * t + ct) % 6 for ct in range(4)]
        lastmm = [None] * 4
        hl = []
        for ct in range(4):
            hi, fr = hb.next()
            hl.append((hi, P.dma("sync", hb.t[hi][:], hT[ct * 128:(ct + 1) * 128, t * TN:(t + 1) * TN], hb.ch[hi], fr)))
        for g in range(KC // G):
            issue_loads(t * (KC // G) + g + 3)
            xi, ltk = loads[(t, g)]
            mm = None
            for ct in range(4):
                for kc in range(G):
                    first = (g == 0 and kc == 0)
                    lastk = (g == KC // G - 1 and kc == G - 1)
                    mm = P.op("tensor", lambda e, o=acc[banks[ct]], k=g * G + kc, ct=ct, x=xr.t[xi], kc=kc, f=first, l=lastk:
                              e.matmul(o[:, :], lhsT=wt[:, k, ct * 128:(ct + 1) * 128], rhs=x[:, kc, :], start=f, stop=l),
                              waits=[ltk, wtk, accfree[banks[ct]] if first else None], sig=(kc == G - 1))
                if g == KC // G - 1:
                    lastmm[ct] = mm
            xr.free[xi] = [mm]
            if g == 1 and deferred:
                for fn in deferred:
                    fn()
                deferred = []
        sqs = []
        for ct in range(4):
            hi, hld = hl[ct]
            oi, fr = ob.next()
            add = P.op("vector", lambda e, o=ob.t[oi], a=acc[banks[ct]], h=hb.t[hi]: e.tensor_tensor(
                out=o[:], in0=a[:, :], in1=h[:], op=ALU.add), waits=[lastmm[ct], hld] + fr)
            accfree[banks[ct]] = add
            hb.free[hi] = [add]
            sd = P.dma("gpsimd", hout[ct * 128:(ct + 1) * 128, t * TN:(t + 1) * TN], ob.t[oi][:], ob.ch[oi], [add])
            si, fr2 = sq.next()
            s2 = P.op("scalar", lambda e, o=sq.t[si], a=ob.t[oi]: e.activation(out=o[:], in_=a[:], func=AF.Square),
                      waits=[add] + fr2)
            gi, fr5 = gb.next()
            tg = P.op("vector", lambda e, o=gb.t[gi], a=ob.t[oi], ct=ct: e.tensor_scalar(
                out=o[:], in0=a[:], scalar1=gsb[:, ct:ct + 1], scalar2=None, op0=ALU.mult), waits=[ldg] + fr5)
            sdg = P.dma("gpsimd", hg[ct * 128:(ct + 1) * 128, t * TN:(t + 1) * TN], gb.t[gi][:], gb.ch[gi], [tg])
            gb.free[gi] = [sdg]
            ob.free[oi] = [sd, s2, tg]
            sqs.append((si, s2))

        def pe_part(t=t, sqs=sqs):
            nonlocal spsfree
            mm = None
            for ct, (si, s2) in enumerate(sqs):
                mm = P.op("tensor", lambda e, a=sq.t[si], ct=ct: e.matmul(sps[0:1, :], lhsT=ones[:, 0:1], rhs=a[:],
                                                                          start=(ct == 0), stop=(ct == 3)),
                          waits=[s2, t_ones, spsfree if ct == 0 else None])
                sq.free[si] = [mm]
            oi, fr = st.next()
            cp = P.op("vector", lambda e, o=st.t[oi]: e.tensor_copy(out=o[0:1, :], in_=sps[0:1, :]), waits=[mm] + fr)
            spsfree = cp
            sd = P.dma("gpsimd", ssq[0:1, t * TN:(t + 1) * TN], st.t[oi][:], st.ch[oi], [cp])
            st.free[oi] = [sd]
        deferred.append(pe_part)
    for fn in deferred:
        fn()
    P.run()
    return nc


def emit_rstd_scratch(P, nc, ssq_all, acc_bank):
    rs = nc.dram_tensor("rs_scratch", [128, S], F32).ap()
    ones8 = P.sb("r_ones8", [8, 128], F32)
    s8 = Slots(P, "r_s8", 2, [8, TN], F32, dma=True)
    ro = Slots(P, "r_ro", 2, [128, TN], F32, dma=True)
    t_ones = P.op("vector", lambda e: e.memset(ones8[:], 1.0))
    bfree = None
    sds = []
    for t in range(NT):
        si, fr = s8.next()
        ld = P.dma("sync", s8.t[si][:], ssq_all[:, t * TN:(t + 1) * TN], s8.ch[si], fr)
        mm = P.op("tensor", lambda e, a=s8.t[si]: e.matmul(acc_bank[:, :], lhsT=ones8[0:8, :], rhs=a[0:8, :], start=True, stop=True),
                  waits=[ld, t_ones, bfree])
        s8.free[si] = [mm]
        oi, fr2 = ro.next()
        a = P.op("vector", lambda e, o=ro.t[oi]: e.tensor_scalar(out=o[:], in0=acc_bank[:, :], scalar1=1.0 / D, scalar2=EPS,
                                                                op0=ALU.mult, op1=ALU.add), waits=[mm] + fr2)
        bfree = a
        s_ = P.op("scalar", lambda e, o=ro.t[oi]: e.activation(out=o[:], in_=o[:], func=AF.Sqrt), waits=[a])
        r_ = P.op("vector", lambda e, o=ro.t[oi]: e.reciprocal(out=o[:], in_=o[:]), waits=[s_])
        sd = P.dma("sync", rs[:, t * TN:(t + 1) * TN], ro.t[oi][:], ro.ch[oi], [r_])
        ro.free[oi] = [sd]
        sds.append(sd)
    return rs, sds, bfree


class InProj:
    def __init__(self, P, xT, w, groups, wcols, nbanks_acc, acc):
        self.P = P
        self.KC = 32
        self.G = 8
        self.xv = xT
        self.wv = w.rearrange("(kc p) n -> p kc n", p=128)
        self.wt = P.sb("ip_wt", [128, self.KC, wcols], BF16)
        self.wld = P.dchan("ip_wld")
        self.xr = Slots(P, "ip_xr", 4, [128, self.G, TN], BF16, dma=True)
        self.groups = groups
        self.jobs = [(gi, t, g) for gi in range(len(groups)) for t in range(NT) for g in range(self.KC // self.G)]
        self.loads = {}
        self.nl = 0
        self.acc = acc
        self.accfree = [None] * len(acc)
        self.bk = -1
        self.wfree = None
        self.lastmm_all = None

    def issue_loads(self, upto):
        P = self.P
        while self.nl < min(upto, len(self.jobs)):
            gi, t, g = self.jobs[self.nl]
            xi, fr = self.xr.next()
            tk = P.dma("sync", self.xr.t[xi][:], self.xv[t, :, g * self.G:(g + 1) * self.G, :], self.xr.ch[xi], fr)
            self.loads[(gi, t, g)] = (xi, tk)
            self.nl += 1

    def run_group(self, gi, epilogue, mid_hook=None):
        P = self.P
        col0, ncols = self.groups[gi]
        nct = ncols // 128
        wtk = None
        for i in range(4):
            wtk = P.dma("gpsimd", self.wt[:, 8 * i:8 * i + 8, 0:ncols], self.wv[:, 8 * i:8 * i + 8, col0:col0 + ncols],
                        self.wld, [self.wfree])
        NG = self.KC // self.G
        base = gi * NT * NG
        for t in range(NT):
            banks = []
            for ct in range(nct):
                self.bk = (self.bk + 1) % len(self.acc)
                banks.append(self.bk)
            lastmm = [None] * nct
            for g in range(NG):
                self.issue_loads(base + t * NG + g + 3)
                xi, ltk = self.loads[(gi, t, g)]
                mm = None
                for ct in range(nct):
                    for kc in range(self.G):
                        first = (g == 0 and kc == 0)
                        lastk = (g == NG - 1 and kc == self.G - 1)
                        mm = P.op("tensor", lambda e, o=self.acc[banks[ct]], k=g * self.G + kc, ct=ct, x=self.xr.t[xi], kc=kc, f=first, l=lastk:
                                  e.matmul(o[:, :], lhsT=self.wt[:, k, ct * 128:(ct + 1) * 128], rhs=x[:, kc, :], start=f, stop=l),
                                  waits=[ltk, wtk, self.accfree[banks[ct]] if first else None], sig=(kc == self.G - 1))
                    if g == NG - 1:
                        lastmm[ct] = mm
                self.xr.free[xi] = [mm]
                if g == 1 and mid_hook is not None:
                    mid_hook()
            self.wfree = mm
            epilogue(t, banks, lastmm)


PATTERNS = ((1, 16), (4, 4), (16, 1))


DEBUG_STOP = 0
DEBUG_G = (0, 1, 2)
DEBUG_NSB = 4
DEBUG_NOFIN = False
DEBUG_SB0 = 0
DEBUG_MIDFLUSH = False
DEBUG_PM_ENG = 'gpsimd'


def build_even():
    nc = _new_nc()
    hgT = _din(nc, "hgT", [NT, 128, 32, TN], BF16)
    ssq_all = _din(nc, "ssq_all", [8, S], F32)
    w = _din(nc, "w", [D, 3840], F32)
    gains = _din(nc, "gains", [128, 2], F32)
    btab = _din(nc, "btab", [128, 6 * 768], F32)
    mtab = _din(nc, "mtab", [128, 256], F32)
    identb = _din(nc, "identb", [128, 128], BF16)
    pw = _din(nc, "pw", [512, 256], F32)
    pvec = _din(nc, "pvec", [128, 8], F32)
    rctab = _din(nc, "rctab", [128, 1024], F32)
    mix = _dout(nc, "mix", [1024, S], BF16)
    P = Prog(nc)

    A = [P.ps(f"A{i}", [128, 512], F32) for i in range(4)]
    N = [P.ps(f"N{i}", [128, 512], F32) for i in range(2)]
    Tb = [P.ps(f"Tps{i}", [128, 1024], BF16) for i in range(2)]

    rs, rs_sds, nfree0 = emit_rstd_scratch(P, nc, ssq_all, N[0])

    cst = P.dchan("cst")
    gsb = P.sb("gsb", [128, 2], F32)
    msb = P.sb("msb", [128, 256], F32)
    idb = P.sb("idb", [128, 128], BF16)
    pvs = P.sb("pvs", [128, 8], F32)
    rcs = P.sb("rcs", [128, 1024], F32)
    pwb = P.sb("pwb", [128, 4, 256], BF16)
    P.dma("sync", gsb[:], gains, cst)
    P.dma("sync", msb[:], mtab, cst)
    P.dma("sync", idb[:], identb, cst)
    P.dma("sync", pvs[:], pvec, cst)
    ldc0 = P.dma("sync", rcs[:], rctab, cst)
    cst2 = P.dchan("cst2")
    ldc = P.dma("gpsimd", pwb[:], pw.rearrange("(kc p) n -> p kc n", p=128), cst2)
    onesq = P.sb("onesq", [128, 128], BF16)
    onesk = P.sb("onesk", [128, 128], BF16)
    epsb = P.sb("epsb", [128, 2], F32)
    P.op("vector", lambda e: e.memset(onesq[:], 1.0))
    P.op("vector", lambda e: e.memset(epsb[:, 0:1], 128.0 * EPS))
    P.op("vector", lambda e: e.memset(epsb[:, 1:2], EPS))
    t_ones = P.op("vector", lambda e: e.memset(onesk[:], 1.0 / 128.0), waits=[ldc0, ldc])

    qT = P.sb("qT", [128, S], BF16)
    kT = P.sb("kT", [128, S], BF16)
    vT = P.sb("vT", [128, S], BF16)
    sz = P.sb("sz", [128, S], BF16)
    accb = P.sb("accb", [128, 2, 2048], F32)
    Eh = [P.sb(f"Eh{i}", [128, 768], F32) for i in range(2)]
    ech = [P.dchan(f"ech{i}") for i in range(2)]
    rsr = Slots(P, "rsr", 2, [128, TN], F32, dma=True)
    f32t = Slots(P, "f32t", 4, [128, TN], F32)
    sqb = Slots(P, "sqb", 4, [128, TN], BF16)
    rqb = Slots(P, "rqb", 3, [128, TN], F32)
    esb = Slots(P, "esb", 3, [128, 256], F32)
    pTb = Slots(P, "pTb", 3, [128, 256], BF16)
    vbb = Slots(P, "vbb", 6, [128, 128], BF16)
    ost = Slots(P, "ost", 2, [128, 2048], BF16, dma=True)

    if DEBUG_STOP == 1:
        P.run()
        return nc
    groups = [(hh * 512, 512) for hh in range(6)] + [(3584, 256), (3072, 512)]
    ip = InProj(P, hgT, w, groups, 512, 5, A)
    nfree = [nfree0, None]
    rs_ready = rs_sds

    deferred = []
    state = {"last_dve": None, "last_act": None}

    def flush_deferred():
        nonlocal deferred
        d, deferred = deferred, []
        for fn in d:
            fn()

    def head_epilogue(t, banks, lastmm):
        hdw = list(state.get("hd", [])) if t == 0 else []
        ri, fr = rsr.next()
        rtk = P.dma("sync", rsr.t[ri][:], rs[:, t * TN:(t + 1) * TN], rsr.ch[ri], list(fr) + (rs_ready if t == 0 else []))
        rst = rsr.t[ri]
        sl = slice(t * TN, (t + 1) * TN)
        users = []
        for which in range(2):
            b = banks[which]
            fi, ffr = f32t.next()
            xs = P.op("vector", lambda e, o=f32t.t[fi], a=A[b]: e.tensor_tensor(out=o[:], in0=a[:, :], in1=rst[:], op=ALU.mult),
                      waits=[lastmm[which], rtk] + ffr + hdw)
            ip.accfree[b] = xs
            users.append(xs)
            si, sfr = sqb.next()
            sq_ = P.op("scalar", lambda e, o=sqb.t[si], a=f32t.t[fi]: e.activation(out=o[:], in_=a[:], func=AF.Square),
                       waits=[xs] + sfr)

            def post(which=which, fi=fi, si=si, sq_=sq_, sl=sl):
                nb = which
                mm = P.op("tensor", lambda e: e.matmul(N[nb][:, :], lhsT=(onesq if which == 0 else onesk)[:, :], rhs=sqb.t[si][:],
                                                       start=True, stop=True), waits=[sq_, t_ones, nfree[nb]])
                sqb.free[si] = [mm]
                qi, qfr = rqb.next()
                rq0 = P.op("scalar", lambda e, o=rqb.t[qi]: e.activation(
                    out=o[:], in_=N[nb][:, :], func=AF.Sqrt, bias=epsb[:, which:which + 1]), waits=[mm, t_ones] + qfr)
                nfree[nb] = rq0
                rq = P.op("vector", lambda e, o=rqb.t[qi]: e.reciprocal(out=o[:], in_=o[:]), waits=[rq0])
                dst = qT if which == 0 else kT
                qn = P.op("vector", lambda e, a=f32t.t[fi], r=rqb.t[qi]: e.scalar_tensor_tensor(
                    out=dst[:, sl], in0=a[:], scalar=gsb[:, which:which + 1], in1=r[:], op0=ALU.mult, op1=ALU.mult))
                f32t.free[fi] = [qn]
                rqb.free[qi] = [qn]
                state["last_dve"] = qn
            deferred.append(post)
        b = banks[2]
        vv = P.op("vector", lambda e, a=A[b]: e.tensor_tensor(out=vT[:, sl], in0=a[:, :], in1=rst[:], op=ALU.mult),
                  waits=[lastmm[2], rtk])
        ip.accfree[b] = vv
        b = banks[3]
        fi, ffr = f32t.next()
        zs = P.op("vector", lambda e, o=f32t.t[fi], a=A[b]: e.tensor_tensor(out=o[:], in0=a[:, :], in1=rst[:], op=ALU.mult),
                  waits=[lastmm[3], rtk] + ffr)
        ip.accfree[b] = zs
        zz = P.op("scalar", lambda e, a=f32t.t[fi]: e.activation(out=sz[:, sl], in_=a[:], func=AF.Silu), waits=[zs] + hdw)
        f32t.free[fi] = [zz]
        rsr.free[ri] = [zs]
        state["last_dve"] = zs
        state["last_act"] = zz

    def blk(ap_t, d, r, cb):
        return ap_t[:].rearrange("p (n d) -> p d n", d=d)[:, r, cb * 128:(cb + 1) * 128]

    tps_free = [None] * 2
    tps_i = [0]

    def attention(hh, ready):
        eb = Eh[hh % 2]
        ld = P.dma("sync", eb[:], btab[:, hh * 768:(hh + 1) * 768], ech[hh % 2], state.get("efree%d" % (hh % 2), []))
        last = None
        for g in range(3):
            last = P.op("vector", lambda e, g=g: e.tensor_tensor(out=eb[:, g * 256:(g + 1) * 256], in0=eb[:, g * 256:(g + 1) * 256],
                                                               in1=msb[:], op=ALU.add), waits=[ld, ldc, ldc0])
        e_ready = P.op("scalar", lambda e: e.activation(out=eb[:], in_=eb[:], func=AF.Exp), waits=[last])
        Sps = [A[0], A[1]]
        Ops = [A[2], A[3]]
        sfree = [ip.accfree[0], ip.accfree[1]]
        ofree = [ip.accfree[2], ip.accfree[3]]
        vcache = {}
        ucount = [0]
        pend = [None]
        last_pool = [None]

        def get_v(g, d, r, cb):
            key = (g, r, cb)
            if key in vcache:
                return vcache[key]
            ti = tps_i[0] % 2
            tps_i[0] += 1
            tr = P.op("tensor", lambda e: e.transpose(Tb[ti][:, 0:128], blk(vT, d, r, cb), idb[:]),
                      waits=list(ready) + [tps_free[ti], ldc, ldc0])
            vi, vfr = vbb.next()
            for k_ in [k_ for k_, v_ in vcache.items() if v_[0] == vi]:
                del vcache[k_]
            ev = P.op("scalar", lambda e, o=vbb.t[vi]: e.activation(out=o[:], in_=Tb[ti][:, 0:128], func=AF.Copy),
                      waits=[tr] + vfr)
            tps_free[ti] = ev
            vcache[key] = (vi, ev)
            return vcache[key]

        def stage34(u):
            (g, d, r, cl, cb, has_prev, pi, ptk, vprev, vcur, SBi) = u
            o = ucount[0] % 2
            ucount[0] += 1
            lo = 0 if has_prev else 128
            waits = [ptk, ofree[o], t_ones]
            mm = None
            if has_prev:
                mm = P.op("tensor", lambda e: e.matmul(Ops[o][:, 0:128], lhsT=vbb.t[vprev[0]][:], rhs=pTb.t[pi][:, 0:128],
                                                       start=True, stop=False), waits=waits + [vprev[1]], sig=False)
            mm = P.op("tensor", lambda e: e.matmul(Ops[o][:, 0:128], lhsT=vbb.t[vcur[0]][:], rhs=pTb.t[pi][:, 128:256],
                                                   start=(not has_prev), stop=True), waits=waits + [vcur[1]], sig=False)
            if has_prev:
                mm = P.op("tensor", lambda e: e.matmul(Ops[o][:, 128:256], lhsT=onesq[:, :], rhs=pTb.t[pi][:, 0:128],
                                                       start=True, stop=False), sig=False)
            mm = P.op("tensor", lambda e: e.matmul(Ops[o][:, 128:256], lhsT=onesq[:, :], rhs=pTb.t[pi][:, 128:256],
                                                   start=(not has_prev), stop=True))
            pTb.free[pi] = [mm]
            if has_prev:
                vbb.free[vprev[0]].append(mm)
            vbb.free[vcur[0]].append(mm)
            dst = accb[:].rearrange("p a (n d) -> p a d n", d=d)[:, :, r, cl * 128:(cl + 1) * 128]
            src = Ops[o][:, 0:256].rearrange("p (a n) -> p a n", a=2)
            if g == 0:
                ac = P.op("vector", lambda e: e.tensor_copy(out=dst, in_=src), waits=[mm])
            else:
                ac = P.op("vector", lambda e: e.tensor_tensor(out=dst, in0=dst, in1=src, op=ALU.add), waits=[mm])
            ofree[o] = ac
            state["last_dve"] = ac

        def finalize(SBi):
            tsl = slice(SBi * 2048, (SBi + 1) * 2048)
            P.op("vector", lambda e: e.reciprocal(out=accb[:, 1, :], in_=accb[:, 1, :]), sig=False)
            P.op("vector", lambda e: e.tensor_tensor(out=accb[:, 0, :], in0=accb[:, 0, :], in1=accb[:, 1, :], op=ALU.mult), sig=False)
            oi, ofr = ost.next()
            fin = P.op("vector", lambda e, o=ost.t[oi], tsl=tsl: e.tensor_tensor(out=o[:], in0=accb[:, 0, :], in1=sz[:, tsl], op=ALU.mult),
                       waits=ofr)
            sd = P.dma("sync", mix[hh * 128:(hh + 1) * 128, tsl], ost.t[oi][:], ost.ch[oi], [fin])
            ost.free[oi] = [sd]
            state["last_dve"] = fin

        for SBi in range(DEBUG_SB0, DEBUG_NSB):
            for g, (d, nb) in enumerate(PATTERNS):
                if g not in DEBUG_G:
                    continue
                for r in range(d):
                    for cl in range(nb):
                        cb = SBi * nb + cl
                        has_prev = cb >= 1
                        s_ = (ucount[0] + (1 if pend[0] is not None else 0)) % 2
                        lo = 0 if has_prev else 128
                        mm = None
                        if has_prev:
                            mm = P.op("tensor", lambda e, d=d, r=r, cb=cb, s_=s_: e.matmul(
                                Sps[s_][:, 0:128], lhsT=blk(kT, d, r, cb - 1), rhs=blk(qT, d, r, cb), start=True, stop=True),
                                waits=list(ready) + [sfree[s_]], sig=False)
                        mm = P.op("tensor", lambda e, d=d, r=r, cb=cb, s_=s_: e.matmul(
                            Sps[s_][:, 128:256], lhsT=blk(kT, d, r, cb), rhs=blk(qT, d, r, cb), start=True, stop=True),
                            waits=list(ready) + [sfree[s_]])
                        vprev = get_v(g, d, r, cb - 1) if has_prev else None
                        vcur = get_v(g, d, r, cb)
                        ei, efr = esb.next()
                        ex = P.op("scalar", lambda e, s_=s_, ei=ei, lo=lo: e.activation(out=esb.t[ei][:, lo:256], in_=Sps[s_][:, lo:256], func=AF.Exp),
                                  waits=[mm] + efr)
                        sfree[s_] = ex
                        pi, pfr = pTb.next()
                        pm = P.op(DEBUG_PM_ENG, lambda e, ei=ei, pi=pi, lo=lo, g=g: e.tensor_tensor(
                            out=pTb.t[pi][:, lo:256], in0=esb.t[ei][:, lo:256], in1=eb[:, g * 256 + lo:(g + 1) * 256], op=ALU.mult),
                            waits=[ex, e_ready] + pfr)
                        esb.free[ei] = [pm]
                        last_pool[0] = pm
                        u = (g, d, r, cl, cb, has_prev, pi, pm, vprev, vcur, SBi)
                        if pend[0] is not None:
                            stage34(pend[0])
                            if pend[0][-1] != SBi:
                                finalize(pend[0][-1])
                        pend[0] = u
            if DEBUG_MIDFLUSH:
                stage34(pend[0])
                finalize(pend[0][-1])
                pend[0] = None
        if pend[0] is not None:
            stage34(pend[0])
            finalize(pend[0][-1])
        pend[0] = None
        state["efree%d" % (hh % 2)] = [last_pool[0]]
        ip.accfree[0] = sfree[0]
        ip.accfree[1] = sfree[1]
        ip.accfree[2] = ofree[0]
        ip.accfree[3] = ofree[1]
        return [state["last_dve"], mm]

    head_done = []
    for hh in range(6):
        state["hd"] = head_done
        ip.run_group(hh, head_epilogue, mid_hook=flush_deferred)
        flush_deferred()
        if DEBUG_STOP == 2 + 10 * hh:
            P.run()
            return nc
        ready = [state["last_dve"], state["last_act"]]
        head_done = attention(hh, ready)
        if DEBUG_STOP == 3 + 10 * hh:
            P.run()
            return nc

    szp = [qT, kT]

    def zp_epilogue(t, banks, lastmm):
        ri, fr = rsr.next()
        rtk = P.dma("sync", rsr.t[ri][:], rs[:, t * TN:(t + 1) * TN], rsr.ch[ri], fr)
        rst = rsr.t[ri]
        sl = slice(t * TN, (t + 1) * TN)
        zs = None
        for oc in range(2):
            b = banks[oc]
            fi, ffr = f32t.next()
            zs = P.op("vector", lambda e, o=f32t.t[fi], a=A[b]: e.tensor_tensor(out=o[:], in0=a[:, :], in1=rst[:], op=ALU.mult),
                      waits=[lastmm[oc], rtk] + ffr + (head_done if t == 0 else []))
            ip.accfree[b] = zs
            zz = P.op("scalar", lambda e, a=f32t.t[fi], oc=oc: e.activation(out=szp[oc][:, sl], in_=a[:], func=AF.Silu),
                      waits=[zs] + (head_done if t == 0 else []))
            f32t.free[fi] = [zz]
            state["last_act"] = zz
        rsr.free[ri] = [zs]

    ip.run_group(6, zp_epilogue)
    if DEBUG_STOP == 70:
        P.run()
        return nc

    W_ = 16 + TN
    flat = accb[:].rearrange("p a n -> p (a n)")
    ubv = [flat[:, c * W_:(c + 1) * W_] for c in range(4)]
    tA = flat[:, 4 * W_:5 * W_]
    tB = flat[:, 5 * W_:6 * W_]
    pooled = Slots(P, "pooled", 2, None, BF16, views=[sz[:, i * 2048:(i + 1) * 2048].rearrange("p (c n) -> p c n", c=4) for i in range(2)])
    pout = Slots(P, "pout", 3, None, BF16, dma=True, views=[sz[:, 4096 + i * TN:4096 + (i + 1) * TN] for i in range(3)])
    pdefer = []

    def u_epilogue(t, banks, lastmm):
        ri, fr = rsr.next()
        rtk = P.dma("sync", rsr.t[ri][:], rs[:, t * TN:(t + 1) * TN], rsr.ch[ri], fr)
        rst = rsr.t[ri]
        sl = slice(t * TN, (t + 1) * TN)
        pi, pfr = pooled.next()
        last = None
        for c in range(4):
            b = banks[c]
            u_ = ubv[c]
            if t == 0:
                P.op("vector", lambda e, u_=u_: e.memset(u_[:, 0:16], 0.0), sig=False, waits=head_done)
            else:
                P.op("vector", lambda e, u_=u_: e.tensor_copy(out=u_[:, 0:16], in_=u_[:, TN:TN + 16]), sig=False)
            ev = P.op("vector", lambda e, u_=u_, a=A[b]: e.tensor_tensor(out=u_[:, 16:16 + TN], in0=a[:, :], in1=rst[:], op=ALU.mult),
                      waits=[lastmm[c], rtk])
            ip.accfree[b] = ev
            P.op("vector", lambda e, u_=u_: e.scalar_tensor_tensor(out=tA[:, 2:W_], in0=u_[:, 1:W_ - 1], scalar=pvs[:, 0:1], in1=u_[:, 2:W_],
                                                                  op0=ALU.mult, op1=ALU.add), sig=False)
            P.op("vector", lambda e: e.scalar_tensor_tensor(out=tB[:, 4:W_], in0=tA[:, 2:W_ - 2], scalar=pvs[:, 1:2], in1=tA[:, 4:W_],
                                                           op0=ALU.mult, op1=ALU.add), sig=False)
            P.op("vector", lambda e: e.scalar_tensor_tensor(out=tA[:, 8:W_], in0=tB[:, 4:W_ - 4], scalar=pvs[:, 2:3], in1=tB[:, 8:W_],
                                                           op0=ALU.mult, op1=ALU.add), sig=False)
            P.op("vector", lambda e: e.scalar_tensor_tensor(out=tB[:, 16:W_], in0=tA[:, 8:W_ - 8], scalar=pvs[:, 3:4], in1=tA[:, 16:W_],
                                                           op0=ALU.mult, op1=ALU.add), sig=False)
            rc = rcs[:, 0:TN] if t == 0 else rcs[:, TN:2 * TN]
            P.op("vector", lambda e, rc=rc: e.tensor_tensor(out=tA[:, 16:W_], in0=tB[:, 16:W_], in1=rc, op=ALU.mult), sig=False)
            last = P.op("vector", lambda e, u_=u_, c=c, pi=pi: e.tensor_tensor(out=pooled.t[pi][:, c, :], in0=tA[:, 16:W_], in1=u_[:, 16:W_],
                                                                             op=ALU.subtract), waits=(pfr + head_done) if c == 0 else [])
        rsr.free[ri] = [last]

        def post(pi=pi, last=last, sl=sl, t=t):
            for oc in range(2):
                mm = None
                for kc in range(4):
                    mm = P.op("tensor", lambda e, oc=oc, kc=kc: e.matmul(N[oc][:, :], lhsT=pwb[:, kc, oc * 128:(oc + 1) * 128],
                                                                          rhs=pooled.t[pi][:, kc, :], start=(kc == 0), stop=(kc == 3)),
                              waits=[last, ldc, ldc0, nfree[oc]], sig=(kc == 3))
                oi, ofr = pout.next()
                fo = P.op("vector", lambda e, oc=oc, o=pout.t[oi]: e.scalar_tensor_tensor(
                    out=o[:], in0=N[oc][:, :], scalar=pvs[:, 4 + oc:5 + oc], in1=szp[oc][:, sl], op0=ALU.mult, op1=ALU.mult),
                    waits=[mm, state["last_act"]] + ofr)
                nfree[oc] = fo
                sd = P.dma("sync", mix[768 + oc * 128:768 + (oc + 1) * 128, sl], pout.t[oi][:], pout.ch[oi], [fo])
                pout.free[oi] = [sd]
            pooled.free[pi] = [mm]
        pdefer.append(post)

    def flush_p():
        nonlocal pdefer
        d, pdefer = pdefer, []
        for fn in d:
            fn()

    ip.run_group(7, u_epilogue, mid_hook=flush_p)
    flush_p()
    P.run()
    return nc


def build_odd1():
    nc = _new_nc()
    hgT = _din(nc, "hgT", [NT, 128, 32, TN], BF16)
    ssq_all = _din(nc, "ssq_all", [8, S], F32)
    w = _din(nc, "w", [D, 3072], F32)
    cvec = _din(nc, "cvec", [128, 8, 8], F32)
    bd = _din(nc, "bd", [128, 3, 8, 128], F32)
    wif = _din(nc, "wif", [128, 3, 8, 16], F32)
    qT_o = _dout(nc, "qT", [1024, S], BF16)
    kT_o = _dout(nc, "kT", [1024, S], BF16)
    vT_o = _dout(nc, "vT", [1024, S], BF16)
    A_o = _dout(nc, "Ao", [1024, S], BF16)
    B_o = _dout(nc, "Bo", [1024, S], BF16)
    gp_o = _dout(nc, "gpart", [16, S], F32)
    P = Prog(nc)
    A = [P.ps(f"A{i}", [128, 512], F32) for i in range(4)]
    Q = [P.ps(f"Q{i}", [128, 512], F32) for i in range(3)]
    Gp = P.ps("Gp", [128, 512], F32)
    rs, rs_sds, qfree0 = emit_rstd_scratch(P, nc, ssq_all, Q[0])
    cst = P.dchan("cst")
    cst2 = P.dchan("cst2")
    cv = P.sb("cv", [128, 8, 8], F32)
    bdb = P.sb("bdb", [128, 3, 8, 128], BF16)
    wifb = P.sb("wifb", [128, 3, 8, 16], BF16)
    ldc0 = P.dma("sync", cv[:], cvec, cst)
    P.dma("gpsimd", bdb[:], bd, cst2)
    ldc = P.dma("gpsimd", wifb[:], wif, cst2)
    gacc = P.sb("gacc", [16, S], F32)
    t_c = P.op("vector", lambda e: e.memset(gacc[:, 0:1], 0.0), waits=[ldc0, ldc])
    rsr = Slots(P, "rsr", 2, [128, TN], F32, dma=True)
    xmb = Slots(P, "xmb", 2, [128, TN + 3], F32)
    f32a = Slots(P, "f32a", 8, [128, TN], F32)
    bfa = Slots(P, "bfa", 6, [128, TN], BF16)
    outb = Slots(P, "outb", 8, [128, TN], BF16, dma=True)
    groups = [(j * 384, 384) for j in range(8)]
    ip = InProj(P, hgT, w, groups, 384, 4, A)
    qfree = [qfree0, None, None]
    gfree = [None]
    deferred = []
    prev_xm = [None]

    def flush():
        nonlocal deferred
        d, deferred = deferred, []
        for fn in d:
            fn()

    def make_epilogue(j):
        def epilogue(t, banks, lastmm):
            ri, fr = rsr.next()
            rtk = P.dma("sync", rsr.t[ri][:], rs[:, t * TN:(t + 1) * TN], rsr.ch[ri], list(fr) + (rs_sds if (j == 0 and t == 0) else []))
            rst = rsr.t[ri]
            sl = slice(t * TN, (t + 1) * TN)
            xi, xfr = xmb.next()
            xm_ = xmb.t[xi]
            if t == 0:
                P.op("vector", lambda e: e.memset(xm_[:, 0:3], 0.0), waits=xfr)
            else:
                pxm = prev_xm[0]
                P.op("vector", lambda e: e.tensor_copy(out=xm_[:, 0:3], in_=pxm[:, TN:TN + 3]), waits=xfr)
            ev = P.op("vector", lambda e: e.tensor_tensor(out=xm_[:, 3:3 + TN], in0=A[banks[0]][:, :], in1=rst[:], op=ALU.mult),
                      waits=[lastmm[0], rtk, t_c])
            ip.accfree[banks[0]] = ev
            prev_xm[0] = xm_
            bi, bfr = bfa.next()
            xm_bf = bfa.t[bi]
            xmc = P.op("scalar", lambda e: e.activation(out=xm_bf[:], in_=xm_[:, 3:3 + TN], func=AF.Copy), waits=[ev] + bfr)
            ci, cfr = f32a.next()
            pre = f32a.t[ci]
            P.op("vector", lambda e: e.tensor_scalar(out=pre[:], in0=xm_[:, 0:TN], scalar1=cv[:, j, 0:1], scalar2=None, op0=ALU.mult), waits=cfr)
            for k_ in (1, 2, 3):
                cvl = P.op("vector", lambda e, k_=k_: e.scalar_tensor_tensor(out=pre[:], in0=xm_[:, k_:k_ + TN], scalar=cv[:, j, k_:k_ + 1], in1=pre[:],
                                                                             op0=ALU.mult, op1=ALU.add))
            xmb.free[xi] = [cvl, xmc]
            si, sfr = f32a.next()
            sg = f32a.t[si]
            sgt = P.op("scalar", lambda e: e.activation(out=sg[:], in_=pre[:], func=AF.Sigmoid, bias=cv[:, j, 4:5]), waits=[cvl] + sfr)
            xci, xcfr = f32a.next()
            xc = f32a.t[xci]
            xct = P.op("vector", lambda e: e.scalar_tensor_tensor(out=xc[:], in0=pre[:], scalar=cv[:, j, 4:5], in1=sg[:], op0=ALU.add, op1=ALU.mult),
                       waits=[sgt] + xcfr)
            f32a.free[ci] = [xct]
            f32a.free[si] = [xct]
            bi2, bfr2 = bfa.next()
            xc_bf = bfa.t[bi2]
            xcc = P.op("scalar", lambda e: e.activation(out=xc_bf[:], in_=xc[:], func=AF.Copy), waits=[xct] + bfr2)
            zi, zfr = f32a.next()
            zs = f32a.t[zi]
            zt = P.op("vector", lambda e: e.tensor_tensor(out=zs[:], in0=A[banks[1]][:, :], in1=rst[:], op=ALU.mult), waits=[lastmm[1], rtk] + zfr)
            ip.accfree[banks[1]] = zt
            gi, gfr = f32a.next()
            sgz = f32a.t[gi]
            sgzt = P.op("scalar", lambda e: e.activation(out=sgz[:], in_=zs[:], func=AF.Sigmoid), waits=[zt] + gfr)
            szt = P.op("vector", lambda e: e.tensor_tensor(out=zs[:], in0=zs[:], in1=sgz[:], op=ALU.mult), waits=[sgzt])
            f32a.free[gi] = [szt]
            oi, ofr = f32a.next()
            os_ = f32a.t[oi]
            ot = P.op("vector", lambda e: e.tensor_tensor(out=os_[:], in0=A[banks[2]][:, :], in1=rst[:], op=ALU.mult), waits=[lastmm[2], rtk] + ofr)
            ip.accfree[banks[2]] = ot
            rsr.free[ri] = [ot]
            ogt = P.op("scalar", lambda e: e.activation(out=os_[:], in_=os_[:], func=AF.Sigmoid), waits=[ot])
            ai, afr = outb.next()
            at = P.op("vector", lambda e: e.scalar_tensor_tensor(out=outb.t[ai][:], in0=os_[:], scalar=cv[:, j, 5:6], in1=zs[:], op0=ALU.mult, op1=ALU.mult),
                      waits=[ogt, szt] + afr)
            outb.free[ai] = [P.dma("sync", A_o[j * 128:(j + 1) * 128, sl], outb.t[ai][:], outb.ch[ai], [at])]
            f32a.free[oi] = [at]
            bi3, bfr3 = outb.next()
            bt = P.op("vector", lambda e: e.scalar_tensor_tensor(out=outb.t[bi3][:], in0=xc[:], scalar=cv[:, j, 6:7], in1=zs[:], op0=ALU.mult, op1=ALU.mult),
                      waits=bfr3)
            outb.free[bi3] = [P.dma("sync", B_o[j * 128:(j + 1) * 128, sl], outb.t[bi3][:], outb.ch[bi3], [bt])]
            f32a.free[xci] = [bt, xcc]
            f32a.free[zi] = [bt]

            def post():
                srcs = [(0, xc_bf, xcc, qT_o), (1, xc_bf, xcc, kT_o), (2, xm_bf, xmc, vT_o)]
                evs = []
                lastm = [None, None, None]
                for which, src, stk, dst in srcs:
                    mm = P.op("tensor", lambda e, which=which, src=src: e.matmul(Q[which][:, :], lhsT=bdb[:, which, j, :], rhs=src[:],
                                                                                 start=True, stop=True), waits=[stk, ldc, qfree[which]])
                    lastm[which] = mm
                    oi2, ofr2 = outb.next()
                    ev2 = P.op("scalar", lambda e, which=which, oi2=oi2: e.activation(out=outb.t[oi2][:], in_=Q[which][:, :], func=AF.Copy),
                               waits=[mm] + ofr2)
                    qfree[which] = ev2
                    sdq = P.dma("sync", dst[j * 128:(j + 1) * 128, sl], outb.t[oi2][:], outb.ch[oi2], [ev2])
                    evs.append((oi2, ev2, sdq))
                bfa.free[bi] = [lastm[2]]
                bfa.free[bi2] = [lastm[0], lastm[1]]
                gm = None
                for which, (oi2, ev2, sdq) in enumerate(evs):
                    gm = P.op("tensor", lambda e, which=which, oi2=oi2: e.matmul(Gp[0:16, :], lhsT=wifb[:, which, j, :], rhs=outb.t[oi2][:],
                                                                                 start=(which == 0), stop=(which == 2)),
                              waits=[ev2, ldc, gfree[0] if which == 0 else None])
                for (oi2, ev2, sdq) in evs:
                    outb.free[oi2] = [sdq, gm]
                if j == 0:
                    ga = P.op("vector", lambda e: e.tensor_copy(out=gacc[:, sl], in_=Gp[0:16, :]), waits=[gm])
                else:
                    ga = P.op("vector", lambda e: e.tensor_tensor(out=gacc[:, sl], in0=gacc[:, sl], in1=Gp[0:16, :], op=ALU.add), waits=[gm])
                gfree[0] = ga
            deferred.append(post)
        return epilogue

    for j in range(8):
        ip.run_group(j, make_epilogue(j), mid_hook=flush)
    flush()
    gch = P.dchan("gch")
    P.dma("sync", gp_o, gacc[:], gch, [gfree[0]])
    P.run()
    return nc


LN32 = 3.4657359027997265


def build_odd2():
    nc = _new_nc()
    qT = _din(nc, "qT", [1024, S], BF16)
    kT = _din(nc, "kT", [1024, S], BF16)
    ktm = _din(nc, "ktm", [S, 1024], BF16)
    vtm = _din(nc, "vtm", [S, 1024], BF16)
    Ai = _din(nc, "Ao", [1024, S], BF16)
    Bi = _din(nc, "Bo", [1024, S], BF16)
    gp_all = _din(nc, "gp_all", [128, S], F32)
    sel = _din(nc, "sel", [128, 2], F32)
    bsel = _din(nc, "bsel", [128, 2], F32)
    tri_i = _din(nc, "tri", [128, 128], F32)
    idf_i = _din(nc, "identf", [128, 128], F32)
    mix = _dout(nc, "mix", [1024, S], BF16)
    P = Prog(nc)
    Sps = P.ps("Sps", [128, 512], F32)
    NUM = [P.ps(f"NUM{i}", [128, 512], F32) for i in range(2)]
    SM = P.ps("SM", [128, 512], F32)
    DC = [P.ps(f"DC{i}", [128, 512], F32) for i in range(2)]
    TP = [P.ps(f"TP{i}", [128, 512], F32) for i in range(2)]
    NC_ = 64
    cst = P.dchan("cst")
    C = P.sb("C", [128, 8200], F32)
    Cv = C[:, 0:8200].rearrange("p (d n) -> p d n", d=8)
    Cbf = P.sb("Cbf", [128, 8200], BF16)
    Cbv = Cbf[:, 0:8200].rearrange("p (d n) -> p d n", d=8)
    sel_s = P.sb("sel_s", [128, 2], F32)
    bs_s = P.sb("bs_s", [128, 2], F32)
    tri = P.sb("tri_s", [128, 128], F32)
    idf = P.sb("idf", [128, 128], F32)
    P.dma("sync", C[:, 0:S], gp_all, cst)
    P.dma("sync", sel_s[:], sel, cst)
    P.dma("sync", bs_s[:], bsel, cst)
    P.dma("sync", tri[:], tri_i, cst)
    ldc = P.dma("sync", idf[:], idf_i, cst)
    ones_r = P.sb("ones_r", [1, 128], F32)
    ones_b = P.sb("ones_b", [128, 1], BF16)
    epsb = P.sb("epsb", [128, 1], F32)
    sm = {n: P.sb("g_" + n, [128, 64], F32) for n in ("ipre", "fpre", "lf", "b", "a", "w", "g", "fl", "Mb", "mhb", "t1")}
    rowA = P.sb("rowA", [1, 64], F32)
    rowB = P.sb("rowB", [1, 64], F32)
    rowM = P.sb("rowM", [1, 192], F32)
    amT = P.sb("amT", [64, 1], F32)
    P.op("vector", lambda e: e.memset(ones_r[:], 1.0))
    P.op("vector", lambda e: e.memset(ones_b[:], 1.0))
    P.op("vector", lambda e: e.memset(epsb[:], EPS))
    t0 = P.op("vector", lambda e: e.memset(rowM[:], -80.0), waits=[ldc])
    mm = None
    for cc in range(NC_):
        mm = P.op("tensor", lambda e, cc=cc: e.matmul(SM[:, 2 * cc:2 * cc + 2], lhsT=C[:, cc * 128:(cc + 1) * 128], rhs=sel_s[:, 0:2],
                                                      start=True, stop=True), waits=[ldc], sig=(cc == NC_ - 1))
    SMv = SM[:, 0:128].rearrange("p (c two) -> p c two", two=2)
    P.op("vector", lambda e: e.tensor_scalar(out=sm["ipre"][:], in0=SMv[:, :, 0], scalar1=bs_s[:, 0:1], scalar2=None, op0=ALU.add), waits=[mm, t0])
    f1 = P.op("vector", lambda e: e.tensor_scalar(out=sm["fpre"][:], in0=SMv[:, :, 1], scalar1=bs_s[:, 1:2], scalar2=None, op0=ALU.add))
    a1 = P.op("scalar", lambda e: e.activation(out=sm["t1"][:], in_=sm["fpre"][:], func=AF.Exp, scale=-1.0), waits=[f1])
    a2 = P.op("scalar", lambda e: e.activation(out=sm["t1"][:], in_=sm["t1"][:], func=AF.Ln, bias=1.0))
    d1 = P.op("vector", lambda e: e.tensor_scalar(out=sm["lf"][:], in0=sm["t1"][:], scalar1=-1.0, scalar2=None, op0=ALU.mult), waits=[a2])
    m1 = P.op("tensor", lambda e: e.matmul(SM[:, 128:192], lhsT=tri[:, :], rhs=sm["lf"][:], start=True, stop=True), waits=[d1, f1])
    d2 = P.op("vector", lambda e: e.tensor_copy(out=sm["b"][:], in_=SM[:, 128:192]), waits=[m1])
    d3 = P.op("vector", lambda e: e.tensor_tensor(out=sm["a"][:], in0=sm["ipre"][:], in1=sm["b"][:], op=ALU.subtract))
    m2 = P.op("tensor", lambda e: e.transpose(SM[0:64, 192:320], sm["a"][:, 0:64], idf[:, :]), waits=[d3])
    d4 = P.op("vector", lambda e: e.tensor_reduce(out=amT[:], in_=SM[0:64, 192:320], axis=AX.X, op=ALU.max), waits=[m2])
    m3 = P.op("tensor", lambda e: e.transpose(SM[0:1, 320:384], amT[0:64, 0:1], idf[0:64, 0:64]), waits=[d4])
    m4 = P.op("tensor", lambda e: e.matmul(SM[0:1, 384:448], lhsT=idf[:, 127:128], rhs=sm["b"][:], start=True, stop=True), waits=[d2])
    P.op("vector", lambda e: e.tensor_copy(out=rowA[:], in_=SM[0:1, 320:384]), waits=[m3, m4])
    P.op("vector", lambda e: e.tensor_copy(out=rowB[:], in_=SM[0:1, 384:448]))
    for cc in range(NC_):
        P.op("vector", lambda e, cc=cc: e.tensor_tensor(out=rowM[0:1, cc:cc + 1], in0=rowM[0:1, 64 + cc:65 + cc], in1=rowA[0:1, cc:cc + 1], op=ALU.max))
        d5 = P.op("vector", lambda e, cc=cc: e.tensor_tensor(out=rowM[0:1, 65 + cc:66 + cc], in0=rowB[0:1, cc:cc + 1], in1=rowM[0:1, cc:cc + 1], op=ALU.add))
    m5 = P.op("tensor", lambda e: e.matmul(SM[:, 0:128], lhsT=ones_r[0:1, :], rhs=rowM[0:1, 0:128], start=True, stop=True), waits=[d5])
    P.op("vector", lambda e: e.tensor_copy(out=sm["Mb"][:], in_=SM[:, 0:64]), waits=[m5])
    d6 = P.op("vector", lambda e: e.tensor_copy(out=sm["mhb"][:], in_=SM[:, 64:128]))
    P.op("vector", lambda e: e.tensor_tensor(out=sm["w"][:], in0=sm["a"][:], in1=sm["Mb"][:], op=ALU.subtract))
    P.op("vector", lambda e: e.tensor_scalar(out=sm["w"][:], in0=sm["w"][:], scalar1=-LN32, scalar2=None, op0=ALU.add))
    P.op("vector", lambda e: e.tensor_tensor(out=sm["g"][:], in0=sm["mhb"][:], in1=sm["Mb"][:], op=ALU.subtract))
    d7 = P.op("vector", lambda e: e.tensor_tensor(out=sm["fl"][:], in0=sm["b"][:], in1=sm["Mb"][:], op=ALU.add))
    P.op("scalar", lambda e: e.activation(out=sm["w"][:], in_=sm["w"][:], func=AF.Exp), waits=[d7])
    P.op("scalar", lambda e: e.activation(out=sm["g"][:], in_=sm["g"][:], func=AF.Exp))
    gts = P.op("scalar", lambda e: e.activation(out=sm["fl"][:], in_=sm["fl"][:], func=AF.Exp, scale=-1.0))
    czero = P.op("vector", lambda e: e.memset(C[:], 0.0), waits=[gts, m5])
    GR = 4
    qv = qT.rearrange("(d p) t -> p d t", p=128)
    kv = kT.rearrange("(d p) t -> p d t", p=128)
    Av = Ai.rearrange("(d p) t -> p d t", p=128)
    Bv = Bi.rearrange("(d p) t -> p d t", p=128)
    mv_ = mix.rearrange("(d p) t -> p d t", p=128)
    ktv = ktm.rearrange("(c p) d -> p c d", p=128)
    vtv = vtm.rearrange("(c p) d -> p c d", p=128)
    ld = {n: Slots(P, "l_" + n, 2, [128, 8, 512] if n in ("q", "k", "A", "B") else [128, GR, 1024], BF16, dma=True)
          for n in ("q", "k", "A", "B", "kt", "vt")}
    kw = Slots(P, "kw", 2, [128, 1024], BF16)
    STb = Slots(P, "STb", 2, [128, 128], BF16)
    hc = Slots(P, "hc", 2, [128, 1024], F32)
    hn = Slots(P, "hn", 2, [128, 1024], F32)
    otmp = Slots(P, "otmp", 2, [128, 512], F32)
    outst = Slots(P, "outst", 2, [128, 8, 128], BF16, dma=True)
    small = Slots(P, "small", 2, [128, 8], F32)
    stats = Slots(P, "stats", 2, [128, 12], F32)
    free = {"S": None, "NUM": [None, None], "SMd": czero, "DC": [None, None], "TP": [None, None]}
    grp = {}
    users = {}

    def load_group(g):
        sl = slice(g * 512, (g + 1) * 512)
        out = {}
        for n, src in (("q", qv[:, :, sl]), ("k", kv[:, :, sl]), ("A", Av[:, :, sl]), ("B", Bv[:, :, sl]),
                       ("kt", ktv[:, g * GR:(g + 1) * GR, :]), ("vt", vtv[:, g * GR:(g + 1) * GR, :])):
            i_, fr = ld[n].next()
            tk = P.dma("sync", ld[n].t[i_][:], src, ld[n].ch[i_], fr)
            out[n] = (i_, tk)
        grp[g] = out

    load_group(0)
    stt8 = {"upd": czero}

    def chunk(cc):
            g, ci = cc // GR, cc % GR
            if ci == 0 and g + 1 < NC_ // GR:
                load_group(g + 1)
            G_ = grp[g]
            csl = slice(ci * 128, (ci + 1) * 128)
            qg = ld["q"].t[G_["q"][0]]
            kg = ld["k"].t[G_["k"][0]]
            Ag = ld["A"].t[G_["A"][0]]
            Bg = ld["B"].t[G_["B"][0]]
            ktg = ld["kt"].t[G_["kt"][0]]
            vtg = ld["vt"].t[G_["vt"][0]]
            wcol = sm["w"][:, cc:cc + 1]
            gcol = sm["g"][:, cc:cc + 1]
            mm = None
            for dt in range(8):
                mm = P.op("tensor", lambda e, dt=dt: e.matmul(Sps[:, 0:128], lhsT=kg[:, dt, csl], rhs=qg[:, dt, csl], start=(dt == 0), stop=(dt == 7)),
                          waits=[G_["k"][1], G_["q"][1], free["S"]] if dt == 0 else [], sig=(dt == 7))
            si, sfr = STb.next()
            ST = STb.t[si]
            stt = P.op("vector", lambda e: e.scalar_tensor_tensor(out=ST[:], in0=Sps[:, 0:128], scalar=wcol, in1=tri[:], op0=ALU.mult, op1=ALU.mult),
                       waits=[mm, gts] + sfr)
            free["S"] = stt
            ki, kfr = kw.next()
            kwt = P.op("gpsimd", lambda e: e.tensor_scalar(out=kw.t[ki][:], in0=ktg[:, ci, :], scalar1=wcol, scalar2=None, op0=ALU.mult),
                       waits=[G_["kt"][1], gts] + kfr)
            cast = P.op("scalar", lambda e: e.activation(out=Cbf[:], in_=C[:], func=AF.Copy, scale=gcol), waits=[stt8["upd"], users.get("Cbf")])
            lastn = None
            for n in range(2):
                P.op("tensor", lambda e, n=n: e.matmul(NUM[n][:, :], lhsT=ST[:], rhs=vtg[:, ci, n * 512:(n + 1) * 512], start=True, stop=False),
                     waits=[stt, G_["vt"][1], cast, free["NUM"][n]], sig=False)
                for dt in range(8):
                    lastn = P.op("tensor", lambda e, n=n, dt=dt: e.matmul(NUM[n][:, :], lhsT=qg[:, dt, csl], rhs=Cbv[:, dt, n * 512:(n + 1) * 512],
                                                                          start=False, stop=(dt == 7)), sig=(dt == 7))
            P.op("tensor", lambda e: e.matmul(SM[:, 0:1], lhsT=ST[:], rhs=ones_b[:, 0:1], start=True, stop=False), waits=[free["SMd"]], sig=False)
            lastd = None
            for dt in range(8):
                lastd = P.op("tensor", lambda e, dt=dt: e.matmul(SM[:, 0:1], lhsT=qg[:, dt, csl], rhs=Cbv[:, dt, 1024:1025], start=False, stop=(dt == 7)),
                             sig=(dt == 7))
            STb.free[si] = [lastd]
            smi, smf = small.next()
            sv = small.t[smi]
            P.op("vector", lambda e: e.tensor_scalar(out=sv[:, 6:7], in0=SM[:, 0:1], scalar1=-1.0, scalar2=None, op0=ALU.mult), waits=[lastd] + smf)
            P.op("vector", lambda e: e.tensor_tensor(out=sv[:, 0:1], in0=SM[:, 0:1], in1=sv[:, 6:7], op=ALU.max))
            P.op("vector", lambda e: e.tensor_tensor(out=sv[:, 0:1], in0=sv[:, 0:1], in1=sm["fl"][:, cc:cc + 1], op=ALU.max))
            rr = P.op("vector", lambda e: e.reciprocal(out=sv[:, 1:2], in_=sv[:, 0:1]))
            lastnc = None
            upd = None
            for dt in range(8):
                for n in range(2):
                    mmc = P.op("tensor", lambda e, dt=dt, n=n: e.matmul(DC[n][:, :], lhsT=kw.t[ki][:, dt * 128:(dt + 1) * 128],
                                                                         rhs=vtg[:, ci, n * 512:(n + 1) * 512], start=True, stop=True),
                               waits=[kwt, free["DC"][n]])
                    upd = P.op("vector", lambda e, dt=dt, n=n: e.scalar_tensor_tensor(
                        out=Cv[:, dt, n * 512:(n + 1) * 512], in0=Cv[:, dt, n * 512:(n + 1) * 512], scalar=gcol, in1=DC[n][:, :],
                        op0=ALU.mult, op1=ALU.add), waits=[mmc, cast])
                    free["DC"][n] = upd
                lastnc = P.op("tensor", lambda e, dt=dt: e.matmul(SM[:, 8 + dt:9 + dt], lhsT=kw.t[ki][:, dt * 128:(dt + 1) * 128], rhs=ones_b[:, 0:1],
                                                                  start=True, stop=True), waits=[rr] if dt == 0 else [], sig=(dt == 7))
            kw.free[ki] = [lastnc]
            stt8["upd"] = P.op("vector", lambda e: e.scalar_tensor_tensor(out=Cv[:, :, 1024], in0=Cv[:, :, 1024], scalar=gcol, in1=SM[:, 8:16],
                                                                       op0=ALU.mult, op1=ALU.add), waits=[lastnc, cast])
            free["SMd"] = stt8["upd"]
            users["Cbf"] = lastd
            hi, hfr = hc.next()
            hct = None
            for n in range(2):
                hct = P.op("scalar", lambda e, n=n: e.activation(out=hc.t[hi][:, n * 512:(n + 1) * 512], in_=NUM[n][:, :], func=AF.Copy, scale=sv[:, 1:2]),
                           waits=[lastn, rr] + (hfr if n == 0 else []))
                free["NUM"][n] = hct
            sti, stf = stats.next()
            stt_ = stats.t[sti]
            for n in range(2):
                P.op("vector", lambda e, n=n: e.bn_stats(out=stt_[:, n * 6:(n + 1) * 6], in_=hc.t[hi][:, n * 512:(n + 1) * 512]), waits=[hct] + stf)
            ag = P.op("vector", lambda e: e.bn_aggr(out=sv[:, 2:4], in_=stt_[:, 0:12]))
            sq_ = P.op("scalar", lambda e: e.activation(out=sv[:, 4:5], in_=sv[:, 3:4], func=AF.Sqrt, bias=epsb[:, 0:1]), waits=[ag])
            P.op("vector", lambda e: e.reciprocal(out=sv[:, 5:6], in_=sv[:, 4:5]), waits=[sq_])
            ni, nfr = hn.next()
            hnt = P.op("vector", lambda e: e.tensor_scalar(out=hn.t[ni][:], in0=hc.t[hi][:], scalar1=sv[:, 2:3], scalar2=sv[:, 5:6],
                                                          op0=ALU.subtract, op1=ALU.mult), waits=nfr)
            hc.free[hi] = [hnt]
            stats.free[sti] = [hnt]
            small.free[smi] = [hnt]
            oi, ofr = outst.next()
            fin = None
            lasttr = None
            for hb_ in range(2):
                for q4 in range(4):
                    dt = hb_ * 4 + q4
                    lasttr = P.op("tensor", lambda e, dt=dt, q4=q4, hb_=hb_: e.transpose(TP[hb_][:, q4 * 128:(q4 + 1) * 128], hn.t[ni][:, dt * 128:(dt + 1) * 128], idf[:, :]),
                                  waits=[hnt, free["TP"][hb_]] if q4 == 0 else [], sig=(q4 == 3))
                ti_, tfr = otmp.next()
                tpv = TP[hb_][:, :].rearrange("p (d n) -> p d n", d=4)
                o1 = P.op("vector", lambda e, hb_=hb_, ti_=ti_, tpv=tpv: e.tensor_tensor(
                    out=otmp.t[ti_][:].rearrange("p (d n) -> p d n", d=4), in0=tpv, in1=Ag[:, hb_ * 4:(hb_ + 1) * 4, csl], op=ALU.mult),
                    waits=[lasttr, G_["A"][1]] + tfr)
                free["TP"][hb_] = o1
                fin = P.op("vector", lambda e, hb_=hb_, ti_=ti_: e.tensor_tensor(
                    out=outst.t[oi][:, hb_ * 4:(hb_ + 1) * 4, :], in0=otmp.t[ti_][:].rearrange("p (d n) -> p d n", d=4),
                    in1=Bg[:, hb_ * 4:(hb_ + 1) * 4, csl], op=ALU.add), waits=[G_["B"][1]] + (ofr if hb_ == 0 else []))
                otmp.free[ti_] = [fin]
            hn.free[ni] = [lasttr]
            sd = P.dma("sync", mv_[:, :, cc * 128:(cc + 1) * 128], outst.t[oi][:], outst.ch[oi], [fin])
            outst.free[oi] = [sd]
            if ci == GR - 1:
                for n in ("q", "k", "A", "B", "kt", "vt"):
                    ld[n].free[G_[n][0]] = [fin, lastd, lastnc, mm]

    for cc in range(NC_):
        chunk(cc)
    P.run()
    return nc


ATTN_W = 6144
POOL_WINDOWS = (2, 4, 8, 16)


def _t5_bucket(dist):
    max_exact = 16
    safe = np.maximum(dist, 1).astype(np.float32)
    large = max_exact + (np.log(safe / max_exact) / np.log(2048 / max_exact) * (32 - max_exact)).astype(np.int32)
    large = np.minimum(large, 31)
    return np.where(dist < max_exact, dist, large).astype(np.int32)


def _vec4(v512):
    return np.ascontiguousarray(v512.reshape(4, 128).T)


def even_static_inputs(rel_bias, w_in, q_gain, k_gain, pool_w, pool_scale):
    j = np.arange(128)[:, None]
    ip = np.arange(256)[None, :]
    rel = np.where(ip < 128, ip + 128 - j, (ip - 128) - j)
    valid = (rel >= 0) & (rel <= 128)
    mtab = np.where(valid, 0.0, -30000.0).astype(np.float32)
    relc = np.clip(rel, 0, 128)
    buckets = [_t5_bucket(relc * d) for d, _ in PATTERNS]
    identb = np.eye(128, dtype=np.float32).astype(NPBF)
    maps = []
    for c in range(NCORES):
        cols = []
        for hh in range(6):
            h = 6 * c + hh
            for base in (0, ATTN_W, 2 * ATTN_W):
                cols.append(np.arange(base + h * 128, base + (h + 1) * 128))
            cols.append(np.arange(20480 + h * 128, 20480 + (h + 1) * 128))
        g, half = c // 2, c % 2
        cols.append(np.arange(18432 + g * 512, 18432 + (g + 1) * 512))
        cols.append(np.arange(20480 + ATTN_W + g * 512 + half * 256, 20480 + ATTN_W + g * 512 + (half + 1) * 256))
        cols = np.concatenate(cols)
        wc = np.ascontiguousarray(w_in[:, cols])
        bt = np.empty((128, 6, 3, 256), np.float32)
        for hh in range(6):
            for g_ in range(3):
                bt[:, hh, g_, :] = rel_bias[buckets[g_], 6 * c + hh]
        wdw = POOL_WINDOWS[g]
        pvec = np.zeros((128, 8), np.float32)
        for i_, sft in enumerate((1, 2, 4, 8)):
            pvec[:, i_] = 1.0 if sft < wdw else 0.0
        ps = pool_scale[g * 512 + half * 256: g * 512 + (half + 1) * 256]
        pvec[:, 4] = ps[0:128]
        pvec[:, 5] = ps[128:256]
        tt = np.arange(512)
        rctab = np.empty((128, 1024), np.float32)
        rctab[:, 0:512] = (1.0 / np.minimum(tt + 1, wdw))[None, :]
        rctab[:, 512:1024] = 1.0 / wdw
        maps.append({
            "w": wc,
            "gains": np.ascontiguousarray(np.stack([q_gain, k_gain], axis=1).astype(np.float32)),
            "btab": np.ascontiguousarray(bt.reshape(128, 6 * 768)),
            "mtab": mtab, "identb": identb,
            "pw": np.ascontiguousarray(pool_w[g][:, half * 256:(half + 1) * 256]),
            "pvec": pvec, "rctab": rctab,
        })
    return maps


def even_wout_perm():
    rows = []
    for c in range(NCORES):
        rows.append(np.arange(c * 768, (c + 1) * 768))
        g, half = c // 2, c % 2
        rows.append(np.arange(ATTN_W + g * 512 + half * 256, ATTN_W + g * 512 + (half + 1) * 256))
    return np.concatenate(rows)


def odd_static_inputs(w_up, conv_w, conv_b, wq, wk, wv, w_if, b_if, gn, skip):
    maps = []
    MW = 8192
    eye32 = np.zeros((32, 4, 32, 4), np.float32)
    for c in range(NCORES):
        cols = []
        for j in range(8):
            for base in (0, MW, 2 * MW):
                cols.append(np.arange(base + c * 1024 + j * 128, base + c * 1024 + (j + 1) * 128))
        cols = np.concatenate(cols)
        wc = np.ascontiguousarray(w_up[:, cols])
        ch = np.arange(c * 1024, (c + 1) * 1024).reshape(8, 128)
        cvec = np.zeros((128, 8, 8), np.float32)
        for k_ in range(4):
            cvec[:, :, k_] = conv_w[k_][ch].T
        cvec[:, :, 4] = conv_b[ch].T
        cvec[:, :, 5] = gn[ch].T
        cvec[:, :, 6] = skip[ch].T
        bd = np.zeros((128, 3, 8, 128), np.float32)
        for wi, wm in enumerate((wq, wk, wv)):
            for j in range(8):
                b0 = (c * 1024 + j * 128) // 4
                blk = wm[b0:b0 + 32]
                m = np.zeros((32, 4, 32, 4), np.float32)
                idx = np.arange(32)
                m[idx, :, idx, :] = blk
                bd[:, wi, j, :] = m.reshape(128, 128)
        wif = np.zeros((128, 3, 8, 16), np.float32)
        for wi in range(3):
            rows = wi * MW + ch
            wif[:, wi, :, :] = np.transpose(w_if[rows], (1, 0, 2))
        sel = np.zeros((128, 2), np.float32)
        for r in range(8):
            sel[r * 16 + c, 0] = 1.0
            sel[r * 16 + 8 + c, 1] = 1.0
        bsel = np.zeros((128, 2), np.float32)
        bsel[:, 0] = b_if[c]
        bsel[:, 1] = b_if[8 + c]
        maps.append({"w": wc, "cvec": cvec, "bd": bd, "wif": wif, "sel": sel, "bsel": bsel})
    return maps


_CACHE = {}


def _tile_fm(a):
    kc = a.shape[0] // 128
    return np.ascontiguousarray(a.reshape(kc, 128, NT, TN).transpose(2, 1, 0, 3))


def _prog(name, builder):
    if name not in _CACHE:
        _CACHE[name] = builder()
    return _CACHE[name]


def _launch(name, builder, in_maps):
    nc = _prog(name, builder)
    res = run_bass_kernel_spmd(nc, in_maps, core_ids=list(range(NCORES)))
    return res.results


def kernel(x, rel_bias, e_norm, e_w_in, e_q_gain, e_k_gain, e_pool_w, e_pool_scale, e_w_out,
           o_norm, o_w_up, o_conv_w, o_conv_b, o_wq, o_wk, o_wv, o_w_if, o_b_if, o_gn, o_skip, o_w_down):
    f32 = np.float32
    x = np.asarray(x, f32)
    xT = np.ascontiguousarray(x[0].T)
    hT = [xT[c * 512:(c + 1) * 512] for c in range(NCORES)]
    norms = [np.asarray(e_norm[0], f32), np.asarray(o_norm[0], f32), np.asarray(e_norm[1], f32), np.asarray(o_norm[1], f32),
             np.ones((D,), f32)]
    res = _launch("prep", build_prep, [{"hT": hT[c], "gain": _vec4(norms[0][c * 512:(c + 1) * 512])} for c in range(NCORES)])
    hg = [r["hg"] for r in res]
    ssq = [r["ssq"][0] for r in res]
    tri = np.triu(np.ones((128, 128), f32))
    identf = np.eye(128, dtype=f32)
    perm = even_wout_perm()
    for layer in range(4):
        j = layer // 2
        hg_full = _tile_fm(np.concatenate(hg, axis=0))
        ssq_all = np.ascontiguousarray(np.stack(ssq).astype(f32))
        if layer % 2 == 0:
            st = even_static_inputs(np.asarray(rel_bias, f32), np.asarray(e_w_in[j], f32), np.asarray(e_q_gain[j], f32),
                                    np.asarray(e_k_gain[j], f32), np.asarray(e_pool_w[j], f32), np.asarray(e_pool_scale[j], f32))
            res = _launch("even", build_even, [dict(st[c], hgT=hg_full, ssq_all=ssq_all) for c in range(NCORES)])
            mix_full = _tile_fm(np.concatenate([r["mix"] for r in res], axis=0))
            wout = np.asarray(e_w_out[j], f32)[perm]
        else:
            st = odd_static_inputs(np.asarray(o_w_up[j], f32), np.asarray(o_conv_w[j], f32), np.asarray(o_conv_b[j], f32),
                                   np.asarray(o_wq[j], f32), np.asarray(o_wk[j], f32), np.asarray(o_wv[j], f32),
                                   np.asarray(o_w_if[j], f32), np.asarray(o_b_if[j], f32), np.asarray(o_gn[j], f32),
                                   np.asarray(o_skip[j], f32))
            r1 = _launch("odd1", build_odd1, [dict(w=st[c]["w"], cvec=st[c]["cvec"], bd=st[c]["bd"], wif=st[c]["wif"],
                                                   hgT=hg_full, ssq_all=ssq_all) for c in range(NCORES)])
            gp_all = np.ascontiguousarray(np.concatenate([r["gpart"] for r in r1], axis=0).astype(f32))
            maps = []
            for c in range(NCORES):
                r = r1[c]
                maps.append(dict(qT=r["qT"], kT=r["kT"], ktm=np.ascontiguousarray(r["kT"].T), vtm=np.ascontiguousarray(r["vT"].T),
                                 Ao=r["Ao"], Bo=r["Bo"], gp_all=gp_all, sel=st[c]["sel"], bsel=st[c]["bsel"], tri=tri, identf=identf))
            r2 = _launch("odd2", build_odd2, maps)
            mix_full = _tile_fm(np.concatenate([r["mix"] for r in r2], axis=0))
            wout = np.asarray(o_w_down[j], f32)
        gnext = norms[layer + 1]
        res = _launch("out", build_out, [{"mixT": mix_full, "w": np.ascontiguousarray(wout[:, c * 512:(c + 1) * 512]), "hT": np.ascontiguousarray(hT[c]),
                                          "gain": _vec4(gnext[c * 512:(c + 1) * 512])} for c in range(NCORES)])
        hT = [r["hout"] for r in res]
        hg = [r["hg"] for r in res]
        ssq = [r["ssq"][0] for r in res]
    out = np.ascontiguousarray(np.concatenate(hT, axis=0).T.astype(f32))[None]
    return out
```

```python
import numpy as np
from contextlib import ExitStack
import ml_dtypes
import concourse.bass as bass
import concourse.mybir as mybir
from concourse.bass_utils import run_bass_kernel_spmd

F32 = mybir.dt.float32
BF16 = mybir.dt.bfloat16
AF = mybir.ActivationFunctionType
ALU = mybir.AluOpType
AX = mybir.AxisListType
NPBF = ml_dtypes.bfloat16

NCORES = 8
S = 8192
D = 4096
EPS = 1e-6
TN = 512
NT = S // TN

_ENGS = ("sync", "scalar", "vector", "gpsimd", "tensor")
SERIALIZE = ("scalar", "vector", "gpsimd")


class Chan:
    def __init__(self, sem, dma):
        self.sem = sem
        self.n = 0
        self.dma = dma


class Prog:
    def __init__(self, nc):
        self.nc = nc
        self.es = ExitStack()
        self.ops = {e: [] for e in _ENGS}
        self.k = 0
        self.ech = {e: self._chan(e, False) for e in _ENGS if e != "sync"}
        self.dchans = []

    def _chan(self, name, dma):
        self.k += 1
        return Chan(self.es.enter_context(self.nc.semaphore(f"s_{name}_{self.k}")), dma)

    def dchan(self, name="d"):
        c = self._chan(name, True)
        self.dchans.append(c)
        return c

    def sb(self, name, shape, dt):
        return self.es.enter_context(self.nc.sbuf_tensor(name, list(shape), dt))

    def ps(self, name, shape, dt):
        return self.es.enter_context(self.nc.psum_tensor(name, list(shape), dt))

    def op(self, eng, fn, waits=(), sig=True):
        if eng in SERIALIZE:
            sig = True
        ch = self.ech[eng] if sig else None
        tk = None
        if ch is not None:
            ch.n += 1
            tk = (ch, ch.n)
        self.ops[eng].append((fn, tuple(w for w in waits if w is not None), ch, 1))
        return tk

    def dma(self, eng, out, in_, chan, waits=()):
        chan.n += 16
        self.ops[eng].append((lambda e, o=out, i=in_: e.dma_start(out=o, in_=i),
                              tuple(w for w in waits if w is not None), chan, 16))
        return (chan, chan.n)

    def simulate(self):
        val = {}
        pos = {e: 0 for e in _ENGS}
        total = sum(len(v) for v in self.ops.values())
        done = 0
        while done < total:
            progressed = False
            for e in _ENGS:
                while pos[e] < len(self.ops[e]):
                    fn, waits, ch, inc = self.ops[e][pos[e]]
                    if any(val.get(id(c), 0) < v for c, v in waits):
                        break
                    if ch is not None:
                        val[id(ch)] = val.get(id(ch), 0) + inc
                    pos[e] += 1
                    done += 1
                    progressed = True
            if not progressed:
                msg = []
                for e in _ENGS:
                    if pos[e] < len(self.ops[e]):
                        fn, waits, ch, inc = self.ops[e][pos[e]]
                        msg.append((e, pos[e], len(self.ops[e]), [(c.sem, v, val.get(id(c), 0)) for c, v in waits if val.get(id(c), 0) < v]))
                raise RuntimeError("DEADLOCK in recorded program: %r" % (msg,))
        return True

    def run(self):
        nc = self.nc
        self.simulate()
        with nc.Block() as block:
            for name in _ENGS:
                def body(eng, name=name):
                    waited = {}
                    nsig = 0
                    for fn, waits, ch, inc in self.ops[name]:
                        if name in SERIALIZE and inc == 1 and nsig > 0:
                            waits = tuple(waits) + ((self.ech[name], nsig),)
                        if name in SERIALIZE and inc == 1:
                            nsig += 1
                        for c, v in waits:
                            if waited.get(id(c), 0) >= v:
                                continue
                            eng.wait_ge(c.sem, v)
                            waited[id(c)] = v
                        ins = fn(eng)
                        if ch is not None:
                            ins.then_inc(ch.sem, inc)
                    if name == "sync":
                        for c in self.dchans:
                            if c.n > 0:
                                eng.wait_ge(c.sem, c.n)
                getattr(block, name)(body)
        self.es.close()


class Slots:
    def __init__(self, P, name, n, shape, dt, dma=False, views=None):
        self.t = list(views) if views is not None else [P.sb(f"{name}{i}", shape, dt) for i in range(n)]
        self.ch = [P.dchan(f"{name}{i}") for i in range(n)] if dma else None
        self.free = [[] for _ in range(n)]
        self.i = -1
        self.n = n

    def next(self):
        self.i = (self.i + 1) % self.n
        fr = self.free[self.i]
        self.free[self.i] = []
        return self.i, fr


def _new_nc():
    return bass.Bass("TRN2", target_bir_lowering=False)


def _din(nc, name, shape, dt):
    return nc.dram_tensor(name, list(shape), dt, kind="ExternalInput").ap()


def _dout(nc, name, shape, dt):
    return nc.dram_tensor(name, list(shape), dt, kind="ExternalOutput").ap()


def build_prep():
    nc = _new_nc()
    hT = _din(nc, "hT", [512, S], F32)
    gain = _din(nc, "gain", [128, 4], F32)
    hg = _dout(nc, "hg", [512, S], BF16)
    ssq = _dout(nc, "ssq", [1, S], F32)
    P = Prog(nc)
    ones = P.sb("ones", [128, 1], F32)
    g = P.sb("g", [128, 4], F32)
    cst = P.dchan("cst")
    hb = Slots(P, "hb", 3, [128, 2048], F32, dma=True)
    sq = Slots(P, "sq", 2, [128, 2048], F32)
    ob = Slots(P, "ob", 3, [128, 2048], BF16, dma=True)
    st = Slots(P, "st", 2, [1, 2048], F32, dma=True)
    acc = [P.ps(f"acc{i}", [128, 512], F32) for i in range(4)]
    ldc = P.dma("sync", g[:], gain, cst)
    t_ones = P.op("vector", lambda e: e.memset(ones[:], 1.0))
    accfree = [None] * 4
    for tb in range(S // 2048):
        mm = None
        for r in range(4):
            hi, fr = hb.next()
            ld = P.dma("sync", hb.t[hi][:], hT[r * 128:(r + 1) * 128, tb * 2048:(tb + 1) * 2048], hb.ch[hi], fr)
            si, fr2 = sq.next()
            tk = P.op("scalar", lambda e, a=sq.t[si], b=hb.t[hi]: e.activation(out=a[:], in_=b[:], func=AF.Square),
                      waits=[ld] + fr2)
            oi, fr4 = ob.next()
            tg = P.op("vector", lambda e, o=ob.t[oi], a=hb.t[hi], r=r: e.tensor_scalar(
                out=o[:], in0=a[:], scalar1=g[:, r:r + 1], scalar2=None, op0=ALU.mult), waits=[ld, ldc] + fr4)
            hb.free[hi] = [tk, tg]
            sd0 = P.dma("gpsimd", hg[r * 128:(r + 1) * 128, tb * 2048:(tb + 1) * 2048], ob.t[oi][:], ob.ch[oi], [tg])
            ob.free[oi] = [sd0]
            for q in range(4):
                mm = P.op("tensor", lambda e, o=acc[q], a=sq.t[si], q=q, r=r: e.matmul(
                    o[0:1, :], lhsT=ones[:, 0:1], rhs=a[:, q * 512:(q + 1) * 512], start=(r == 0), stop=(r == 3)),
                    waits=[tk, t_ones, accfree[q] if r == 0 else None])
            sq.free[si] = [mm]
        oi, fr3 = st.next()
        cp = None
        for q in range(4):
            cp = P.op("vector", lambda e, o=st.t[oi], a=acc[q], q=q: e.tensor_copy(out=o[0:1, q * 512:(q + 1) * 512], in_=a[0:1, :]),
                      waits=[mm] + fr3)
            accfree[q] = cp
        sd = P.dma("sync", ssq[0:1, tb * 2048:(tb + 1) * 2048], st.t[oi][:], st.ch[oi], [cp])
        st.free[oi] = [sd]
    P.run()
    return nc


def build_out():
    nc = _new_nc()
    KC = 64
    mixT = _din(nc, "mixT", [NT, 128, 64, TN], BF16)
    w = _din(nc, "w", [8192, 512], F32)
    hT = _din(nc, "hT", [512, S], F32)
    gain = _din(nc, "gain", [128, 4], F32)
    hout = _dout(nc, "hout", [512, S], F32)
    hg = _dout(nc, "hg", [512, S], BF16)
    ssq = _dout(nc, "ssq", [1, S], F32)
    P = Prog(nc)
    gsb = P.sb("gsb", [128, 4], F32)
    gb = Slots(P, "gb", 4, [128, TN], BF16, dma=True)
    wt = P.sb("wt", [128, KC, 512], BF16)
    ones = P.sb("ones", [128, 1], F32)
    G = 16
    xr = Slots(P, "xr", 4, [128, G, TN], BF16, dma=True)
    hb = Slots(P, "hb", 4, [128, TN], F32, dma=True)
    ob = Slots(P, "ob", 4, [128, TN], F32, dma=True)
    sq = Slots(P, "sq", 4, [128, TN], F32)
    st = Slots(P, "st", 2, [1, TN], F32, dma=True)
    acc = [P.ps(f"acc{i}", [128, 512], F32) for i in range(6)]
    sps = P.ps("sps", [128, 512], F32)
    accfree = [None] * 6
    wld = P.dchan("wld")
    cst = P.dchan("cst")
    ldg = P.dma("sync", gsb[:], gain, cst)
    wv = w.rearrange("(kc p) n -> p kc n", p=128)
    wtk = None
    for i in range(8):
        wtk = P.dma("gpsimd", wt[:, 8 * i:8 * i + 8, :], wv[:, 8 * i:8 * i + 8, :], wld)
    t_ones = P.op("vector", lambda e: e.memset(ones[:], 1.0))
    jobs = [(t, g) for t in range(NT) for g in range(KC // G)]
    loads = {}
    nl = 0

    def issue_loads(upto):
        nonlocal nl
        while nl < min(upto, len(jobs)):
            t, g = jobs[nl]
            xi, fr = xr.next()
            tk = P.dma("sync", xr.t[xi][:], mixT[t, :, g * G:(g + 1) * G, :], xr.ch[xi], fr)
            loads[(t, g)] = (xi, tk)
            nl += 1

    deferred = []
    spsfree = None
    for t in range(NT):
        banks = [(4 * t + ct) % 6 for ct in range(4)]
        lastmm = [None] * 4
        hl = []
        for ct in range(4):
            hi, fr = hb.next()
            hl.append((hi, P.dma("sync", hb.t[hi][:], hT[ct * 128:(ct + 1) * 128, t * TN:(t + 1) * TN], hb.ch[hi], fr)))
        for g in range(KC // G):
            issue_loads(t * (KC // G) + g + 3)
            xi, ltk = loads[(t, g)]
            mm = None
            for ct in range(4):
                for kc in range(G):
                    first = (g == 0 and kc == 0)
                    lastk = (g == KC // G - 1 and kc == G - 1)
                    mm = P.op("tensor", lambda e, o=acc[banks[ct]], k=g * G + kc, ct=ct, x=xr.t[xi], kc=kc, f=first, l=lastk:
                              e.matmul(o[:, :], lhsT=wt[:, k, ct * 128:(ct + 1) * 128], rhs=x[:, kc, :], start=f, stop=l),
                              waits=[ltk, wtk, accfree[banks[ct]] if first else None], sig=(kc == G - 1))
                if g == KC // G - 1:
                    lastmm[ct] = mm
            xr.free[xi] = [mm]
            if g == 1 and deferred:
                for fn in deferred:
                    fn()
                deferred = []
        sqs = []
        for ct in range(4):
            hi, hld = hl[ct]
            oi, fr = ob.next()
            add = P.op("vector", lambda e, o=ob.t[oi], a=acc[banks[ct]], h=hb.t[hi]: e.tensor_tensor(
                out=o[:], in0=a[:, :], in1=h[:], op=ALU.add), waits=[lastmm[ct], hld] + fr)
            accfree[banks[ct]] = add
            hb.free[hi] = [add]
            sd = P.dma("gpsimd", hout[ct * 128:(ct + 1) * 128, t * TN:(t + 1) * TN], ob.t[oi][:], ob.ch[oi], [add])
            si, fr2 = sq.next()
            s2 = P.op("scalar", lambda e, o=sq.t[si], a=ob.t[oi]: e.activation(out=o[:], in_=a[:], func=AF.Square),
                      waits=[add] + fr2)
            gi, fr5 = gb.next()
            tg = P.op("vector", lambda e, o=gb.t[gi], a=ob.t[oi], ct=ct: e.tensor_scalar(
                out=o[:], in0=a[:], scalar1=gsb[:, ct:ct + 1], scalar2=None, op0=ALU.mult), waits=[ldg] + fr5)
            sdg = P.dma("gpsimd", hg[ct * 128:(ct + 1) * 128, t * TN:(t + 1) * TN], gb.t[gi][:], gb.ch[gi], [tg])
            gb.free[gi] = [sdg]
            ob.free[oi] = [sd, s2, tg]
            sqs.append((si, s2))

        def pe_part(t=t, sqs=sqs):
            nonlocal spsfree
            mm = None
            for ct, (si, s2) in enumerate(sqs):
                mm = P.op("tensor", lambda e, a=sq.t[si], ct=ct: e.matmul(sps[0:1, :], lhsT=ones[:, 0:1], rhs=a[:],
                                                                          start=(ct == 0), stop=(ct == 3)),
                          waits=[s2, t_ones, spsfree if ct == 0 else None])
                sq.free[si] = [mm]
            oi, fr = st.next()
            cp = P.op("vector", lambda e, o=st.t[oi]: e.tensor_copy(out=o[0:1, :], in_=sps[0:1, :]), waits=[mm] + fr)
            spsfree = cp
            sd = P.dma("gpsimd", ssq[0:1, t * TN:(t + 1) * TN], st.t[oi][:], st.ch[oi], [cp])
            st.free[oi] = [sd]
        deferred.append(pe_part)
    for fn in deferred:
        fn()
    P.run()
    return nc


def emit_rstd_scratch(P, nc, ssq_all, acc_bank):
    rs = nc.dram_tensor("rs_scratch", [128, S], F32).ap()
    ones8 = P.sb("r_ones8", [8, 128], F32)
    s8 = Slots(P, "r_s8", 2, [8, TN], F32, dma=True)
    ro = Slots(P, "r_ro", 2, [128, TN], F32, dma=True)
    t_ones = P.op("vector", lambda e: e.memset(ones8[:], 1.0))
    bfree = None
    sds = []
    for t in range(NT):
        si, fr = s8.next()
        ld = P.dma("sync", s8.t[si][:], ssq_all[:, t * TN:(t + 1) * TN], s8.ch[si], fr)
        mm = P.op("tensor", lambda e, a=s8.t[si]: e.matmul(acc_bank[:, :], lhsT=ones8[0:8, :], rhs=a[0:8, :], start=True, stop=True),
                  waits=[ld, t_ones, bfree])
        s8.free[si] = [mm]
        oi, fr2 = ro.next()
        a = P.op("vector", lambda e, o=ro.t[oi]: e.tensor_scalar(out=o[:], in0=acc_bank[:, :], scalar1=1.0 / D, scalar2=EPS,
                                                                op0=ALU.mult, op1=ALU.add), waits=[mm] + fr2)
        bfree = a
        s_ = P.op("scalar", lambda e, o=ro.t[oi]: e.activation(out=o[:], in_=o[:], func=AF.Sqrt), waits=[a])
        r_ = P.op("vector", lambda e, o=ro.t[oi]: e.reciprocal(out=o[:], in_=o[:]), waits=[s_])
        sd = P.dma("sync", rs[:, t * TN:(t + 1) * TN], ro.t[oi][:], ro.ch[oi], [r_])
        ro.free[oi] = [sd]
        sds.append(sd)
    return rs, sds, bfree


class InProj:
    def __init__(self, P, xT, w, groups, wcols, nbanks_acc, acc, nwbuf=1):
        self.P = P
        self.KC = 32
        self.G = 8
        self.xv = xT
        self.wv = w.rearrange("(kc p) n -> p kc n", p=128)
        self.nwbuf = nwbuf
        self.wts = [P.sb(f"ip_wt{i}", [128, self.KC, wcols], BF16) for i in range(nwbuf)]
        self.wlds = [P.dchan(f"ip_wld{i}") for i in range(nwbuf)]
        self.wbuf_free = [None] * nwbuf
        self.wtk = {}
        self.xr = Slots(P, "ip_xr", 4, [128, self.G, TN], BF16, dma=True)
        self.groups = groups
        self.jobs = [(gi, t, g) for gi in range(len(groups)) for t in range(NT) for g in range(self.KC // self.G)]
        self.loads = {}
        self.nl = 0
        self.acc = acc
        self.accfree = [None] * len(acc)
        self.bk = -1
        self.wfree = None
        self.lastmm_all = None

    def issue_loads(self, upto):
        P = self.P
        while self.nl < min(upto, len(self.jobs)):
            gi, t, g = self.jobs[self.nl]
            xi, fr = self.xr.next()
            tk = P.dma("sync", self.xr.t[xi][:], self.xv[t, :, g * self.G:(g + 1) * self.G, :], self.xr.ch[xi], fr)
            self.loads[(gi, t, g)] = (xi, tk)
            self.nl += 1

    def run_group(self, gi, epilogue, mid_hook=None):
        P = self.P
        col0, ncols = self.groups[gi]
        nct = ncols // 128
        if gi not in self.wtk:
            self.load_w(gi)
        if self.nwbuf == 2 and gi + 1 < len(self.groups):
            self.load_w(gi + 1)
        wtk = self.wtk[gi]
        wt = self.wts[gi % self.nwbuf]
        NG = self.KC // self.G
        base = gi * NT * NG
        for t in range(NT):
            banks = []
            for ct in range(nct):
                self.bk = (self.bk + 1) % len(self.acc)
                banks.append(self.bk)
            lastmm = [None] * nct
            for g in range(NG):
                self.issue_loads(base + t * NG + g + 3)
                xi, ltk = self.loads[(gi, t, g)]
                mm = None
                for ct in range(nct):
                    for kc in range(self.G):
                        first = (g == 0 and kc == 0)
                        lastk = (g == NG - 1 and kc == self.G - 1)
                        mm = P.op("tensor", lambda e, o=self.acc[banks[ct]], k=g * self.G + kc, ct=ct, x=self.xr.t[xi], kc=kc, f=first, l=lastk:
                                  e.matmul(o[:, :], lhsT=wt[:, k, ct * 128:(ct + 1) * 128], rhs=x[:, kc, :], start=f, stop=l),
                                  waits=[ltk, wtk, self.accfree[banks[ct]] if first else None], sig=(kc == self.G - 1))
                    if g == NG - 1:
                        lastmm[ct] = mm
                self.xr.free[xi] = [mm]
                if g == 1 and mid_hook is not None:
                    mid_hook()
            self.wfree = mm
            if t == NT - 1:
                self.wbuf_free[gi % self.nwbuf] = mm
                if self.nwbuf == 1 and gi + 1 < len(self.groups):
                    self.load_w(gi + 1)
            epilogue(t, banks, lastmm)

    def load_w(self, gi):
        P = self.P
        col0, ncols = self.groups[gi]
        b = gi % self.nwbuf
        tk = None
        for i in range(4):
            tk = P.dma("gpsimd", self.wts[b][:, 8 * i:8 * i + 8, 0:ncols], self.wv[:, 8 * i:8 * i + 8, col0:col0 + ncols],
                       self.wlds[b], [self.wbuf_free[b]])
        self.wtk[gi] = tk


PATTERNS = ((1, 16), (4, 4), (16, 1))


DEBUG_STOP = 0
DEBUG_G = (0, 1, 2)
DEBUG_NSB = 4
DEBUG_NOFIN = False
DEBUG_SB0 = 0
DEBUG_MIDFLUSH = False
DEBUG_PM_ENG = 'gpsimd'


def build_even():
    nc = _new_nc()
    hgT = _din(nc, "hgT", [NT, 128, 32, TN], BF16)
    ssq_all = _din(nc, "ssq_all", [8, S], F32)
    w = _din(nc, "w", [D, 3840], F32)
    gains = _din(nc, "gains", [128, 2], F32)
    btab = _din(nc, "btab", [128, 6 * 768], F32)
    mtab = _din(nc, "mtab", [128, 256], F32)
    identb = _din(nc, "identb", [128, 128], BF16)
    pw = _din(nc, "pw", [512, 256], F32)
    pvec = _din(nc, "pvec", [128, 8], F32)
    rctab = _din(nc, "rctab", [128, 1024], F32)
    mix = _dout(nc, "mix", [1024, S], BF16)
    P = Prog(nc)

    A = [P.ps(f"A{i}", [128, 512], F32) for i in range(4)]
    N = [P.ps(f"N{i}", [128, 512], F32) for i in range(2)]
    Tb = [P.ps(f"Tps{i}", [128, 1024], BF16) for i in range(2)]

    rs, rs_sds, nfree0 = emit_rstd_scratch(P, nc, ssq_all, N[0])

    cst = P.dchan("cst")
    gsb = P.sb("gsb", [128, 2], F32)
    msb = P.sb("msb", [128, 256], F32)
    idb = P.sb("idb", [128, 128], BF16)
    pvs = P.sb("pvs", [128, 8], F32)
    rcs = P.sb("rcs", [128, 1024], F32)
    pwb = P.sb("pwb", [128, 4, 256], BF16)
    P.dma("sync", gsb[:], gains, cst)
    P.dma("sync", msb[:], mtab, cst)
    P.dma("sync", idb[:], identb, cst)
    P.dma("sync", pvs[:], pvec, cst)
    ldc0 = P.dma("sync", rcs[:], rctab, cst)
    cst2 = P.dchan("cst2")
    ldc = P.dma("gpsimd", pwb[:], pw.rearrange("(kc p) n -> p kc n", p=128), cst2)
    onesq = P.sb("onesq", [128, 128], BF16)
    onesk = P.sb("onesk", [128, 128], BF16)
    epsb = P.sb("epsb", [128, 2], F32)
    P.op("vector", lambda e: e.memset(onesq[:], 1.0))
    P.op("vector", lambda e: e.memset(epsb[:, 0:1], 128.0 * EPS))
    P.op("vector", lambda e: e.memset(epsb[:, 1:2], EPS))
    t_ones = P.op("vector", lambda e: e.memset(onesk[:], 1.0 / 128.0), waits=[ldc0, ldc])

    qT = P.sb("qT", [128, S], BF16)
    kT = P.sb("kT", [128, S], BF16)
    vT = P.sb("vT", [128, S], BF16)
    sz = P.sb("sz", [128, S], BF16)
    accb = P.sb("accb", [128, 2, 2048], F32)
    Eh = [P.sb(f"Eh{i}", [128, 768], F32) for i in range(2)]
    ech = [P.dchan(f"ech{i}") for i in range(2)]
    rsr = Slots(P, "rsr", 2, [128, TN], F32, dma=True)
    f32t = Slots(P, "f32t", 4, [128, TN], F32)
    sqb = Slots(P, "sqb", 4, [128, TN], BF16)
    rqb = Slots(P, "rqb", 3, [128, TN], F32)
    esb = Slots(P, "esb", 3, [128, 256], F32)
    pTb = Slots(P, "pTb", 3, [128, 256], BF16)
    vbb = Slots(P, "vbb", 6, [128, 128], BF16)
    ost = Slots(P, "ost", 2, [128, 2048], BF16, dma=True)

    if DEBUG_STOP == 1:
        P.run()
        return nc
    groups = [(hh * 512, 512) for hh in range(6)] + [(3584, 256), (3072, 512)]
    ip = InProj(P, hgT, w, groups, 512, 5, A)
    nfree = [nfree0, None]
    rs_ready = rs_sds

    deferred = []
    state = {"last_dve": None, "last_act": None}

    def flush_deferred():
        nonlocal deferred
        d, deferred = deferred, []
        for fn in d:
            fn()

    def head_epilogue(t, banks, lastmm):
        hdw = list(state.get("hd", [])) if t == 0 else []
        ri, fr = rsr.next()
        rtk = P.dma("sync", rsr.t[ri][:], rs[:, t * TN:(t + 1) * TN], rsr.ch[ri], list(fr) + (rs_ready if t == 0 else []))
        rst = rsr.t[ri]
        sl = slice(t * TN, (t + 1) * TN)
        users = []
        for which in range(2):
            b = banks[which]
            fi, ffr = f32t.next()
            xs = P.op("vector", lambda e, o=f32t.t[fi], a=A[b]: e.tensor_tensor(out=o[:], in0=a[:, :], in1=rst[:], op=ALU.mult),
                      waits=[lastmm[which], rtk] + ffr + hdw)
            ip.accfree[b] = xs
            users.append(xs)
            si, sfr = sqb.next()
            sq_ = P.op("scalar", lambda e, o=sqb.t[si], a=f32t.t[fi]: e.activation(out=o[:], in_=a[:], func=AF.Square),
                       waits=[xs] + sfr)

            def post(which=which, fi=fi, si=si, sq_=sq_, sl=sl):
                nb = which
                mm = P.op("tensor", lambda e: e.matmul(N[nb][:, :], lhsT=(onesq if which == 0 else onesk)[:, :], rhs=sqb.t[si][:],
                                                       start=True, stop=True), waits=[sq_, t_ones, nfree[nb]])
                sqb.free[si] = [mm]
                qi, qfr = rqb.next()
                rq0 = P.op("scalar", lambda e, o=rqb.t[qi]: e.activation(
                    out=o[:], in_=N[nb][:, :], func=AF.Sqrt, bias=epsb[:, which:which + 1]), waits=[mm, t_ones] + qfr)
                nfree[nb] = rq0
                rq = P.op("vector", lambda e, o=rqb.t[qi]: e.reciprocal(out=o[:], in_=o[:]), waits=[rq0])
                dst = qT if which == 0 else kT
                qn = P.op("vector", lambda e, a=f32t.t[fi], r=rqb.t[qi]: e.scalar_tensor_tensor(
                    out=dst[:, sl], in0=a[:], scalar=gsb[:, which:which + 1], in1=r[:], op0=ALU.mult, op1=ALU.mult))
                f32t.free[fi] = [qn]
                rqb.free[qi] = [qn]
                state["last_dve"] = qn
            deferred.append(post)
        b = banks[2]
        vv = P.op("vector", lambda e, a=A[b]: e.tensor_tensor(out=vT[:, sl], in0=a[:, :], in1=rst[:], op=ALU.mult),
                  waits=[lastmm[2], rtk])
        ip.accfree[b] = vv
        b = banks[3]
        fi, ffr = f32t.next()
        zs = P.op("vector", lambda e, o=f32t.t[fi], a=A[b]: e.tensor_tensor(out=o[:], in0=a[:, :], in1=rst[:], op=ALU.mult),
                  waits=[lastmm[3], rtk] + ffr)
        ip.accfree[b] = zs
        zz = P.op("scalar", lambda e, a=f32t.t[fi]: e.activation(out=sz[:, sl], in_=a[:], func=AF.Silu), waits=[zs] + hdw)
        f32t.free[fi] = [zz]
        rsr.free[ri] = [zs]
        state["last_dve"] = zs
        state["last_act"] = zz

    def blk(ap_t, d, r, cb):
        return ap_t[:].rearrange("p (n d) -> p d n", d=d)[:, r, cb * 128:(cb + 1) * 128]

    tps_free = [None] * 2
    tps_i = [0]

    def attention(hh, ready):
        eb = Eh[hh % 2]
        ld = P.dma("sync", eb[:], btab[:, hh * 768:(hh + 1) * 768], ech[hh % 2], state.get("efree%d" % (hh % 2), []))
        last = None
        for g in range(3):
            last = P.op("vector", lambda e, g=g: e.tensor_tensor(out=eb[:, g * 256:(g + 1) * 256], in0=eb[:, g * 256:(g + 1) * 256],
                                                               in1=msb[:], op=ALU.add), waits=[ld, ldc, ldc0])
        e_ready = P.op("scalar", lambda e: e.activation(out=eb[:], in_=eb[:], func=AF.Exp), waits=[last])
        Sps = [A[0], A[1]]
        Ops = [A[2], A[3]]
        sfree = [ip.accfree[0], ip.accfree[1]]
        ofree = [ip.accfree[2], ip.accfree[3]]
        vcache = {}
        ucount = [0]
        pend = [None]
        last_pool = [None]

        def get_v(g, d, r, cb):
            key = (g, r, cb)
            if key in vcache:
                return vcache[key]
            ti = tps_i[0] % 2
            tps_i[0] += 1
            tr = P.op("tensor", lambda e: e.transpose(Tb[ti][:, 0:128], blk(vT, d, r, cb), idb[:]),
                      waits=list(ready) + [tps_free[ti], ldc, ldc0])
            vi, vfr = vbb.next()
            for k_ in [k_ for k_, v_ in vcache.items() if v_[0] == vi]:
                del vcache[k_]
            ev = P.op("scalar", lambda e, o=vbb.t[vi]: e.activation(out=o[:], in_=Tb[ti][:, 0:128], func=AF.Copy),
                      waits=[tr] + vfr)
            tps_free[ti] = ev
            vcache[key] = (vi, ev)
            return vcache[key]

        def stage34(u):
            (g, d, r, cl, cb, has_prev, pi, ptk, vprev, vcur, SBi) = u
            o = ucount[0] % 2
            ucount[0] += 1
            lo = 0 if has_prev else 128
            waits = [ptk, ofree[o], t_ones]
            mm = None
            if has_prev:
                mm = P.op("tensor", lambda e: e.matmul(Ops[o][:, 0:128], lhsT=vbb.t[vprev[0]][:], rhs=pTb.t[pi][:, 0:128],
                                                       start=True, stop=False), waits=waits + [vprev[1]], sig=False)
            mm = P.op("tensor", lambda e: e.matmul(Ops[o][:, 0:128], lhsT=vbb.t[vcur[0]][:], rhs=pTb.t[pi][:, 128:256],
                                                   start=(not has_prev), stop=True), waits=waits + [vcur[1]], sig=False)
            if has_prev:
                mm = P.op("tensor", lambda e: e.matmul(Ops[o][:, 128:256], lhsT=onesq[:, :], rhs=pTb.t[pi][:, 0:128],
                                                       start=True, stop=False), sig=False)
            mm = P.op("tensor", lambda e: e.matmul(Ops[o][:, 128:256], lhsT=onesq[:, :], rhs=pTb.t[pi][:, 128:256],
                                                   start=(not has_prev), stop=True))
            pTb.free[pi] = [mm]
            if has_prev:
                vbb.free[vprev[0]].append(mm)
            vbb.free[vcur[0]].append(mm)
            dst = accb[:].rearrange("p a (n d) -> p a d n", d=d)[:, :, r, cl * 128:(cl + 1) * 128]
            src = Ops[o][:, 0:256].rearrange("p (a n) -> p a n", a=2)
            if g == 0:
                ac = P.op("vector", lambda e: e.tensor_copy(out=dst, in_=src), waits=[mm])
            else:
                ac = P.op("vector", lambda e: e.tensor_tensor(out=dst, in0=dst, in1=src, op=ALU.add), waits=[mm])
            ofree[o] = ac
            state["last_dve"] = ac

        def finalize(SBi):
            tsl = slice(SBi * 2048, (SBi + 1) * 2048)
            P.op("vector", lambda e: e.reciprocal(out=accb[:, 1, :], in_=accb[:, 1, :]), sig=False)
            P.op("vector", lambda e: e.tensor_tensor(out=accb[:, 0, :], in0=accb[:, 0, :], in1=accb[:, 1, :], op=ALU.mult), sig=False)
            oi, ofr = ost.next()
            fin = P.op("vector", lambda e, o=ost.t[oi], tsl=tsl: e.tensor_tensor(out=o[:], in0=accb[:, 0, :], in1=sz[:, tsl], op=ALU.mult),
                       waits=ofr)
            sd = P.dma("sync", mix[hh * 128:(hh + 1) * 128, tsl], ost.t[oi][:], ost.ch[oi], [fin])
            ost.free[oi] = [sd]
            state["last_dve"] = fin

        for SBi in range(DEBUG_SB0, DEBUG_NSB):
            for g, (d, nb) in enumerate(PATTERNS):
                if g not in DEBUG_G:
                    continue
                for r in range(d):
                    for cl in range(nb):
                        cb = SBi * nb + cl
                        has_prev = cb >= 1
                        s_ = (ucount[0] + (1 if pend[0] is not None else 0)) % 2
                        lo = 0 if has_prev else 128
                        mm = None
                        if has_prev:
                            mm = P.op("tensor", lambda e, d=d, r=r, cb=cb, s_=s_: e.matmul(
                                Sps[s_][:, 0:128], lhsT=blk(kT, d, r, cb - 1), rhs=blk(qT, d, r, cb), start=True, stop=True),
                                waits=list(ready) + [sfree[s_]], sig=False)
                        mm = P.op("tensor", lambda e, d=d, r=r, cb=cb, s_=s_: e.matmul(
                            Sps[s_][:, 128:256], lhsT=blk(kT, d, r, cb), rhs=blk(qT, d, r, cb), start=True, stop=True),
                            waits=list(ready) + [sfree[s_]])
                        vprev = get_v(g, d, r, cb - 1) if has_prev else None
                        vcur = get_v(g, d, r, cb)
                        ei, efr = esb.next()
                        ex = P.op("scalar", lambda e, s_=s_, ei=ei, lo=lo: e.activation(out=esb.t[ei][:, lo:256], in_=Sps[s_][:, lo:256], func=AF.Exp),
                                  waits=[mm] + efr)
                        sfree[s_] = ex
                        pi, pfr = pTb.next()
                        pm = P.op(DEBUG_PM_ENG, lambda e, ei=ei, pi=pi, lo=lo, g=g: e.tensor_tensor(
                            out=pTb.t[pi][:, lo:256], in0=esb.t[ei][:, lo:256], in1=eb[:, g * 256 + lo:(g + 1) * 256], op=ALU.mult),
                            waits=[ex, e_ready] + pfr)
                        esb.free[ei] = [pm]
                        last_pool[0] = pm
                        u = (g, d, r, cl, cb, has_prev, pi, pm, vprev, vcur, SBi)
                        if pend[0] is not None:
                            stage34(pend[0])
                            if pend[0][-1] != SBi:
                                finalize(pend[0][-1])
                        pend[0] = u
            if DEBUG_MIDFLUSH:
                stage34(pend[0])
                finalize(pend[0][-1])
                pend[0] = None
        if pend[0] is not None:
            stage34(pend[0])
            finalize(pend[0][-1])
        pend[0] = None
        state["efree%d" % (hh % 2)] = [last_pool[0]]
        ip.accfree[0] = sfree[0]
        ip.accfree[1] = sfree[1]
        ip.accfree[2] = ofree[0]
        ip.accfree[3] = ofree[1]
        return [state["last_dve"], mm]

    head_done = []
    for hh in range(6):
        state["hd"] = head_done
        ip.run_group(hh, head_epilogue, mid_hook=flush_deferred)
        flush_deferred()
        if DEBUG_STOP == 2 + 10 * hh:
            P.run()
            return nc
        ready = [state["last_dve"], state["last_act"]]
        head_done = attention(hh, ready)
        if DEBUG_STOP == 3 + 10 * hh:
            P.run()
            return nc

    szp = [qT, kT]

    def zp_epilogue(t, banks, lastmm):
        ri, fr = rsr.next()
        rtk = P.dma("sync", rsr.t[ri][:], rs[:, t * TN:(t + 1) * TN], rsr.ch[ri], fr)
        rst = rsr.t[ri]
        sl = slice(t * TN, (t + 1) * TN)
        zs = None
        for oc in range(2):
            b = banks[oc]
            fi, ffr = f32t.next()
            zs = P.op("vector", lambda e, o=f32t.t[fi], a=A[b]: e.tensor_tensor(out=o[:], in0=a[:, :], in1=rst[:], op=ALU.mult),
                      waits=[lastmm[oc], rtk] + ffr + (head_done if t == 0 else []))
            ip.accfree[b] = zs
            zz = P.op("scalar", lambda e, a=f32t.t[fi], oc=oc: e.activation(out=szp[oc][:, sl], in_=a[:], func=AF.Silu),
                      waits=[zs] + (head_done if t == 0 else []))
            f32t.free[fi] = [zz]
            state["last_act"] = zz
        rsr.free[ri] = [zs]

    ip.run_group(6, zp_epilogue)
    if DEBUG_STOP == 70:
        P.run()
        return nc

    W_ = 16 + TN
    flat = accb[:].rearrange("p a n -> p (a n)")
    ubv = [flat[:, c * W_:(c + 1) * W_] for c in range(4)]
    tA = flat[:, 4 * W_:5 * W_]
    tB = flat[:, 5 * W_:6 * W_]
    pooled = Slots(P, "pooled", 2, None, BF16, views=[sz[:, i * 2048:(i + 1) * 2048].rearrange("p (c n) -> p c n", c=4) for i in range(2)])
    pout = Slots(P, "pout", 3, None, BF16, dma=True, views=[sz[:, 4096 + i * TN:4096 + (i + 1) * TN] for i in range(3)])
    pdefer = []

    def u_epilogue(t, banks, lastmm):
        ri, fr = rsr.next()
        rtk = P.dma("sync", rsr.t[ri][:], rs[:, t * TN:(t + 1) * TN], rsr.ch[ri], fr)
        rst = rsr.t[ri]
        sl = slice(t * TN, (t + 1) * TN)
        pi, pfr = pooled.next()
        last = None
        for c in range(4):
            b = banks[c]
            u_ = ubv[c]
            if t == 0:
                P.op("vector", lambda e, u_=u_: e.memset(u_[:, 0:16], 0.0), sig=False, waits=head_done)
            else:
                P.op("vector", lambda e, u_=u_: e.tensor_copy(out=u_[:, 0:16], in_=u_[:, TN:TN + 16]), sig=False)
            ev = P.op("vector", lambda e, u_=u_, a=A[b]: e.tensor_tensor(out=u_[:, 16:16 + TN], in0=a[:, :], in1=rst[:], op=ALU.mult),
                      waits=[lastmm[c], rtk])
            ip.accfree[b] = ev
            P.op("vector", lambda e, u_=u_: e.scalar_tensor_tensor(out=tA[:, 2:W_], in0=u_[:, 1:W_ - 1], scalar=pvs[:, 0:1], in1=u_[:, 2:W_],
                                                                  op0=ALU.mult, op1=ALU.add), sig=False)
            P.op("vector", lambda e: e.scalar_tensor_tensor(out=tB[:, 4:W_], in0=tA[:, 2:W_ - 2], scalar=pvs[:, 1:2], in1=tA[:, 4:W_],
                                                           op0=ALU.mult, op1=ALU.add), sig=False)
            P.op("vector", lambda e: e.scalar_tensor_tensor(out=tA[:, 8:W_], in0=tB[:, 4:W_ - 4], scalar=pvs[:, 2:3], in1=tB[:, 8:W_],
                                                           op0=ALU.mult, op1=ALU.add), sig=False)
            P.op("vector", lambda e: e.scalar_tensor_tensor(out=tB[:, 16:W_], in0=tA[:, 8:W_ - 8], scalar=pvs[:, 3:4], in1=tA[:, 16:W_],
                                                           op0=ALU.mult, op1=ALU.add), sig=False)
            rc = rcs[:, 0:TN] if t == 0 else rcs[:, TN:2 * TN]
            P.op("vector", lambda e, rc=rc: e.tensor_tensor(out=tA[:, 16:W_], in0=tB[:, 16:W_], in1=rc, op=ALU.mult), sig=False)
            last = P.op("vector", lambda e, u_=u_, c=c, pi=pi: e.tensor_tensor(out=pooled.t[pi][:, c, :], in0=tA[:, 16:W_], in1=u_[:, 16:W_],
                                                                             op=ALU.subtract), waits=(pfr + head_done) if c == 0 else [])
        rsr.free[ri] = [last]

        def post(pi=pi, last=last, sl=sl, t=t):
            for oc in range(2):
                mm = None
                for kc in range(4):
                    mm = P.op("tensor", lambda e, oc=oc, kc=kc: e.matmul(N[oc][:, :], lhsT=pwb[:, kc, oc * 128:(oc + 1) * 128],
                                                                          rhs=pooled.t[pi][:, kc, :], start=(kc == 0), stop=(kc == 3)),
                              waits=[last, ldc, ldc0, nfree[oc]], sig=(kc == 3))
                oi, ofr = pout.next()
                fo = P.op("vector", lambda e, oc=oc, o=pout.t[oi]: e.scalar_tensor_tensor(
                    out=o[:], in0=N[oc][:, :], scalar=pvs[:, 4 + oc:5 + oc], in1=szp[oc][:, sl], op0=ALU.mult, op1=ALU.mult),
                    waits=[mm, state["last_act"]] + ofr)
                nfree[oc] = fo
                sd = P.dma("sync", mix[768 + oc * 128:768 + (oc + 1) * 128, sl], pout.t[oi][:], pout.ch[oi], [fo])
                pout.free[oi] = [sd]
            pooled.free[pi] = [mm]
        pdefer.append(post)

    def flush_p():
        nonlocal pdefer
        d, pdefer = pdefer, []
        for fn in d:
            fn()

    ip.run_group(7, u_epilogue, mid_hook=flush_p)
    flush_p()
    P.run()
    return nc


def build_odd1():
    nc = _new_nc()
    hgT = _din(nc, "hgT", [NT, 128, 32, TN], BF16)
    ssq_all = _din(nc, "ssq_all", [8, S], F32)
    w = _din(nc, "w", [D, 3072], F32)
    cvec = _din(nc, "cvec", [128, 8, 8], F32)
    bd = _din(nc, "bd", [128, 3, 8, 128], F32)
    wif = _din(nc, "wif", [128, 3, 8, 16], F32)
    qT_o = _dout(nc, "qT", [1024, S], BF16)
    kT_o = _dout(nc, "kT", [1024, S], BF16)
    vT_o = _dout(nc, "vT", [1024, S], BF16)
    A_o = _dout(nc, "Ao", [1024, S], BF16)
    B_o = _dout(nc, "Bo", [1024, S], BF16)
    gp_o = _dout(nc, "gpart", [16, S], F32)
    P = Prog(nc)
    A = [P.ps(f"A{i}", [128, 512], F32) for i in range(4)]
    Q = [P.ps(f"Q{i}", [128, 512], F32) for i in range(3)]
    Gp = P.ps("Gp", [128, 512], F32)
    rs, rs_sds, qfree0 = emit_rstd_scratch(P, nc, ssq_all, Q[0])
    cst = P.dchan("cst")
    cst2 = P.dchan("cst2")
    cv = P.sb("cv", [128, 8, 8], F32)
    bdb = P.sb("bdb", [128, 3, 8, 128], BF16)
    wifb = P.sb("wifb", [128, 3, 8, 16], BF16)
    ldc0 = P.dma("sync", cv[:], cvec, cst)
    P.dma("gpsimd", bdb[:], bd, cst2)
    ldc = P.dma("gpsimd", wifb[:], wif, cst2)
    gacc = P.sb("gacc", [16, S], F32)
    t_c = P.op("vector", lambda e: e.memset(gacc[:, 0:1], 0.0), waits=[ldc0, ldc])
    rsr = Slots(P, "rsr", 2, [128, TN], F32, dma=True)
    xmb = Slots(P, "xmb", 2, [128, TN + 3], F32)
    f32a = Slots(P, "f32a", 8, [128, TN], F32)
    bfa = Slots(P, "bfa", 6, [128, TN], BF16)
    outb = Slots(P, "outb", 8, [128, TN], BF16, dma=True)
    groups = [(j * 384, 384) for j in range(8)]
    ip = InProj(P, hgT, w, groups, 384, 4, A, nwbuf=2)
    qfree = [qfree0, None, None]
    gfree = [None]
    deferred = []
    prev_xm = [None]

    def flush():
        nonlocal deferred
        d, deferred = deferred, []
        for fn in d:
            fn()

    def make_epilogue(j):
        def epilogue(t, banks, lastmm):
            ri, fr = rsr.next()
            rtk = P.dma("sync", rsr.t[ri][:], rs[:, t * TN:(t + 1) * TN], rsr.ch[ri], list(fr) + (rs_sds if (j == 0 and t == 0) else []))
            rst = rsr.t[ri]
            sl = slice(t * TN, (t + 1) * TN)
            xi, xfr = xmb.next()
            xm_ = xmb.t[xi]
            if t == 0:
                P.op("vector", lambda e: e.memset(xm_[:, 0:3], 0.0), waits=xfr)
            else:
                pxm = prev_xm[0]
                P.op("vector", lambda e: e.tensor_copy(out=xm_[:, 0:3], in_=pxm[:, TN:TN + 3]), waits=xfr)
            ev = P.op("vector", lambda e: e.tensor_tensor(out=xm_[:, 3:3 + TN], in0=A[banks[0]][:, :], in1=rst[:], op=ALU.mult),
                      waits=[lastmm[0], rtk, t_c])
            ip.accfree[banks[0]] = ev
            prev_xm[0] = xm_
            bi, bfr = bfa.next()
            xm_bf = bfa.t[bi]
            xmc = P.op("scalar", lambda e: e.activation(out=xm_bf[:], in_=xm_[:, 3:3 + TN], func=AF.Copy), waits=[ev] + bfr)
            ci, cfr = f32a.next()
            pre = f32a.t[ci]
            P.op("vector", lambda e: e.tensor_scalar(out=pre[:], in0=xm_[:, 0:TN], scalar1=cv[:, j, 0:1], scalar2=None, op0=ALU.mult), waits=cfr)
            for k_ in (1, 2, 3):
                cvl = P.op("vector", lambda e, k_=k_: e.scalar_tensor_tensor(out=pre[:], in0=xm_[:, k_:k_ + TN], scalar=cv[:, j, k_:k_ + 1], in1=pre[:],
                                                                             op0=ALU.mult, op1=ALU.add))
            xmb.free[xi] = [cvl, xmc]
            si, sfr = f32a.next()
            sg = f32a.t[si]
            sgt = P.op("scalar", lambda e: e.activation(out=sg[:], in_=pre[:], func=AF.Sigmoid, bias=cv[:, j, 4:5]), waits=[cvl] + sfr)
            xci, xcfr = f32a.next()
            xc = f32a.t[xci]
            xct = P.op("vector", lambda e: e.scalar_tensor_tensor(out=xc[:], in0=pre[:], scalar=cv[:, j, 4:5], in1=sg[:], op0=ALU.add, op1=ALU.mult),
                       waits=[sgt] + xcfr)
            f32a.free[ci] = [xct]
            f32a.free[si] = [xct]
            bi2, bfr2 = bfa.next()
            xc_bf = bfa.t[bi2]
            xcc = P.op("scalar", lambda e: e.activation(out=xc_bf[:], in_=xc[:], func=AF.Copy), waits=[xct] + bfr2)
            zi, zfr = f32a.next()
            zs = f32a.t[zi]
            zt = P.op("vector", lambda e: e.tensor_tensor(out=zs[:], in0=A[banks[1]][:, :], in1=rst[:], op=ALU.mult), waits=[lastmm[1], rtk] + zfr)
            ip.accfree[banks[1]] = zt
            gi, gfr = f32a.next()
            sgz = f32a.t[gi]
            sgzt = P.op("scalar", lambda e: e.activation(out=sgz[:], in_=zs[:], func=AF.Sigmoid), waits=[zt] + gfr)
            szt = P.op("vector", lambda e: e.tensor_tensor(out=zs[:], in0=zs[:], in1=sgz[:], op=ALU.mult), waits=[sgzt])
            f32a.free[gi] = [szt]
            oi, ofr = f32a.next()
            os_ = f32a.t[oi]
            ot = P.op("vector", lambda e: e.tensor_tensor(out=os_[:], in0=A[banks[2]][:, :], in1=rst[:], op=ALU.mult), waits=[lastmm[2], rtk] + ofr)
            ip.accfree[banks[2]] = ot
            rsr.free[ri] = [ot]
            ogt = P.op("scalar", lambda e: e.activation(out=os_[:], in_=os_[:], func=AF.Sigmoid), waits=[ot])
            ai, afr = outb.next()
            at = P.op("vector", lambda e: e.scalar_tensor_tensor(out=outb.t[ai][:], in0=os_[:], scalar=cv[:, j, 5:6], in1=zs[:], op0=ALU.mult, op1=ALU.mult),
                      waits=[ogt, szt] + afr)
            outb.free[ai] = [P.dma("sync", A_o[j * 128:(j + 1) * 128, sl], outb.t[ai][:], outb.ch[ai], [at])]
            f32a.free[oi] = [at]
            bi3, bfr3 = outb.next()
            bt = P.op("vector", lambda e: e.scalar_tensor_tensor(out=outb.t[bi3][:], in0=xc[:], scalar=cv[:, j, 6:7], in1=zs[:], op0=ALU.mult, op1=ALU.mult),
                      waits=bfr3)
            outb.free[bi3] = [P.dma("sync", B_o[j * 128:(j + 1) * 128, sl], outb.t[bi3][:], outb.ch[bi3], [bt])]
            f32a.free[xci] = [bt, xcc]
            f32a.free[zi] = [bt]

            def post():
                srcs = [(0, xc_bf, xcc, qT_o), (1, xc_bf, xcc, kT_o), (2, xm_bf, xmc, vT_o)]
                evs = []
                lastm = [None, None, None]
                for which, src, stk, dst in srcs:
                    mm = P.op("tensor", lambda e, which=which, src=src: e.matmul(Q[which][:, :], lhsT=bdb[:, which, j, :], rhs=src[:],
                                                                                 start=True, stop=True), waits=[stk, ldc, qfree[which]])
                    lastm[which] = mm
                    oi2, ofr2 = outb.next()
                    ev2 = P.op("scalar", lambda e, which=which, oi2=oi2: e.activation(out=outb.t[oi2][:], in_=Q[which][:, :], func=AF.Copy),
                               waits=[mm] + ofr2)
                    qfree[which] = ev2
                    sdq = P.dma("sync", dst[j * 128:(j + 1) * 128, sl], outb.t[oi2][:], outb.ch[oi2], [ev2])
                    evs.append((oi2, ev2, sdq))
                bfa.free[bi] = [lastm[2]]
                bfa.free[bi2] = [lastm[0], lastm[1]]
                gm = None
                for which, (oi2, ev2, sdq) in enumerate(evs):
                    gm = P.op("tensor", lambda e, which=which, oi2=oi2: e.matmul(Gp[0:16, :], lhsT=wifb[:, which, j, :], rhs=outb.t[oi2][:],
                                                                                 start=(which == 0), stop=(which == 2)),
                              waits=[ev2, ldc, gfree[0] if which == 0 else None])
                for (oi2, ev2, sdq) in evs:
                    outb.free[oi2] = [sdq, gm]
                if j == 0:
                    ga = P.op("vector", lambda e: e.tensor_copy(out=gacc[:, sl], in_=Gp[0:16, :]), waits=[gm])
                else:
                    ga = P.op("vector", lambda e: e.tensor_tensor(out=gacc[:, sl], in0=gacc[:, sl], in1=Gp[0:16, :], op=ALU.add), waits=[gm])
                gfree[0] = ga
            deferred.append(post)
        return epilogue

    for j in range(8):
        ip.run_group(j, make_epilogue(j), mid_hook=flush)
    flush()
    gch = P.dchan("gch")
    P.dma("sync", gp_o, gacc[:], gch, [gfree[0]])
    P.run()
    return nc


LN32 = 3.4657359027997265


def build_odd2():
    nc = _new_nc()
    qT = _din(nc, "qT", [1024, S], BF16)
    kT = _din(nc, "kT", [1024, S], BF16)
    ktm = _din(nc, "ktm", [S, 1024], BF16)
    vtm = _din(nc, "vtm", [S, 1024], BF16)
    Ai = _din(nc, "Ao", [1024, S], BF16)
    Bi = _din(nc, "Bo", [1024, S], BF16)
    gp_all = _din(nc, "gp_all", [128, S], F32)
    sel = _din(nc, "sel", [128, 2], F32)
    bsel = _din(nc, "bsel", [128, 2], F32)
    tri_i = _din(nc, "tri", [128, 128], F32)
    idf_i = _din(nc, "identf", [128, 128], F32)
    mix = _dout(nc, "mix", [1024, S], BF16)
    P = Prog(nc)
    Sps = P.ps("Sps", [128, 512], F32)
    NUM = [P.ps(f"NUM{i}", [128, 512], F32) for i in range(2)]
    SM = P.ps("SM", [128, 512], F32)
    DC = [P.ps(f"DC{i}", [128, 512], F32) for i in range(2)]
    TP = [P.ps(f"TP{i}", [128, 512], F32) for i in range(2)]
    NC_ = 64
    cst = P.dchan("cst")
    C = P.sb("C", [128, 8200], F32)
    Cv = C[:, 0:8200].rearrange("p (d n) -> p d n", d=8)
    Cbf = P.sb("Cbf", [128, 8200], BF16)
    Cbv = Cbf[:, 0:8200].rearrange("p (d n) -> p d n", d=8)
    sel_s = P.sb("sel_s", [128, 2], F32)
    bs_s = P.sb("bs_s", [128, 2], F32)
    tri = P.sb("tri_s", [128, 128], F32)
    idf = P.sb("idf", [128, 128], F32)
    P.dma("sync", C[:, 0:S], gp_all, cst)
    P.dma("sync", sel_s[:], sel, cst)
    P.dma("sync", bs_s[:], bsel, cst)
    P.dma("sync", tri[:], tri_i, cst)
    ldc = P.dma("sync", idf[:], idf_i, cst)
    ones_r = P.sb("ones_r", [1, 128], F32)
    ones_b = P.sb("ones_b", [128, 1], BF16)
    epsb = P.sb("epsb", [128, 1], F32)
    sm = {n: P.sb("g_" + n, [128, 64], F32) for n in ("ipre", "fpre", "lf", "b", "a", "w", "g", "fl", "Mb", "mhb", "t1")}
    rowA = P.sb("rowA", [1, 64], F32)
    rowB = P.sb("rowB", [1, 64], F32)
    rowM = P.sb("rowM", [1, 192], F32)
    amT = P.sb("amT", [64, 1], F32)
    P.op("vector", lambda e: e.memset(ones_r[:], 1.0))
    P.op("vector", lambda e: e.memset(ones_b[:], 1.0))
    P.op("vector", lambda e: e.memset(epsb[:], EPS))
    t0 = P.op("vector", lambda e: e.memset(rowM[:], -80.0), waits=[ldc])
    mm = None
    for cc in range(NC_):
        mm = P.op("tensor", lambda e, cc=cc: e.matmul(SM[:, 2 * cc:2 * cc + 2], lhsT=C[:, cc * 128:(cc + 1) * 128], rhs=sel_s[:, 0:2],
                                                      start=True, stop=True), waits=[ldc], sig=(cc == NC_ - 1))
    SMv = SM[:, 0:128].rearrange("p (c two) -> p c two", two=2)
    P.op("vector", lambda e: e.tensor_scalar(out=sm["ipre"][:], in0=SMv[:, :, 0], scalar1=bs_s[:, 0:1], scalar2=None, op0=ALU.add), waits=[mm, t0])
    f1 = P.op("vector", lambda e: e.tensor_scalar(out=sm["fpre"][:], in0=SMv[:, :, 1], scalar1=bs_s[:, 1:2], scalar2=None, op0=ALU.add))
    a1 = P.op("scalar", lambda e: e.activation(out=sm["t1"][:], in_=sm["fpre"][:], func=AF.Exp, scale=-1.0), waits=[f1])
    a2 = P.op("scalar", lambda e: e.activation(out=sm["t1"][:], in_=sm["t1"][:], func=AF.Ln, bias=1.0))
    d1 = P.op("vector", lambda e: e.tensor_scalar(out=sm["lf"][:], in0=sm["t1"][:], scalar1=-1.0, scalar2=None, op0=ALU.mult), waits=[a2])
    m1 = P.op("tensor", lambda e: e.matmul(SM[:, 128:192], lhsT=tri[:, :], rhs=sm["lf"][:], start=True, stop=True), waits=[d1, f1])
    d2 = P.op("vector", lambda e: e.tensor_copy(out=sm["b"][:], in_=SM[:, 128:192]), waits=[m1])
    d3 = P.op("vector", lambda e: e.tensor_tensor(out=sm["a"][:], in0=sm["ipre"][:], in1=sm["b"][:], op=ALU.subtract))
    m2 = P.op("tensor", lambda e: e.transpose(SM[0:64, 192:320], sm["a"][:, 0:64], idf[:, :]), waits=[d3])
    d4 = P.op("vector", lambda e: e.tensor_reduce(out=amT[:], in_=SM[0:64, 192:320], axis=AX.X, op=ALU.max), waits=[m2])
    m3 = P.op("tensor", lambda e: e.transpose(SM[0:1, 320:384], amT[0:64, 0:1], idf[0:64, 0:64]), waits=[d4])
    m4 = P.op("tensor", lambda e: e.matmul(SM[0:1, 384:448], lhsT=idf[:, 127:128], rhs=sm["b"][:], start=True, stop=True), waits=[d2])
    P.op("vector", lambda e: e.tensor_copy(out=rowA[:], in_=SM[0:1, 320:384]), waits=[m3, m4])
    P.op("vector", lambda e: e.tensor_copy(out=rowB[:], in_=SM[0:1, 384:448]))
    for cc in range(NC_):
        P.op("vector", lambda e, cc=cc: e.tensor_tensor(out=rowM[0:1, cc:cc + 1], in0=rowM[0:1, 64 + cc:65 + cc], in1=rowA[0:1, cc:cc + 1], op=ALU.max))
        d5 = P.op("vector", lambda e, cc=cc: e.tensor_tensor(out=rowM[0:1, 65 + cc:66 + cc], in0=rowB[0:1, cc:cc + 1], in1=rowM[0:1, cc:cc + 1], op=ALU.add))
    m5 = P.op("tensor", lambda e: e.matmul(SM[:, 0:128], lhsT=ones_r[0:1, :], rhs=rowM[0:1, 0:128], start=True, stop=True), waits=[d5])
    P.op("vector", lambda e: e.tensor_copy(out=sm["Mb"][:], in_=SM[:, 0:64]), waits=[m5])
    d6 = P.op("vector", lambda e: e.tensor_copy(out=sm["mhb"][:], in_=SM[:, 64:128]))
    P.op("vector", lambda e: e.tensor_tensor(out=sm["w"][:], in0=sm["a"][:], in1=sm["Mb"][:], op=ALU.subtract))
    P.op("vector", lambda e: e.tensor_scalar(out=sm["w"][:], in0=sm["w"][:], scalar1=-LN32, scalar2=None, op0=ALU.add))
    P.op("vector", lambda e: e.tensor_tensor(out=sm["g"][:], in0=sm["mhb"][:], in1=sm["Mb"][:], op=ALU.subtract))
    d7 = P.op("vector", lambda e: e.tensor_tensor(out=sm["fl"][:], in0=sm["b"][:], in1=sm["Mb"][:], op=ALU.add))
    P.op("scalar", lambda e: e.activation(out=sm["w"][:], in_=sm["w"][:], func=AF.Exp), waits=[d7])
    P.op("scalar", lambda e: e.activation(out=sm["g"][:], in_=sm["g"][:], func=AF.Exp))
    gts = P.op("scalar", lambda e: e.activation(out=sm["fl"][:], in_=sm["fl"][:], func=AF.Exp, scale=-1.0))
    czero = P.op("vector", lambda e: e.memset(C[:], 0.0), waits=[gts, m5])
    GR = 4
    qv = qT.rearrange("(d p) t -> p d t", p=128)
    kv = kT.rearrange("(d p) t -> p d t", p=128)
    Av = Ai.rearrange("(d p) t -> p d t", p=128)
    Bv = Bi.rearrange("(d p) t -> p d t", p=128)
    mv_ = mix.rearrange("(d p) t -> p d t", p=128)
    ktv = ktm.rearrange("(c p) d -> p c d", p=128)
    vtv = vtm.rearrange("(c p) d -> p c d", p=128)
    ld = {n: Slots(P, "l_" + n, 2, [128, 8, 512] if n in ("q", "k", "A", "B") else [128, GR, 1024], BF16, dma=True)
          for n in ("q", "k", "A", "B", "kt", "vt")}
    kw = Slots(P, "kw", 2, [128, 1024], BF16)
    STb = Slots(P, "STb", 2, [128, 128], BF16)
    hc = Slots(P, "hc", 2, [128, 1024], F32)
    hn = Slots(P, "hn", 2, [128, 1024], F32)
    otmp = Slots(P, "otmp", 2, [128, 512], F32)
    outst = Slots(P, "outst", 2, [128, 8, 128], BF16, dma=True)
    small = Slots(P, "small", 2, [128, 8], F32)
    stats = Slots(P, "stats", 2, [128, 12], F32)
    free = {"S": None, "NUM": [None, None], "SMd": czero, "DC": [None, None], "TP": [None, None]}
    grp = {}
    users = {}

    def load_group(g):
        sl = slice(g * 512, (g + 1) * 512)
        out = {}
        for n, src in (("q", qv[:, :, sl]), ("k", kv[:, :, sl]), ("A", Av[:, :, sl]), ("B", Bv[:, :, sl]),
                       ("kt", ktv[:, g * GR:(g + 1) * GR, :]), ("vt", vtv[:, g * GR:(g + 1) * GR, :])):
            i_, fr = ld[n].next()
            tk = P.dma("sync", ld[n].t[i_][:], src, ld[n].ch[i_], fr)
            out[n] = (i_, tk)
        grp[g] = out

    load_group(0)
    stt8 = {"upd": czero}

    def chunk(cc):
            g, ci = cc // GR, cc % GR
            if ci == 0 and g + 1 < NC_ // GR:
                load_group(g + 1)
            G_ = grp[g]
            csl = slice(ci * 128, (ci + 1) * 128)
            qg = ld["q"].t[G_["q"][0]]
            kg = ld["k"].t[G_["k"][0]]
            Ag = ld["A"].t[G_["A"][0]]
            Bg = ld["B"].t[G_["B"][0]]
            ktg = ld["kt"].t[G_["kt"][0]]
            vtg = ld["vt"].t[G_["vt"][0]]
            wcol = sm["w"][:, cc:cc + 1]
            gcol = sm["g"][:, cc:cc + 1]
            mm = None
            for dt in range(8):
                mm = P.op("tensor", lambda e, dt=dt: e.matmul(Sps[:, 0:128], lhsT=kg[:, dt, csl], rhs=qg[:, dt, csl], start=(dt == 0), stop=(dt == 7)),
                          waits=[G_["k"][1], G_["q"][1], free["S"]] if dt == 0 else [], sig=(dt == 7))
            si, sfr = STb.next()
            ST = STb.t[si]
            stt = P.op("vector", lambda e: e.scalar_tensor_tensor(out=ST[:], in0=Sps[:, 0:128], scalar=wcol, in1=tri[:], op0=ALU.mult, op1=ALU.mult),
                       waits=[mm, gts] + sfr)
            free["S"] = stt
            ki, kfr = kw.next()
            kwt = P.op("gpsimd", lambda e: e.tensor_scalar(out=kw.t[ki][:], in0=ktg[:, ci, :], scalar1=wcol, scalar2=None, op0=ALU.mult),
                       waits=[G_["kt"][1], gts] + kfr)
            cast = P.op("scalar", lambda e: e.activation(out=Cbf[:], in_=C[:], func=AF.Copy, scale=gcol), waits=[stt8["upd"], users.get("Cbf")])
            lastn = None
            for n in range(2):
                P.op("tensor", lambda e, n=n: e.matmul(NUM[n][:, :], lhsT=ST[:], rhs=vtg[:, ci, n * 512:(n + 1) * 512], start=True, stop=False),
                     waits=[stt, G_["vt"][1], cast, free["NUM"][n]], sig=False)
                for dt in range(8):
                    lastn = P.op("tensor", lambda e, n=n, dt=dt: e.matmul(NUM[n][:, :], lhsT=qg[:, dt, csl], rhs=Cbv[:, dt, n * 512:(n + 1) * 512],
                                                                          start=False, stop=(dt == 7)), sig=(dt == 7))
            P.op("tensor", lambda e: e.matmul(SM[:, 0:1], lhsT=ST[:], rhs=ones_b[:, 0:1], start=True, stop=False), waits=[free["SMd"]], sig=False)
            lastd = None
            for dt in range(8):
                lastd = P.op("tensor", lambda e, dt=dt: e.matmul(SM[:, 0:1], lhsT=qg[:, dt, csl], rhs=Cbv[:, dt, 1024:1025], start=False, stop=(dt == 7)),
                             sig=(dt == 7))
            STb.free[si] = [lastd]
            smi, smf = small.next()
            sv = small.t[smi]
            P.op("vector", lambda e: e.tensor_scalar(out=sv[:, 6:7], in0=SM[:, 0:1], scalar1=-1.0, scalar2=None, op0=ALU.mult), waits=[lastd] + smf)
            P.op("vector", lambda e: e.tensor_tensor(out=sv[:, 0:1], in0=SM[:, 0:1], in1=sv[:, 6:7], op=ALU.max))
            P.op("vector", lambda e: e.tensor_tensor(out=sv[:, 0:1], in0=sv[:, 0:1], in1=sm["fl"][:, cc:cc + 1], op=ALU.max))
            rr = P.op("vector", lambda e: e.reciprocal(out=sv[:, 1:2], in_=sv[:, 0:1]))
            lastnc = None
            upd = None
            for dt in range(8):
                for n in range(2):
                    mmc = P.op("tensor", lambda e, dt=dt, n=n: e.matmul(DC[n][:, :], lhsT=kw.t[ki][:, dt * 128:(dt + 1) * 128],
                                                                         rhs=vtg[:, ci, n * 512:(n + 1) * 512], start=True, stop=True),
                               waits=[kwt, free["DC"][n]])
                    upd = P.op("vector", lambda e, dt=dt, n=n: e.scalar_tensor_tensor(
                        out=Cv[:, dt, n * 512:(n + 1) * 512], in0=Cv[:, dt, n * 512:(n + 1) * 512], scalar=gcol, in1=DC[n][:, :],
                        op0=ALU.mult, op1=ALU.add), waits=[mmc, cast])
                    free["DC"][n] = upd
                lastnc = P.op("tensor", lambda e, dt=dt: e.matmul(SM[:, 8 + dt:9 + dt], lhsT=kw.t[ki][:, dt * 128:(dt + 1) * 128], rhs=ones_b[:, 0:1],
                                                                  start=True, stop=True), waits=[rr] if dt == 0 else [], sig=(dt == 7))
            kw.free[ki] = [lastnc]
            stt8["upd"] = P.op("vector", lambda e: e.scalar_tensor_tensor(out=Cv[:, :, 1024], in0=Cv[:, :, 1024], scalar=gcol, in1=SM[:, 8:16],
                                                                       op0=ALU.mult, op1=ALU.add), waits=[lastnc, cast])
            free["SMd"] = stt8["upd"]
            users["Cbf"] = lastd
            hi, hfr = hc.next()
            hct = None
            for n in range(2):
                hct = P.op("scalar", lambda e, n=n: e.activation(out=hc.t[hi][:, n * 512:(n + 1) * 512], in_=NUM[n][:, :], func=AF.Copy, scale=sv[:, 1:2]),
                           waits=[lastn, rr] + (hfr if n == 0 else []))
                free["NUM"][n] = hct
            sti, stf = stats.next()
            stt_ = stats.t[sti]
            for n in range(2):
                P.op("vector", lambda e, n=n: e.bn_stats(out=stt_[:, n * 6:(n + 1) * 6], in_=hc.t[hi][:, n * 512:(n + 1) * 512]), waits=[hct] + stf)
            ag = P.op("vector", lambda e: e.bn_aggr(out=sv[:, 2:4], in_=stt_[:, 0:12]))
            sq_ = P.op("scalar", lambda e: e.activation(out=sv[:, 4:5], in_=sv[:, 3:4], func=AF.Sqrt, bias=epsb[:, 0:1]), waits=[ag])
            P.op("vector", lambda e: e.reciprocal(out=sv[:, 5:6], in_=sv[:, 4:5]), waits=[sq_])
            ni, nfr = hn.next()
            hnt = P.op("vector", lambda e: e.tensor_scalar(out=hn.t[ni][:], in0=hc.t[hi][:], scalar1=sv[:, 2:3], scalar2=sv[:, 5:6],
                                                          op0=ALU.subtract, op1=ALU.mult), waits=nfr)
            hc.free[hi] = [hnt]
            stats.free[sti] = [hnt]
            small.free[smi] = [hnt]
            oi, ofr = outst.next()
            fin = None
            lasttr = None
            for hb_ in range(2):
                for q4 in range(4):
                    dt = hb_ * 4 + q4
                    lasttr = P.op("tensor", lambda e, dt=dt, q4=q4, hb_=hb_: e.transpose(TP[hb_][:, q4 * 128:(q4 + 1) * 128], hn.t[ni][:, dt * 128:(dt + 1) * 128], idf[:, :]),
                                  waits=[hnt, free["TP"][hb_]] if q4 == 0 else [], sig=(q4 == 3))
                ti_, tfr = otmp.next()
                tpv = TP[hb_][:, :].rearrange("p (d n) -> p d n", d=4)
                o1 = P.op("vector", lambda e, hb_=hb_, ti_=ti_, tpv=tpv: e.tensor_tensor(
                    out=otmp.t[ti_][:].rearrange("p (d n) -> p d n", d=4), in0=tpv, in1=Ag[:, hb_ * 4:(hb_ + 1) * 4, csl], op=ALU.mult),
                    waits=[lasttr, G_["A"][1]] + tfr)
                free["TP"][hb_] = o1
                fin = P.op("vector", lambda e, hb_=hb_, ti_=ti_: e.tensor_tensor(
                    out=outst.t[oi][:, hb_ * 4:(hb_ + 1) * 4, :], in0=otmp.t[ti_][:].rearrange("p (d n) -> p d n", d=4),
                    in1=Bg[:, hb_ * 4:(hb_ + 1) * 4, csl], op=ALU.add), waits=[G_["B"][1]] + (ofr if hb_ == 0 else []))
                otmp.free[ti_] = [fin]
            hn.free[ni] = [lasttr]
            sd = P.dma("sync", mv_[:, :, cc * 128:(cc + 1) * 128], outst.t[oi][:], outst.ch[oi], [fin])
            outst.free[oi] = [sd]
            if ci == GR - 1:
                for n in ("q", "k", "A", "B", "kt", "vt"):
                    ld[n].free[G_[n][0]] = [fin, lastd, lastnc, mm]

    for cc in range(NC_):
        chunk(cc)
    P.run()
    return nc


ATTN_W = 6144
POOL_WINDOWS = (2, 4, 8, 16)


def _t5_bucket(dist):
    max_exact = 16
    safe = np.maximum(dist, 1).astype(np.float32)
    large = max_exact + (np.log(safe / max_exact) / np.log(2048 / max_exact) * (32 - max_exact)).astype(np.int32)
    large = np.minimum(large, 31)
    return np.where(dist < max_exact, dist, large).astype(np.int32)


def _vec4(v512):
    return np.ascontiguousarray(v512.reshape(4, 128).T)


def even_static_inputs(rel_bias, w_in, q_gain, k_gain, pool_w, pool_scale):
    j = np.arange(128)[:, None]
    ip = np.arange(256)[None, :]
    rel = np.where(ip < 128, ip + 128 - j, (ip - 128) - j)
    valid = (rel >= 0) & (rel <= 128)
    mtab = np.where(valid, 0.0, -30000.0).astype(np.float32)
    relc = np.clip(rel, 0, 128)
    buckets = [_t5_bucket(relc * d) for d, _ in PATTERNS]
    identb = np.eye(128, dtype=np.float32).astype(NPBF)
    maps = []
    for c in range(NCORES):
        cols = []
        for hh in range(6):
            h = 6 * c + hh
            for base in (0, ATTN_W, 2 * ATTN_W):
                cols.append(np.arange(base + h * 128, base + (h + 1) * 128))
            cols.append(np.arange(20480 + h * 128, 20480 + (h + 1) * 128))
        g, half = c // 2, c % 2
        cols.append(np.arange(18432 + g * 512, 18432 + (g + 1) * 512))
        cols.append(np.arange(20480 + ATTN_W + g * 512 + half * 256, 20480 + ATTN_W + g * 512 + (half + 1) * 256))
        cols = np.concatenate(cols)
        wc = np.ascontiguousarray(w_in[:, cols])
        bt = np.empty((128, 6, 3, 256), np.float32)
        for hh in range(6):
            for g_ in range(3):
                bt[:, hh, g_, :] = rel_bias[buckets[g_], 6 * c + hh]
        wdw = POOL_WINDOWS[g]
        pvec = np.zeros((128, 8), np.float32)
        for i_, sft in enumerate((1, 2, 4, 8)):
            pvec[:, i_] = 1.0 if sft < wdw else 0.0
        ps = pool_scale[g * 512 + half * 256: g * 512 + (half + 1) * 256]
        pvec[:, 4] = ps[0:128]
        pvec[:, 5] = ps[128:256]
        tt = np.arange(512)
        rctab = np.empty((128, 1024), np.float32)
        rctab[:, 0:512] = (1.0 / np.minimum(tt + 1, wdw))[None, :]
        rctab[:, 512:1024] = 1.0 / wdw
        maps.append({
            "w": wc,
            "gains": np.ascontiguousarray(np.stack([q_gain, k_gain], axis=1).astype(np.float32)),
            "btab": np.ascontiguousarray(bt.reshape(128, 6 * 768)),
            "mtab": mtab, "identb": identb,
            "pw": np.ascontiguousarray(pool_w[g][:, half * 256:(half + 1) * 256]),
            "pvec": pvec, "rctab": rctab,
        })
    return maps


def even_wout_perm():
    rows = []
    for c in range(NCORES):
        rows.append(np.arange(c * 768, (c + 1) * 768))
        g, half = c // 2, c % 2
        rows.append(np.arange(ATTN_W + g * 512 + half * 256, ATTN_W + g * 512 + (half + 1) * 256))
    return np.concatenate(rows)


def odd_static_inputs(w_up, conv_w, conv_b, wq, wk, wv, w_if, b_if, gn, skip):
    maps = []
    MW = 8192
    eye32 = np.zeros((32, 4, 32, 4), np.float32)
    for c in range(NCORES):
        cols = []
        for j in range(8):
            for base in (0, MW, 2 * MW):
                cols.append(np.arange(base + c * 1024 + j * 128, base + c * 1024 + (j + 1) * 128))
        cols = np.concatenate(cols)
        wc = np.ascontiguousarray(w_up[:, cols])
        ch = np.arange(c * 1024, (c + 1) * 1024).reshape(8, 128)
        cvec = np.zeros((128, 8, 8), np.float32)
        for k_ in range(4):
            cvec[:, :, k_] = conv_w[k_][ch].T
        cvec[:, :, 4] = conv_b[ch].T
        cvec[:, :, 5] = gn[ch].T
        cvec[:, :, 6] = skip[ch].T
        bd = np.zeros((128, 3, 8, 128), np.float32)
        for wi, wm in enumerate((wq, wk, wv)):
            for j in range(8):
                b0 = (c * 1024 + j * 128) // 4
                blk = wm[b0:b0 + 32]
                m = np.zeros((32, 4, 32, 4), np.float32)
                idx = np.arange(32)
                m[idx, :, idx, :] = blk
                bd[:, wi, j, :] = m.reshape(128, 128)
        wif = np.zeros((128, 3, 8, 16), np.float32)
        for wi in range(3):
            rows = wi * MW + ch
            wif[:, wi, :, :] = np.transpose(w_if[rows], (1, 0, 2))
        sel = np.zeros((128, 2), np.float32)
        for r in range(8):
            sel[r * 16 + c, 0] = 1.0
            sel[r * 16 + 8 + c, 1] = 1.0
        bsel = np.zeros((128, 2), np.float32)
        bsel[:, 0] = b_if[c]
        bsel[:, 1] = b_if[8 + c]
        maps.append({"w": wc, "cvec": cvec, "bd": bd, "wif": wif, "sel": sel, "bsel": bsel})
    return maps


_CACHE = {}


def _tile_fm(a):
    kc = a.shape[0] // 128
    return np.ascontiguousarray(a.reshape(kc, 128, NT, TN).transpose(2, 1, 0, 3))


def _prog(name, builder):
    if name not in _CACHE:
        _CACHE[name] = builder()
    return _CACHE[name]


def _launch(name, builder, in_maps):
    nc = _prog(name, builder)
    res = run_bass_kernel_spmd(nc, in_maps, core_ids=list(range(NCORES)))
    return res.results


def kernel(x, rel_bias, e_norm, e_w_in, e_q_gain, e_k_gain, e_pool_w, e_pool_scale, e_w_out,
           o_norm, o_w_up, o_conv_w, o_conv_b, o_wq, o_wk, o_wv, o_w_if, o_b_if, o_gn, o_skip, o_w_down):
    f32 = np.float32
    x = np.asarray(x, f32)
    xT = np.ascontiguousarray(x[0].T)
    hT = [xT[c * 512:(c + 1) * 512] for c in range(NCORES)]
    norms = [np.asarray(e_norm[0], f32), np.asarray(o_norm[0], f32), np.asarray(e_norm[1], f32), np.asarray(o_norm[1], f32),
             np.ones((D,), f32)]
    res = _launch("prep", build_prep, [{"hT": hT[c], "gain": _vec4(norms[0][c * 512:(c + 1) * 512])} for c in range(NCORES)])
    hg = [r["hg"] for r in res]
    ssq = [r["ssq"][0] for r in res]
    tri = np.triu(np.ones((128, 128), f32))
    identf = np.eye(128, dtype=f32)
    perm = even_wout_perm()
    for layer in range(4):
        j = layer // 2
        hg_full = _tile_fm(np.concatenate(hg, axis=0))
        ssq_all = np.ascontiguousarray(np.stack(ssq).astype(f32))
        if layer % 2 == 0:
            st = even_static_inputs(np.asarray(rel_bias, f32), np.asarray(e_w_in[j], f32), np.asarray(e_q_gain[j], f32),
                                    np.asarray(e_k_gain[j], f32), np.asarray(e_pool_w[j], f32), np.asarray(e_pool_scale[j], f32))
            res = _launch("even", build_even, [dict(st[c], hgT=hg_full, ssq_all=ssq_all) for c in range(NCORES)])
            mix_full = _tile_fm(np.concatenate([r["mix"] for r in res], axis=0))
            wout = np.asarray(e_w_out[j], f32)[perm]
        else:
            st = odd_static_inputs(np.asarray(o_w_up[j], f32), np.asarray(o_conv_w[j], f32), np.asarray(o_conv_b[j], f32),
                                   np.asarray(o_wq[j], f32), np.asarray(o_wk[j], f32), np.asarray(o_wv[j], f32),
                                   np.asarray(o_w_if[j], f32), np.asarray(o_b_if[j], f32), np.asarray(o_gn[j], f32),
                                   np.asarray(o_skip[j], f32))
            r1 = _launch("odd1", build_odd1, [dict(w=st[c]["w"], cvec=st[c]["cvec"], bd=st[c]["bd"], wif=st[c]["wif"],
                                                   hgT=hg_full, ssq_all=ssq_all) for c in range(NCORES)])
            gp_all = np.ascontiguousarray(np.concatenate([r["gpart"] for r in r1], axis=0).astype(f32))
            maps = []
            for c in range(NCORES):
                r = r1[c]
                maps.append(dict(qT=r["qT"], kT=r["kT"], ktm=np.ascontiguousarray(r["kT"].T), vtm=np.ascontiguousarray(r["vT"].T),
                                 Ao=r["Ao"], Bo=r["Bo"], gp_all=gp_all, sel=st[c]["sel"], bsel=st[c]["bsel"], tri=tri, identf=identf))
            r2 = _launch("odd2", build_odd2, maps)
            mix_full = _tile_fm(np.concatenate([r["mix"] for r in r2], axis=0))
            wout = np.asarray(o_w_down[j], f32)
        gnext = norms[layer + 1]
        res = _launch("out", build_out, [{"mixT": mix_full, "w": np.ascontiguousarray(wout[:, c * 512:(c + 1) * 512]), "hT": np.ascontiguousarray(hT[c]),
                                          "gain": _vec4(gnext[c * 512:(c + 1) * 512])} for c in range(NCORES)])
        hT = [r["hout"] for r in res]
        hg = [r["hg"] for r in res]
        ssq = [r["ssq"][0] for r in res]
    out = np.ascontiguousarray(np.concatenate(hT, axis=0).T.astype(f32))[None]
    return out
```
